# Optimizing a Trainium2 kernel written in Bass

```python
import math
import jax, jax.numpy as jnp
from jax import lax
import numpy as np

D_MODEL = 1024
BATCH = 16
SEQ = 2048
DEPTH = 4

N_A_LAYERS = DEPTH // 2
N_B_LAYERS = DEPTH - N_A_LAYERS
HEAD_DIM = 64
MEM_LEN = 256
MEM_HEADS = 4
MEM_WIDTH = MEM_HEADS * HEAD_DIM
MIX_WIDTH = D_MODEL - MEM_WIDTH
LRU_WIDTH = MIX_WIDTH
LRU_BLOCKS = LRU_WIDTH // HEAD_DIM
LRU_BLOCK = LRU_WIDTH // LRU_BLOCKS
LRU_CONV = 4
LRU_C = 8.0
SWA_HEADS = MIX_WIDTH // HEAD_DIM
SWA_KV_HEADS = 4
SWA_GROUP = SWA_HEADS // SWA_KV_HEADS
WINDOW = 128
BLOCK = 128
D_FF = 2816
FFN_CONV = 3
EPS = 1e-6

kernel_name = "hawk_yoco_swa_sink_alibi_hybrid"


def rmsnorm(x, g):
    xf = x.astype(jnp.float32)
    y = xf * lax.rsqrt(jnp.mean(xf * xf, axis=-1, keepdims=True) + EPS)
    return (y * g.astype(jnp.float32)).astype(x.dtype)


def causal_dwconv(x, w, b):
    width, ch = w.shape
    y = lax.conv_general_dilated(
        x, w[:, None, :].astype(x.dtype), window_strides=(1,), padding=[(width - 1, 0)],
        dimension_numbers=("NWC", "WIO", "NWC"), feature_group_count=ch)
    return y + b.astype(x.dtype)


def alibi_slopes(n):
    def pow2_slopes(m):
        start = 2.0 ** (-8.0 / m)
        return [start ** (i + 1) for i in range(m)]
    c = 2 ** int(math.floor(math.log2(n)))
    s = pow2_slopes(c)
    if c != n:
        s = s + pow2_slopes(2 * c)[0::2][: n - c]
    return np.asarray(s, dtype=np.float32)


def rglru(u_x, u_gate, w_conv, b_conv, w_r, b_r, w_i, b_i, lam):
    bsz, t, _ = u_x.shape
    xc = causal_dwconv(u_x, w_conv, b_conv)
    xb = xc.reshape(bsz, t, LRU_BLOCKS, LRU_BLOCK)
    r = jax.nn.sigmoid(jnp.einsum("btni,nij->btnj", xb, w_r) + b_r).reshape(bsz, t, LRU_WIDTH)
    i = jax.nn.sigmoid(jnp.einsum("btni,nij->btnj", xb, w_i) + b_i).reshape(bsz, t, LRU_WIDTH)
    log_a = -LRU_C * r.astype(jnp.float32) * jax.nn.softplus(-lam.astype(jnp.float32))
    a = jnp.exp(log_a)
    b = jnp.sqrt(-jnp.expm1(2.0 * log_a)) * (i * xc).astype(jnp.float32)

    def combine(lhs, rhs):
        a1, b1 = lhs
        a2, b2 = rhs
        return a1 * a2, a2 * b1 + b2

    _, h = lax.associative_scan(combine, (a, b), axis=1)
    return h.astype(u_x.dtype) * jax.nn.gelu(u_gate, approximate=True)


def band_blocks(t):
    bsz, seq, nh, hd = t.shape
    tb = t.reshape(bsz, seq // BLOCK, BLOCK, nh, hd)
    prev = jnp.pad(tb[:, :-1], ((0, 0), (1, 0), (0, 0), (0, 0), (0, 0)))
    return jnp.concatenate([prev, tb], axis=2)


def swa_sink_attention(q, k_blk, v_blk, sinks, slopes):
    bsz, seq, _, hd = q.shape
    nb = seq // BLOCK
    qb = q.reshape(bsz, nb, BLOCK, SWA_KV_HEADS, SWA_GROUP, hd)
    s = jnp.einsum("bnqkgd,bnskd->bnkgqs", qb, k_blk,
                   preferred_element_type=jnp.float32) * (hd ** -0.5)
    q_pos = jnp.arange(BLOCK)[:, None] + BLOCK
    k_pos = jnp.arange(2 * BLOCK)[None, :]
    dist = q_pos - k_pos
    in_window = (dist >= 0) & (dist < WINDOW)
    has_prev = (jnp.arange(nb)[:, None, None] > 0) | (k_pos[None] >= BLOCK)
    mask = in_window[None] & has_prev
    alibi = -slopes.reshape(SWA_KV_HEADS, SWA_GROUP, 1, 1) * dist.astype(jnp.float32)
    s = jnp.where(mask[None, :, None, None], s + alibi, -jnp.inf)
    sink = sinks.astype(jnp.float32).reshape(1, 1, SWA_KV_HEADS, SWA_GROUP, 1, 1)
    m = jnp.maximum(jnp.max(s, axis=-1, keepdims=True), sink)
    p = jnp.exp(s - m)
    p = p / (jnp.sum(p, axis=-1, keepdims=True) + jnp.exp(sink - m))
    o = jnp.einsum("bnkgqs,bnskd->bnqkgd", p.astype(v_blk.dtype), v_blk)
    return o.reshape(bsz, seq, SWA_HEADS * hd)


def memory_attention(q, k, v):
    s = jnp.einsum("bthd,bmhd->bhtm", q, k, preferred_element_type=jnp.float32) * (q.shape[-1] ** -0.5)
    p = jax.nn.softmax(s, axis=-1)
    o = jnp.einsum("bhtm,bmhd->bthd", p.astype(v.dtype), v)
    return o.reshape(q.shape[0], q.shape[1], MEM_WIDTH)


def conv_gated_ffn(h, w_up, w_conv, b_conv, w_down):
    u = causal_dwconv(h @ w_up, w_conv, b_conv)
    g, v = jnp.split(u, 2, axis=-1)
    return (jax.nn.gelu(g, approximate=True) * v) @ w_down


def setup_inputs(seed: int = 0) -> dict:
    key = jax.random.key(seed)
    ks = list(jax.random.split(key, 25))

    def nrm(k, shape, scale):
        return jax.random.normal(k, shape, jnp.float32) * scale

    din = D_MODEL ** -0.5
    x = nrm(ks[0], (BATCH, SEQ, D_MODEL), 1.0)
    mem = nrm(ks[1], (BATCH, MEM_LEN, D_MODEL), 1.0)
    g_mix_pre = 1.0 + nrm(ks[2], (DEPTH, D_MODEL), 0.02)
    g_mix_post = 1.0 + nrm(ks[3], (DEPTH, D_MODEL), 0.02)
    g_ffn_pre = 1.0 + nrm(ks[4], (DEPTH, D_MODEL), 0.02)
    g_ffn_post = 1.0 + nrm(ks[5], (DEPTH, D_MODEL), 0.02)
    g_mem = 1.0 + nrm(ks[6], (DEPTH, D_MODEL), 0.02)
    w_mem_kv = nrm(ks[7], (DEPTH, D_MODEL, 2 * MEM_WIDTH), din)
    w_mix_out = nrm(ks[8], (DEPTH, MIX_WIDTH + MEM_WIDTH, D_MODEL), (MIX_WIDTH + MEM_WIDTH) ** -0.5)
    w_ffn_up = nrm(ks[9], (DEPTH, D_MODEL, 2 * D_FF), din)
    w_ffn_conv = nrm(ks[10], (DEPTH, FFN_CONV, 2 * D_FF), FFN_CONV ** -0.5)
    b_ffn_conv = nrm(ks[11], (DEPTH, 2 * D_FF), 0.01)
    w_ffn_down = nrm(ks[12], (DEPTH, D_FF, D_MODEL), D_FF ** -0.5)
    w_in_a = nrm(ks[13], (N_A_LAYERS, D_MODEL, 2 * LRU_WIDTH + MEM_WIDTH), din)
    w_conv_a = nrm(ks[14], (N_A_LAYERS, LRU_CONV, LRU_WIDTH), LRU_CONV ** -0.5)
    b_conv_a = nrm(ks[15], (N_A_LAYERS, LRU_WIDTH), 0.01)
    w_rg_r = nrm(ks[16], (N_A_LAYERS, LRU_BLOCKS, LRU_BLOCK, LRU_BLOCK), LRU_BLOCK ** -0.5)
    b_rg_r = nrm(ks[17], (N_A_LAYERS, LRU_BLOCKS, LRU_BLOCK), 0.01)
    w_rg_i = nrm(ks[18], (N_A_LAYERS, LRU_BLOCKS, LRU_BLOCK, LRU_BLOCK), LRU_BLOCK ** -0.5)
    b_rg_i = nrm(ks[19], (N_A_LAYERS, LRU_BLOCKS, LRU_BLOCK), 0.01)
    a_base = jax.random.uniform(ks[20], (N_A_LAYERS, LRU_WIDTH), jnp.float32, 0.9, 0.999) ** (1.0 / LRU_C)
    lru_lambda = jnp.log(a_base) - jnp.log1p(-a_base)
    w_in_b = nrm(ks[21], (N_B_LAYERS, D_MODEL, MIX_WIDTH + MEM_WIDTH), din)
    sinks_b = nrm(ks[22], (N_B_LAYERS, SWA_HEADS), 0.5)
    g_kv = 1.0 + nrm(ks[23], (D_MODEL,), 0.02)
    w_kv = nrm(ks[24], (D_MODEL, 2 * SWA_KV_HEADS * HEAD_DIM), din)
    return {"x": x, "mem": mem, "g_mix_pre": g_mix_pre, "g_mix_post": g_mix_post,
            "g_ffn_pre": g_ffn_pre, "g_ffn_post": g_ffn_post, "g_mem": g_mem,
            "w_mem_kv": w_mem_kv, "w_mix_out": w_mix_out, "w_ffn_up": w_ffn_up,
            "w_ffn_conv": w_ffn_conv, "b_ffn_conv": b_ffn_conv, "w_ffn_down": w_ffn_down,
            "w_in_a": w_in_a, "w_conv_a": w_conv_a, "b_conv_a": b_conv_a,
            "w_rg_r": w_rg_r, "b_rg_r": b_rg_r, "w_rg_i": w_rg_i, "b_rg_i": b_rg_i,
            "lru_lambda": lru_lambda, "w_in_b": w_in_b, "sinks_b": sinks_b,
            "g_kv": g_kv, "w_kv": w_kv}


def reference(x, mem, g_mix_pre, g_mix_post, g_ffn_pre, g_ffn_post, g_mem, w_mem_kv, w_mix_out,
              w_ffn_up, w_ffn_conv, b_ffn_conv, w_ffn_down, w_in_a, w_conv_a, b_conv_a,
              w_rg_r, b_rg_r, w_rg_i, b_rg_i, lru_lambda, w_in_b, sinks_b, g_kv, w_kv):
    bsz, seq, _ = x.shape
    mem_len = mem.shape[1]
    slopes = jnp.asarray(alibi_slopes(SWA_HEADS))
    k_blk = None
    v_blk = None
    for layer in range(DEPTH):
        if layer == N_A_LAYERS:
            kv = (rmsnorm(x, g_kv) @ w_kv).reshape(bsz, seq, 2, SWA_KV_HEADS, HEAD_DIM)
            k_blk = band_blocks(kv[:, :, 0])
            v_blk = band_blocks(kv[:, :, 1])

        h = rmsnorm(x, g_mix_pre[layer])
        mkv = (rmsnorm(mem, g_mem[layer]) @ w_mem_kv[layer]).reshape(bsz, mem_len, 2, MEM_HEADS, HEAD_DIM)
        if layer < N_A_LAYERS:
            j = layer
            proj = h @ w_in_a[j]
            u_gate, u_x, q_mem = jnp.split(proj, [LRU_WIDTH, 2 * LRU_WIDTH], axis=-1)
            y_main = rglru(u_x, u_gate, w_conv_a[j], b_conv_a[j], w_rg_r[j], b_rg_r[j],
                           w_rg_i[j], b_rg_i[j], lru_lambda[j])
        else:
            j = layer - N_A_LAYERS
            proj = h @ w_in_b[j]
            q_swa, q_mem = jnp.split(proj, [MIX_WIDTH], axis=-1)
            y_main = swa_sink_attention(q_swa.reshape(bsz, seq, SWA_HEADS, HEAD_DIM),
                                        k_blk, v_blk, sinks_b[j], slopes)
        y_mem = memory_attention(q_mem.reshape(bsz, seq, MEM_HEADS, HEAD_DIM), mkv[:, :, 0], mkv[:, :, 1])
        y = jnp.concatenate([y_main, y_mem], axis=-1) @ w_mix_out[layer]
        x = x + rmsnorm(y, g_mix_post[layer])

        h = rmsnorm(x, g_ffn_pre[layer])
        f = conv_gated_ffn(h, w_ffn_up[layer], w_ffn_conv[layer], b_ffn_conv[layer], w_ffn_down[layer])
        x = x + rmsnorm(f, g_ffn_post[layer])
    return x
```

```python
import math
from contextlib import ExitStack
import numpy as np
import concourse.bass as bass
import concourse.mybir as mybir
from concourse.bass_utils import run_bass_kernel_spmd

F32 = mybir.dt.float32
BF16 = mybir.dt.bfloat16
AF = mybir.ActivationFunctionType
ALU = mybir.AluOpType

NCORES = 8
SEQ = 2048
ST = 1024
TC = 512
NST = SEQ // ST
D = 1024
DFF = 2816
NJ = DFF // 128
BIG = 1.0e6
EPS = 1e-6

ENGS = ("pe", "act", "dve", "pool", "sp")


class Ev:
    __slots__ = ("kind", "eng", "idx", "key", "value", "needed")

    def __init__(self, kind, eng=None, idx=0, key=None, value=0):
        self.kind = kind
        self.eng = eng
        self.idx = idx
        self.key = key
        self.value = value
        self.needed = False


class Buf:
    __slots__ = ("name", "lw", "rd", "alias", "excl")

    def __init__(self, name):
        self.name = name
        self.lw = None
        self.rd = []
        self.alias = []
        self.excl = False


class Op:
    __slots__ = ("eng", "fn", "waits", "ev", "dma_key")

    def __init__(self, eng, fn):
        self.eng = eng
        self.fn = fn
        self.waits = []
        self.ev = None
        self.dma_key = None


def alias(a_list, b_list):
    for a in a_list:
        for b in b_list:
            if b not in a.alias:
                a.alias.append(b)
            if a not in b.alias:
                b.alias.append(a)


class Prog:
    def __init__(self):
        self.ops = {e: [] for e in ENGS}
        self.seen = {e: {} for e in ENGS}
        self.dma_cnt = {}
        self.nbuf = 0
        self.last = {}

    def buf(self, name=None):
        self.nbuf += 1
        return Buf(name or f"b{self.nbuf}")

    def bufs(self, n, name="b"):
        return [self.buf(f"{name}{i}") for i in range(n)]

    def _need(self, op, ev, raw):
        if ev is None:
            return
        e = op.eng
        if ev.kind == "eng":
            if ev.eng == e and e == "pe":
                return
            key = ev.eng
            pos = ev.idx
        else:
            key = ev.key
            pos = ev.value
        if self.seen[e].get(key, -1) >= pos:
            return
        self.seen[e][key] = pos
        ev.needed = True
        op.waits.append(ev)

    def op(self, eng, fn, reads=(), writes=(), dma=None, after=()):
        o = Op(eng, fn)
        lst = self.ops[eng]
        for ev in after:
            self._need(o, ev, True)
        wr = []
        for b in writes:
            wr.append(b)
            wr.extend(b.alias)
        if eng != "pe":
            for b in reads:
                if b.excl and b not in wr:
                    wr.append(b)
        for b in reads:
            self._need(o, b.lw, True)
            for a in b.alias:
                self._need(o, a.lw, True)
        for b in wr:
            self._need(o, b.lw, False)
            for r in b.rd:
                self._need(o, r, False)
        if dma is not None:
            c = self.dma_cnt.get(dma, 0) + 16
            self.dma_cnt[dma] = c
            o.ev = Ev("dma", key=dma, value=c)
            o.dma_key = dma
        else:
            o.ev = Ev("eng", eng=eng, idx=len(lst))
        best = {}
        for w in o.waits:
            k = w.eng if w.kind == "eng" else w.key
            p = w.idx if w.kind == "eng" else w.value
            if k not in best or p > best[k][0]:
                best[k] = (p, w)
        o.waits = [v[1] for v in best.values()]
        for b in wr:
            b.lw = o.ev
            b.rd = []
        for b in reads:
            b.rd.append(o.ev)
        lst.append(o)
        self.last[eng] = o
        return o

    def emit(self, nc, final_waits=()):
        for e in ENGS:
            c = 0
            for o in self.ops[e]:
                if o.ev.kind == "eng" and o.ev.needed:
                    c += 1
                    o.ev.value = c
        with ExitStack() as st:
            esem = {e: st.enter_context(nc.semaphore(f"s_{e}")) for e in ENGS}
            dsem = {k: st.enter_context(nc.semaphore(f"d_{k}")) for k in self.dma_cnt}
            block = st.enter_context(nc.Block())

            def semof(ev):
                return esem[ev.eng] if ev.kind == "eng" else dsem[ev.key]

            def run(e, eng):
                for o in self.ops[e]:
                    for w in o.waits:
                        eng.wait_ge(semof(w), w.value)
                    ins = o.fn(eng)
                    if o.dma_key is not None:
                        ins.then_inc(dsem[o.dma_key], 16)
                    elif o.ev.needed:
                        ins.then_inc(esem[e], 1)

            @block.tensor
            def _(eng):
                run("pe", eng)

            @block.scalar
            def _(eng):
                run("act", eng)

            @block.vector
            def _(eng):
                run("dve", eng)

            @block.gpsimd
            def _(eng):
                run("pool", eng)

            @block.sync
            def _(eng):
                run("sp", eng)
                for ev in final_waits:
                    eng.wait_ge(semof(ev), ev.value)


def alibi_slopes(n):
    def pow2_slopes(m):
        start = 2.0 ** (-8.0 / m)
        return [start ** (i + 1) for i in range(m)]
    c = 2 ** int(math.floor(math.log2(n)))
    s = pow2_slopes(c)
    if c != n:
        s = s + pow2_slopes(2 * c)[0::2][: n - c]
    return [float(np.float32(v)) for v in s]


SLOPES = alibi_slopes(12)


def kmajor(W):
    K, N = W.shape
    nk = K // 128
    return np.ascontiguousarray(W.reshape(nk, 128, N).transpose(1, 0, 2)).reshape(128, nk * N)


def colvec(v):
    n = v.shape[0] // 128
    return np.ascontiguousarray(v.reshape(n, 128).T)


def layer_units(l):
    units = {}
    off = 0

    def add(name, ln):
        nonlocal off
        units[name] = (off, ln)
        off += ln
    if l < 2:
        add("A0", 4096)
        add("A1", 4096)
        add("G", 1536)
        add("A2", 4096)
        add("A3", 2048)
    else:
        add("B0", 4096)
        add("B1", 4096)
    add("O0", 4096)
    add("O1", 4096)
    for j2 in range(11):
        add(f"U{j2}", 4096)
    for m in range(8):
        add(f"D{m}", 2816)
    return units, off


COLMAP = {}
_ncol = 0


def _addcol(name, w):
    global _ncol
    COLMAP[name] = _ncol
    _ncol += w


for _l in range(4):
    for _n in ("gmp", "gmo", "gfp", "gfo", "gme"):
        _addcol(f"{_n}{_l}", 8)
    _addcol(f"fcw{_l}", 44 * 3)
    _addcol(f"fcb{_l}", 44)
_addcol("gkv", 8)
for _j in range(2):
    _addcol(f"cw{_j}", 24)
    for _n in ("cb", "br", "bi", "lam"):
        _addcol(f"{_n}{_j}", 6)
    _addcol(f"sk{_j}", 6)
NCOL = _ncol


def build_dt5():
    s = np.arange(128)[:, None].astype(np.float64)
    t = np.arange(128)[None, :].astype(np.float64)
    cur = np.where(t >= s, 8.0 * (t - s), BIG)
    prev = np.where(s > t, 8.0 * (t + 128 - s), BIG)
    dt = np.concatenate([cur, prev], axis=1)
    dt5 = np.concatenate([dt, dt, dt, cur, prev], axis=1)
    return np.ascontiguousarray(dt5.astype(np.float32))


def prep_shared(inp):
    f = lambda k: np.asarray(inp[k], dtype=np.float32)
    sh = {}
    w_in_a, w_in_b = f("w_in_a"), f("w_in_b")
    w_mix_out, w_up, w_down = f("w_mix_out"), f("w_ffn_up"), f("w_ffn_down")
    w_rg_r, w_rg_i = f("w_rg_r"), f("w_rg_i")
    for l in range(4):
        units, tot = layer_units(l)
        arr = np.zeros((128, tot), np.float32)

        def put(name, a):
            o, ln = units[name]
            assert a.shape == (128, ln), (name, a.shape, ln)
            arr[:, o:o + ln] = a
        if l < 2:
            W = w_in_a[l]
            put("A0", kmajor(W[:, 0:512]))
            put("A1", kmajor(np.concatenate([W[:, 512:768], W[:, 768:1024]], axis=1)))
            put("A2", kmajor(W[:, 1024:1536]))
            put("A3", kmajor(W[:, 1536:1792]))
            g = np.zeros((128, 12, 128), np.float32)
            for gi, wg in enumerate((w_rg_r[l], w_rg_i[l])):
                for c in range(6):
                    g[0:64, gi * 6 + c, 0:64] = wg[2 * c]
                    g[64:128, gi * 6 + c, 64:128] = wg[2 * c + 1]
            put("G", g.reshape(128, 1536))
        else:
            W = w_in_b[l - 2]
            put("B0", kmajor(W[:, 0:512]))
            put("B1", kmajor(W[:, 512:1024]))
        put("O0", kmajor(w_mix_out[l][:, 0:512]))
        put("O1", kmajor(w_mix_out[l][:, 512:1024]))
        for j2 in range(11):
            put(f"U{j2}", kmajor(np.concatenate([w_up[l][:, j2 * 256:(j2 + 1) * 256],
                                                 w_up[l][:, DFF + j2 * 256:DFF + (j2 + 1) * 256]], axis=1)))
        for m in range(8):
            put(f"D{m}", kmajor(w_down[l][:, m * 128:(m + 1) * 128]))
        sh[f"w{l}"] = arr
    sh["wm"] = np.concatenate([kmajor(f("w_mem_kv")[l]) for l in range(4)], axis=1)
    wkv = f("w_kv")
    kd = np.concatenate([wkv[:, (k // 2) * 64:(k // 2) * 64 + 64] for k in range(8)], axis=1)
    sh["wk"] = np.concatenate([kmajor(kd), kmajor(wkv[:, 256:512])], axis=1)
    cols = np.zeros((128, NCOL), np.float32)

    def pc(name, a):
        cols[:, COLMAP[name]:COLMAP[name] + a.shape[1]] = a
    for l in range(4):
        pc(f"gmp{l}", colvec(f("g_mix_pre")[l]))
        pc(f"gmo{l}", colvec(f("g_mix_post")[l]))
        pc(f"gfp{l}", colvec(f("g_ffn_pre")[l]))
        pc(f"gfo{l}", colvec(f("g_ffn_post")[l]))
        pc(f"gme{l}", colvec(f("g_mem")[l]))
        wc = f("w_ffn_conv")[l]
        pc(f"fcw{l}", np.ascontiguousarray(wc.reshape(3, 44, 128).transpose(2, 1, 0)).reshape(128, 132))
        pc(f"fcb{l}", colvec(f("b_ffn_conv")[l]))
    pc("gkv", colvec(f("g_kv")))
    for j in range(2):
        wc = f("w_conv_a")[j]
        pc(f"cw{j}", np.ascontiguousarray(wc.reshape(4, 6, 128).transpose(2, 1, 0)).reshape(128, 24))
        pc(f"cb{j}", colvec(f("b_conv_a")[j]))
        pc(f"br{j}", colvec(f("b_rg_r")[j].reshape(768)))
        pc(f"bi{j}", colvec(f("b_rg_i")[j].reshape(768)))
        pc(f"lam{j}", colvec(f("lru_lambda")[j]))
        sk = f("sinks_b")[j]
        pc(f"sk{j}", np.ascontiguousarray(np.repeat(sk.reshape(6, 2), 64, axis=1).T))
    sh["cols"] = cols
    sh["dt5"] = build_dt5()
    return sh


def prep_core(inp, core):
    x = np.asarray(inp["x"], dtype=np.float32)
    mem = np.asarray(inp["mem"], dtype=np.float32)
    xs = x[2 * core:2 * core + 2]
    xT = np.ascontiguousarray(xs.reshape(2, SEQ, 8, 128).transpose(0, 3, 2, 1))
    ms = mem[2 * core:2 * core + 2]
    memT = np.ascontiguousarray(ms.reshape(2, 256, 8, 128).transpose(0, 3, 2, 1))
    return {"xT": xT, "memT": memT}


def build(n_layers=4, n_seq=2, dbg="full"):
    nc = bass.Bass("TRN2", target_bir_lowering=False)
    P = Prog()
    LU = [layer_units(l) for l in range(4)]

    def din(name, shape):
        return nc.dram_tensor(name, shape, F32, kind="ExternalInput").ap()
    xT = din("xT", [2, 128, 8, SEQ])
    memT = din("memT", [2, 128, 8, 256])
    wl = [din(f"w{l}", [128, LU[l][1]]) for l in range(4)]
    wm = din("wm", [128, 4 * 4096])
    wk = din("wk", [128, 4096 + 2048])
    colsd = din("cols", [128, NCOL])
    dt5d = din("dt5", [128, 1024])
    yT = nc.dram_tensor("yT", [2, 128, 8, SEQ], F32, kind="ExternalOutput").ap()

    with ExitStack() as es:
        def sb(name, shape, dt):
            return es.enter_context(nc.sbuf_tensor(name, shape, dt))
        XR = sb("XR", [128, 8 * ST], F32)
        RG = sb("RG", [128, 4 * 4096], BF16)
        PG = sb("PG", [128, 29 * 512], F32)
        PGb = PG.bitcast(BF16)
        HT = sb("HT", [128, 8 * ST], BF16)
        FB1 = HT.bitcast(F32)
        ET = sb("ET", [128, 4 * 1026], F32)
        SQ = ET.bitcast(BF16)
        FB0 = sb("FB0", [128, 8 * 512], F32)
        RS = sb("RS", [128, 2 * 512], F32)
        T1 = sb("T1", [128, 512], F32)
        KT = sb("KT", [128, 4 * 1152], BF16)
        VT = sb("VT", [128, 9 * 256], BF16)
        KM = sb("KM", [128, 4 * 2 * 256], BF16)
        VM = sb("VM", [128, 4 * 2 * 256], BF16)
        DT5 = sb("DT5", [128, 1024], F32)
        COLS = sb("COLS", [128, NCOL], F32)
        DER = sb("DER", [128, 64], F32)
        CST = sb("CST", [128, 4], F32)
        ONESM = sb("ONESM", [128, 128], BF16)
        ONES64 = sb("ONES64", [128, 64], BF16)
        TAILS = sb("TAILS", [128, 4 * 44 * 2], F32)
        UXT = sb("UXT", [128, 2 * 6 * 3], F32)
        HST = sb("HST", [128, 2 * 6], F32)
        PS = es.enter_context(nc.psum_tensor("PS", [128, 8 * 512], F32))

        XB = P.bufs(2, "xb")
        RB = P.bufs(4, "ring")
        PB = P.bufs(29, "pg")
        HB = [[P.buf(f"ht{kc}_{t}") for t in range(2)] for kc in range(8)]
        F1 = P.bufs(8, "fb1")
        for m in range(8):
            alias([F1[m]], HB[m])
        EB = P.bufs(4, "ext")
        SQB = P.bufs(8, "sq")
        alias(SQB, [EB[0], EB[1]])
        F0 = P.bufs(8, "fb0")
        AC = P.bufs(4, "acc")
        for i in range(4):
            alias([AC[i]], [F0[2 * i], F0[2 * i + 1]])
        RSB = P.bufs(2, "rs")
        T1B = P.buf("t1")
        KB = P.bufs(9, "kt")
        VB = P.bufs(9, "vt")
        KMB = P.bufs(4, "km")
        VMB = P.bufs(4, "vm")
        CB = P.buf("consts")
        DERB = P.buf("der")
        TLB = [[P.buf(f"tl{l}_{c}") for c in range(44)] for l in range(4)]
        UXB = [[P.buf(f"ux{j}_{c}") for c in range(6)] for j in range(2)]
        HSB = [[P.buf(f"hs{j}_{c}") for c in range(6)] for j in range(2)]
        BK = P.bufs(8, "bank")
        for b in BK:
            b.excl = True

        def bank(i, n=512, off=0):
            return PS[:, i * 512 + off:i * 512 + off + n]

        def pgf(i, a=0, b=512):
            return PG[:, i * 512 + a:i * 512 + b]

        def pgb(i, a=0, b=1024):
            return PGb[:, i * 1024 + a:i * 1024 + b]

        def xr(kc, tc):
            return XR[:, kc * ST + tc * TC:kc * ST + (tc + 1) * TC]

        def ht(kc, a, b):
            return HT[:, kc * ST + a:kc * ST + b]

        def sq(kc, n=512):
            return SQ[:, kc * 512:kc * 512 + n]

        def col(name, i=0):
            c = COLMAP[name] + i
            return COLS[:, c:c + 1]

        DERMAP = {}
        dn = 0
        for j in range(2):
            for n_ in ("nbr", "nbi", "cl", "cl2", "esk"):
                DERMAP[f"{n_}{j}"] = dn
                dn += 6

        def der(name, i=0):
            c = DERMAP[name] + i
            return DER[:, c:c + 1]

        def MM(out, lhsT, rhs, start, stop, rd, wr):
            P.op("pe", lambda e: e.matmul(out, lhsT, rhs, start=start, stop=stop), reads=rd, writes=wr)

        def ACTV(out, in_, func, rd, wr, bias=None, scale=1.0):
            if bias is None:
                P.op("act", lambda e: e.activation(out=out, in_=in_, func=func, scale=scale), reads=rd, writes=wr)
            else:
                P.op("act", lambda e: e.activation(out=out, in_=in_, func=func, bias=bias, scale=scale),
                     reads=rd, writes=wr)

        def TS(out, in0, s1, s2, op0, op1, rd, wr, eng="dve"):
            if s2 is None:
                P.op(eng, lambda e: e.tensor_scalar(out=out, in0=in0, scalar1=s1, scalar2=None, op0=op0),
                     reads=rd, writes=wr)
            else:
                P.op(eng, lambda e: e.tensor_scalar(out=out, in0=in0, scalar1=s1, scalar2=s2, op0=op0, op1=op1),
                     reads=rd, writes=wr)

        def STT(out, in0, scalar, in1, op0, op1, rd, wr, eng="dve"):
            P.op(eng, lambda e: e.scalar_tensor_tensor(out=out, in0=in0, scalar=scalar, in1=in1, op0=op0, op1=op1),
                 reads=rd, writes=wr)

        def TT(out, in0, in1, op, rd, wr, eng="dve"):
            P.op(eng, lambda e: e.tensor_tensor(out=out, in0=in0, in1=in1, op=op), reads=rd, writes=wr)

        def CP(out, in_, rd, wr, eng="dve"):
            P.op(eng, lambda e: e.tensor_copy(out=out, in_=in_), reads=rd, writes=wr)

        def MSET(ap, v, wr, eng="dve"):
            P.op(eng, lambda e: e.memset(ap, v), writes=wr)

        def warm(n=4, bk=None):
            if bk is None:
                bk = nb1()
            evs = [P.last[e].ev for e in ("act", "dve") if e in P.last]
            for i in range(n):
                P.op("pe", lambda e: e.matmul(bank(bk), ONESM[:, :], KM[:, 0:512], start=True, stop=True),
                     reads=[CB, KMB[0]], writes=[BK[bk]], after=evs if i == 0 else ())

        bstate = {"s": 0, "p": 0}

        def nb1():
            b = bstate["s"]
            bstate["s"] = (b + 1) % 6
            return b

        def nb2():
            b = bstate["p"]
            bstate["p"] = (b + 1) % 3
            return 2 * b

        rstate = {"n": 0}

        class Slot(int):
            pass
        slot_gen = [0, 0, 0, 0]

        def load_unit(src_ap, ln):
            s = Slot(rstate["n"] % 4)
            rstate["n"] += 1
            slot_gen[s] += 1
            s.gen = slot_gen[s]
            dst = RG[:, s * 4096:s * 4096 + ln]
            P.op("pool", lambda e: e.dma_start(out=dst, in_=src_ap), writes=[RB[s]], dma=f"ring{s}")
            return s

        def chk(s):
            assert s.gen == slot_gen[s], "weight ring slot was recycled before its last use"
            return s

        def lu(l, name):
            o, ln = LU[l][0][name]
            return load_unit(wl[l][:, o:o + ln], ln)

        def rg(s, kc, c0, n, width=512):
            chk(s)
            base = s * 4096 + kc * width + c0
            return RG[:, base:base + n]

        P.op("sp", lambda e: e.dma_start(out=COLS[:, :], in_=colsd), writes=[CB], dma="cols")
        P.op("sp", lambda e: e.dma_start(out=DT5[:, :], in_=dt5d), writes=[CB], dma="cols")
        MSET(CST[:, 0:1], EPS, [CB])
        MSET(CST[:, 1:2], 1.0, [CB])
        MSET(CST[:, 2:3], 0.0, [CB])
        MSET(ONESM[:, :], 1.0 / 1024.0, [CB])
        MSET(ONES64[:, :], 1.0, [CB])
        for j in range(2):
            TS(DER[:, DERMAP[f"nbr{j}"]:DERMAP[f"nbr{j}"] + 6], COLS[:, COLMAP[f"br{j}"]:COLMAP[f"br{j}"] + 6],
               -1.0, None, ALU.mult, None, [CB], [DERB])
            TS(DER[:, DERMAP[f"nbi{j}"]:DERMAP[f"nbi{j}"] + 6], COLS[:, COLMAP[f"bi{j}"]:COLMAP[f"bi{j}"] + 6],
               -1.0, None, ALU.mult, None, [CB], [DERB])
            cl = DER[:, DERMAP[f"cl{j}"]:DERMAP[f"cl{j}"] + 6]
            cl2 = DER[:, DERMAP[f"cl2{j}"]:DERMAP[f"cl2{j}"] + 6]
            ACTV(cl, COLS[:, COLMAP[f"lam{j}"]:COLMAP[f"lam{j}"] + 6], AF.Exp, [CB], [DERB], scale=-1.0)
            ACTV(cl, cl, AF.Ln, [DERB, CB], [DERB], bias=CST[:, 1:2])
            TS(cl2, cl, -16.0, None, ALU.mult, None, [DERB], [DERB])
            TS(cl, cl, -8.0, None, ALU.mult, None, [DERB], [DERB])
            ACTV(DER[:, DERMAP[f"esk{j}"]:DERMAP[f"esk{j}"] + 6], COLS[:, COLMAP[f"sk{j}"]:COLMAP[f"sk{j}"] + 6],
                 AF.Exp, [CB], [DERB])

        def rstd_from_bank(bk, rs_i, n=512):
            ACTV(T1[:, 0:n], bank(bk, n), AF.Ln, [BK[bk], CB], [T1B], bias=CST[:, 0:1])
            ACTV(RS[:, rs_i * 512:rs_i * 512 + n], T1[:, 0:n], AF.Exp, [T1B], [RSB[rs_i]], scale=-0.5)

        def prenorm(tc, gname):
            sbk = 6 + tc
            ACTV(SQ[:, 0:4096].rearrange("p (k n) -> p k n", k=8),
                 XR[:, :].rearrange("p (k n) -> p k n", k=8)[:, :, tc * TC:(tc + 1) * TC], AF.Square, [XB[tc]], list(SQB))
            for kc in range(8):
                MM(bank(sbk), ONESM[:, :], sq(kc), kc == 0, kc == 7, [SQB[kc], CB], [BK[sbk]])
            rstd_from_bank(sbk, tc)
            warm(4)
            for kc in range(8):
                STT(ht(kc, tc * TC, (tc + 1) * TC), xr(kc, tc), col(gname, kc), RS[:, tc * 512:(tc + 1) * 512],
                    ALU.mult, ALU.mult, [XB[tc], RSB[tc], CB], [HB[kc][tc]])
                if kc in (1, 3, 5):
                    warm(3)

        def post_evac(m, bk, tc, fbv, fbB, gname, sqi=None):
            sbk = 6 + tc
            sqi = m if sqi is None else sqi
            ACTV(sq(sqi), bank(bk), AF.Square, [BK[bk]], [SQB[sqi]])
            TS(fbv, bank(bk), col(gname, m), None, ALU.mult, None, [BK[bk], CB], [fbB])
            return lambda: MM(bank(sbk), ONESM[:, :], sq(sqi), m == 0, m == 7, [SQB[sqi], CB], [BK[sbk]])

        def post_finish(tc, fbview, fbBs):
            sbk = 6 + tc
            rstd_from_bank(sbk, tc)
            FBT = FB0 if fbBs is F0 else FB1
            fb3 = FBT[:, 0:4096].rearrange("p (m n) -> p m n", m=8)
            rs3 = RS[:, tc * 512:(tc + 1) * 512].unsqueeze(1).broadcast_to([128, 8, 512])
            x3 = XR[:, :].rearrange("p (k n) -> p k n", k=8)[:, :, tc * TC:(tc + 1) * TC]
            warm(6)
            TT(fb3, fb3, rs3, ALU.mult, list(fbBs) + [RSB[tc]], list(fbBs))
            warm(12)
            TT(x3, x3, fb3, ALU.add, list(fbBs) + [XB[tc]], [XB[tc]])
            warm(12)

        def fb0v(m):
            return FB0[:, m * 512:(m + 1) * 512]

        def fb1v(m):
            return FB1[:, m * 512:(m + 1) * 512]

        def mem_attention(l, qpages, ypage_of, tmp_pages):
            tp = 0
            for hp in range(2):
                nbk = nb1()
                dbk = nb1()
                ptp = []
                for hh in range(2):
                    h = 2 * hp + hh
                    o = 64 * hh
                    sb2 = nb2()
                    for mb in range(2):
                        MM(bank(sb2 + mb), KM[o:o + 64, (l * 2 + hp) * 256 + mb * 128:(l * 2 + hp) * 256 + (mb + 1) * 128],
                           pgb(qpages[hp], 0, 512)[o:o + 64, :], True, True,
                           [KMB[l], PB[qpages[hp]]], [BK[sb2 + mb]])
                    pt = tmp_pages[tp % len(tmp_pages)]
                    tp += 1
                    ACTV(pgb(pt), PS[:, sb2 * 512:sb2 * 512 + 1024], AF.Exp, [BK[sb2], BK[sb2 + 1]], [PB[pt]],
                         scale=0.125)
                    ptp.append(pt)
                for hh in range(2):
                    h = 2 * hp + hh
                    o = 64 * hh
                    pt = ptp[hh]
                    for mb in range(2):
                        MM(PS[o:o + 64, nbk * 512:(nbk + 1) * 512],
                           VM[:, (l * 2 + mb) * 256 + h * 64:(l * 2 + mb) * 256 + (h + 1) * 64],
                           pgb(pt, mb * 512, (mb + 1) * 512), mb == 0, mb == 1, [VMB[l], PB[pt]], [BK[nbk]])
                    for mb in range(2):
                        MM(PS[o:o + 64, dbk * 512:(dbk + 1) * 512], ONES64[:, :],
                           pgb(pt, mb * 512, (mb + 1) * 512), mb == 0, mb == 1, [CB, PB[pt]], [BK[dbk]])
                yap, yb = ypage_of(hp)
                tq = tmp_pages[tp % len(tmp_pages)]
                tp += 1
                ACTV(pgf(tq), bank(dbk), AF.Ln, [BK[dbk]], [PB[tq]])
                ACTV(pgf(tq), pgf(tq), AF.Exp, [PB[tq]], [PB[tq]], scale=-1.0)
                TT(yap, bank(nbk), pgf(tq), ALU.mult, [BK[nbk], PB[tq]], [yb])

        def out_proj_and_post(l, ycat_ap, ycat_bufs, tc):
            s0 = lu(l, "O0")
            s1 = lu(l, "O1")
            pend = None
            for m in range(8):
                s = s0 if m < 4 else s1
                bk = nb1()
                for kc in range(8):
                    MM(bank(bk), rg(s, kc, (m % 4) * 128, 128), ycat_ap(kc), kc == 0, kc == 7,
                       [RB[s], ycat_bufs[kc]], [BK[bk]])
                if pend is not None:
                    pend()
                pend = post_evac(m, bk, tc, fb0v(m), F0[m], f"gmo{l}")
            pend()
            post_finish(tc, fb0v, F0)

        def mixer_p1(l, tc):
            if l == 2:
                kv_project(tc)
            prenorm(tc, f"gmp{l}")

        def mixer_a(l, tc):
            j = l
            sA0 = lu(l, "A0")
            sA1 = lu(l, "A1")
            GG = [0, 1, 2, 3, 4, 5]
            YC = [6, 7, 8, 9]
            QM = [10, 11]
            TMP = list(range(12, 22))
            tstate = {"n": 0}

            def tmp():
                t = TMP[tstate["n"] % len(TMP)]
                tstate["n"] += 1
                return t

            def ycat_ap(kc):
                return pgb(YC[kc // 2], (kc % 2) * 512, (kc % 2 + 1) * 512)
            ycat_bufs = [PB[YC[kc // 2]] for kc in range(8)]
            hts = (tc * TC, (tc + 1) * TC)
            for c in range(6):
                s, c0 = (sA0, c * 128) if c < 4 else (sA1, (c - 4) * 128)
                bk = nb1()
                for kc in range(8):
                    MM(bank(bk), rg(s, kc, c0, 128), ht(kc, *hts), kc == 0, kc == 7, [RB[s], HB[kc][tc]], [BK[bk]])
                ACTV(pgf(GG[c]), bank(bk), AF.Gelu_apprx_tanh, [BK[bk]], [PB[GG[c]]])
            sG = lu(l, "G")
            sA2 = lu(l, "A2")

            def chain(c, q):
                s, c0 = (sA1, 256 + c * 128) if c < 2 else (sA2, (c - 2) * 128)
                base = 10 + 5 * q
                ta, t2, t3, t4, t5 = range(base, base + 5)
                tb = 25 + q // 2
                tbo = (q % 2) * 512
                ei = q + 1
                E = ET[:, ei * 1026:ei * 1026 + 515]
                uo = (j * 6 + c) * 3
                acc = pgf(ta)
                hso = j * 6 + c
                st = {}

                def s0():
                    bk = nb1()
                    st["bk"] = bk
                    for kc in range(8):
                        MM(bank(bk), rg(s, kc, c0, 128), ht(kc, *hts), kc == 0, kc == 7, [RB[s], HB[kc][tc]], [BK[bk]])

                def s1():
                    bk = st["bk"]
                    CP(E[:, 0:3], UXT[:, uo:uo + 3], [UXB[j][c]], [EB[ei]])
                    ACTV(E[:, 3:515], bank(bk), AF.Identity, [BK[bk]], [EB[ei]])
                    CP(UXT[:, uo:uo + 3], E[:, 512:515], [EB[ei]], [UXB[j][c]])
                    ACTV(acc, E[:, 3:515], AF.Identity, [EB[ei], CB], [PB[ta]], bias=col(f"cb{j}", c),
                         scale=col(f"cw{j}", c * 4 + 3))

                def s2():
                    for k in range(3):
                        STT(acc, E[:, k:k + 512], col(f"cw{j}", c * 4 + k), acc, ALU.mult, ALU.add,
                            [EB[ei], PB[ta], CB], [PB[ta]])
                    CP(pgb(tb, tbo, tbo + 512), acc, [PB[ta]], [PB[tb]])

                def s3():
                    bkr = nb1()
                    MM(bank(bkr), RG[:, chk(sG) * 4096 + c * 128:sG * 4096 + (c + 1) * 128], pgb(tb, tbo, tbo + 512), True, True,
                       [RB[sG], PB[tb]], [BK[bkr]])
                    bki = nb1()
                    MM(bank(bki), RG[:, sG * 4096 + (6 + c) * 128:sG * 4096 + (7 + c) * 128], pgb(tb, tbo, tbo + 512),
                       True, True, [RB[sG], PB[tb]], [BK[bki]])
                    st["bkr"], st["bki"] = bkr, bki

                def s4():
                    bkr, bki = st["bkr"], st["bki"]
                    ACTV(pgf(t2), bank(bkr), AF.Exp, [BK[bkr], DERB], [PB[t2]], bias=der(f"nbr{j}", c), scale=-1.0)
                    ACTV(pgf(t3), bank(bki), AF.Exp, [BK[bki], DERB], [PB[t3]], bias=der(f"nbi{j}", c), scale=-1.0)

                def s5():
                    ACTV(pgf(t2), pgf(t2), AF.Ln, [PB[t2], CB], [PB[t2]], bias=CST[:, 1:2])
                    ACTV(pgf(t3), pgf(t3), AF.Ln, [PB[t3], CB], [PB[t3]], bias=CST[:, 1:2])

                def s6():
                    ACTV(pgf(t2), pgf(t2), AF.Exp, [PB[t2]], [PB[t2]], scale=-1.0)

                def s7():
                    ACTV(pgf(t4), pgf(t2), AF.Exp, [PB[t2], DERB], [PB[t4]], scale=der(f"cl{j}", c))
                    ACTV(pgf(t5), pgf(t2), AF.Exp, [PB[t2], DERB], [PB[t5]], scale=der(f"cl2{j}", c))

                def s8():
                    TS(pgf(t5), pgf(t5), 0.99999994, None, ALU.min, None, [PB[t5]], [PB[t5]])

                def s9():
                    ACTV(pgf(t5), pgf(t5), AF.Ln, [PB[t5], CB], [PB[t5]], bias=CST[:, 1:2], scale=-1.0)

                def s10():
                    STT(pgf(t5), pgf(t5), 0.5, pgf(t3), ALU.mult, ALU.subtract, [PB[t5], PB[t3]], [PB[t5]])

                def s11():
                    ACTV(pgf(t5), pgf(t5), AF.Exp, [PB[t5]], [PB[t5]])

                def s12():
                    TT(pgf(t5), pgf(t5), acc, ALU.mult, [PB[t5], PB[ta]], [PB[t5]])
                    P.op("dve", lambda e, o_=pgf(t3), a_=pgf(t4), b_=pgf(t5), i_=HST[:, hso:hso + 1]:
                         e.tensor_tensor_scan(out=o_, data0=a_, data1=b_, initial=i_, op0=ALU.mult, op1=ALU.add),
                         reads=[PB[t4], PB[t5], HSB[j][c]], writes=[PB[t3]])
                    CP(HST[:, hso:hso + 1], pgf(t3, 511, 512), [PB[t3]], [HSB[j][c]])
                    TT(ycat_ap(c), pgf(t3), pgf(GG[c]), ALU.mult, [PB[t3], PB[GG[c]]], [ycat_bufs[c]])

                return [s0, s1, s2, s3, s4, s5, s6, s7, s8, s9, s10, s11, s12]

            sA3 = lu(l, "A3")
            for grp in range(2):
                ch = [chain(3 * grp + q, q) for q in range(3)]
                for k in range(len(ch[0])):
                    for q in range(3):
                        ch[q][k]()
                    if k >= 4:
                        warm(4, 7)
            for hp in range(2):
                bk = nb1()
                for kc in range(8):
                    MM(bank(bk), rg(sA3, kc, hp * 128, 128, width=256), ht(kc, *hts), kc == 0, kc == 7,
                       [RB[sA3], HB[kc][tc]], [BK[bk]])
                ACTV(pgb(QM[hp], 0, 512), bank(bk), AF.Identity, [BK[bk]], [PB[QM[hp]]])
            mem_attention(l, QM, lambda hp: (ycat_ap(6 + hp), ycat_bufs[6 + hp]), TMP)
            return lambda: out_proj_and_post(l, ycat_ap, ycat_bufs, tc)

        def kv_project(tc):
            prenorm(tc, "gkv")
            hts = (tc * TC, (tc + 1) * TC)
            s0 = load_unit(wk[:, 0:4096], 4096)
            s1 = load_unit(wk[:, 4096:6144], 2048)
            for k in range(4):
                bk = nb1()
                for kc in range(8):
                    MM(bank(bk), rg(s0, kc, k * 128, 128), ht(kc, *hts), kc == 0, kc == 7, [RB[s0], HB[kc][tc]], [BK[bk]])
                ACTV(KT[:, k * 1152 + 128 + tc * 512:k * 1152 + 128 + (tc + 1) * 512], bank(bk), AF.Identity, [BK[bk]],
                     [KB[1 + 4 * tc + i] for i in range(4)])
            for tb in range(4):
                bk = nb1()
                for kc in range(8):
                    MM(bank(bk, 256), ht(kc, tc * TC + tb * 128, tc * TC + (tb + 1) * 128), rg(s1, kc, 0, 256, width=256),
                       kc == 0, kc == 7, [RB[s1], HB[kc][tc]], [BK[bk]])
                idx = 1 + 4 * tc + tb
                ACTV(VT[:, idx * 256:(idx + 1) * 256], bank(bk, 256), AF.Identity, [BK[bk]], [VB[idx]])

        def mixer_b(l, tc, first):
            j = l - 2
            hts = (tc * TC, (tc + 1) * TC)
            sB0 = lu(l, "B0")
            sB1 = lu(l, "B1")
            QP = list(range(0, 8))
            YC = [8, 9, 10, 11]
            TMP = list(range(16, 22))
            tstate = {"n": 0}
            sstate = {"n": 0}

            def tmp():
                t = TMP[tstate["n"] % len(TMP)]
                tstate["n"] += 1
                return t

            def ycat_ap(kc):
                return pgb(YC[kc // 2], (kc % 2) * 512, (kc % 2 + 1) * 512)
            ycat_bufs = [PB[YC[kc // 2]] for kc in range(8)]
            for cq in range(8):
                s, c0 = (sB0, cq * 128) if cq < 4 else (sB1, (cq - 4) * 128)
                bk = nb1()
                for kc in range(8):
                    MM(bank(bk), rg(s, kc, c0, 128), ht(kc, *hts), kc == 0, kc == 7, [RB[s], HB[kc][tc]], [BK[bk]])
                ACTV(pgb(QP[cq], 0, 512), bank(bk), AF.Identity, [BK[bk]], [PB[QP[cq]]])
            noprev = first and tc == 0
            ncol = 896 if noprev else 1024

            def stA(cq):
                pts = []
                for hh in range(2):
                    h = 2 * cq + hh
                    o = 64 * hh
                    k = h // 3
                    sb2 = 2 * hh
                    q = pgb(QP[cq], 0, 512)
                    for i in range(5):
                        if i == 0 and noprev:
                            continue
                        sidx = 4 * tc + i
                        kap = KT[o:o + 64, k * 1152 + sidx * 128:k * 1152 + (sidx + 1) * 128]
                        if i == 0:
                            c0, qa, qb = 896, 0, 128
                        elif i == 4:
                            c0, qa, qb = 768, 384, 512
                        else:
                            c0, qa, qb = (i - 1) * 256, (i - 1) * 128, (i + 1) * 128
                        bki = sb2 + c0 // 512
                        MM(PS[:, sb2 * 512 + c0:sb2 * 512 + c0 + (qb - qa)], kap, q[o:o + 64, qa:qb], True, True,
                           [KB[sidx], PB[QP[cq]]], [BK[bki]])
                    sp = 12 + 2 * (sstate["n"] % 2)
                    sstate["n"] += 1
                    spv = PG[:, sp * 512:sp * 512 + ncol]
                    STT(spv, DT5[:, 0:ncol], -SLOPES[h], PS[:, sb2 * 512:sb2 * 512 + ncol], ALU.mult, ALU.add,
                        [CB, BK[sb2], BK[sb2 + 1]], [PB[sp], PB[sp + 1]])
                    pt = tmp()
                    ACTV(pgb(pt, 0, ncol), spv, AF.Exp, [PB[sp], PB[sp + 1]], [PB[pt]], scale=0.125)
                    pts.append(pt)
                return pts

            def stB(cq, pts):
                nbk, dbk = (4, 5) if cq % 2 == 0 else (6, 7)
                for hh in range(2):
                    h = 2 * cq + hh
                    o = 64 * hh
                    k = h // 3
                    pt = pts[hh]
                    for qb_ in range(4):
                        srcs = []
                        if not (noprev and qb_ == 0):
                            pc0 = 896 if qb_ == 0 else (qb_ - 1) * 256 + 128
                            srcs.append((4 * tc + qb_, pc0))
                        cc0 = 768 if qb_ == 3 else qb_ * 256
                        srcs.append((4 * tc + qb_ + 1, cc0))
                        for which in ("n", "d"):
                            bkx = nbk if which == "n" else dbk
                            for si, (sidx, c0) in enumerate(srcs):
                                if which == "n":
                                    lhs = VT[:, sidx * 256 + k * 64:sidx * 256 + (k + 1) * 64]
                                    rd = [VB[sidx], PB[pt]]
                                else:
                                    lhs = ONES64[:, :]
                                    rd = [CB, PB[pt]]
                                MM(PS[o:o + 64, bkx * 512 + qb_ * 128:bkx * 512 + (qb_ + 1) * 128], lhs,
                                   pgb(pt, c0, c0 + 128), si == 0, si == len(srcs) - 1, rd, [BK[bkx]])

            def stC(cq):
                nbk, dbk = (4, 5) if cq % 2 == 0 else (6, 7)
                tq = tmp()
                ACTV(pgf(tq), bank(dbk), AF.Ln, [BK[dbk], DERB], [PB[tq]], bias=der(f"esk{j}", cq))
                ACTV(pgf(tq), pgf(tq), AF.Exp, [PB[tq]], [PB[tq]], scale=-1.0)
                TT(ycat_ap(cq), bank(nbk), pgf(tq), ALU.mult, [BK[nbk], PB[tq]], [ycat_bufs[cq]])

            nxt = stA(0)
            for cq in range(6):
                cur = nxt
                if cq + 1 < 6:
                    nxt = stA(cq + 1)
                stB(cq, cur)
                stC(cq)
            mem_attention(l, [QP[6], QP[7]], lambda hp: (ycat_ap(6 + hp), ycat_bufs[6 + hp]), TMP)
            return lambda: out_proj_and_post(l, ycat_ap, ycat_bufs, tc)

        def ffn(l, next_p1=None):
            units = {}

            def stage1(jn):
                j2, jj = divmod(jn, 2)
                if jj == 0:
                    units[j2] = lu(l, f"U{j2}")
                s = units[j2]
                st_ = jn % 2
                gb, vb = (0, 2) if st_ == 0 else (4, 6)
                for b0, c0 in ((gb, jj * 128), (vb, 256 + jj * 128)):
                    for tc in range(2):
                        for kc in range(8):
                            MM(bank(b0 + tc), rg(s, kc, c0, 128), ht(kc, tc * TC, (tc + 1) * TC), kc == 0, kc == 7,
                               [RB[s], HB[kc][tc]], [BK[b0 + tc]])

            def halves(jn):
                st_ = jn % 2
                gb, vb = (0, 2) if st_ == 0 else (4, 6)
                out = []
                for hi, (b0, ch) in enumerate(((gb, jn), (vb, 22 + jn))):
                    ei = 2 * st_ + hi
                    out.append((b0, ch, ei, ET[:, ei * 1026:(ei + 1) * 1026], (l * 44 + ch) * 2,
                                FB0[:, ei * 1024:(ei + 1) * 1024]))
                return out

            def stage2a(jn):
                for b0, ch, ei, E, to, acc in halves(jn):
                    CP(E[:, 0:2], TAILS[:, to:to + 2], [TLB[l][ch]], [EB[ei]])

            def stage2b(jn):
                for b0, ch, ei, E, to, acc in halves(jn):
                    ACTV(E[:, 2:1026], PS[:, b0 * 512:b0 * 512 + 1024], AF.Identity, [BK[b0], BK[b0 + 1]], [EB[ei]])
                for b0, ch, ei, E, to, acc in halves(jn):
                    ACTV(acc, E[:, 2:1026], AF.Identity, [EB[ei], CB], [AC[ei]], bias=col(f"fcb{l}", ch),
                         scale=col(f"fcw{l}", ch * 3 + 2))

            def stage2c(jn):
                accs = []
                for b0, ch, ei, E, to, acc in halves(jn):
                    CP(TAILS[:, to:to + 2], E[:, 1024:1026], [EB[ei]], [TLB[l][ch]])
                for b0, ch, ei, E, to, acc in halves(jn):
                    for k in range(2):
                        STT(acc, E[:, k:k + 1024], col(f"fcw{l}", ch * 3 + k), acc, ALU.mult, ALU.add,
                            [EB[ei], AC[ei], CB], [AC[ei]])
                    accs.append((acc, AC[ei]))
                return accs

            def stage3(jn, accs):
                ACTV(accs[0][0], accs[0][0], AF.Gelu_apprx_tanh, [accs[0][1]], [accs[0][1]])
                TT(pgb(jn), accs[1][0], accs[0][0], ALU.mult, [accs[0][1], accs[1][1]], [PB[jn]])

            prev = None
            stage2a(0)
            for jn in range(NJ):
                stage1(jn)
                stage2b(jn)
                if jn + 1 < NJ:
                    stage2a(jn + 1)
                accs = stage2c(jn)
                if prev is not None:
                    stage3(*prev)
                prev = (jn, accs)
            stage3(*prev)
            pend = None
            for m in range(8):
                s = lu(l, f"D{m}")
                for tc in range(2):
                    bk = nb1()
                    for jn in range(NJ):
                        MM(bank(bk), RG[:, chk(s) * 4096 + jn * 128:s * 4096 + (jn + 1) * 128],
                           pgb(jn, tc * 512, (tc + 1) * 512), jn == 0, jn == NJ - 1, [RB[s], PB[jn]], [BK[bk]])
                    if pend is not None:
                        pend()
                    sqi = (2 * m + tc) % 8
                    if tc == 0:
                        pend = post_evac(m, bk, 0, fb1v(m), F1[m], f"gfo{l}", sqi)
                    else:
                        pend = post_evac(m, bk, 1, fb0v(m), F0[m], f"gfo{l}", sqi)
            pend()
            post_finish(0, fb1v, F1)
            if next_p1 is not None:
                next_p1()
            post_finish(1, fb0v, F0)

        def seq_prologue(s):
            for l in range(4):
                for ch in range(44):
                    pass
            P.op("dve", lambda e: e.memset(TAILS[:, :], 0.0), writes=[b for l in range(4) for b in TLB[l]])
            P.op("dve", lambda e: e.memset(UXT[:, :], 0.0), writes=[b for j in range(2) for b in UXB[j]])
            P.op("dve", lambda e: e.memset(HST[:, :], 0.0), writes=[b for j in range(2) for b in HSB[j]])
            MT = PG[:, 0:2048]
            P.op("sp", lambda e: e.dma_start(out=MT.rearrange("p (k n) -> p k n", k=8), in_=memT[s]),
                 writes=[PB[0], PB[1], PB[2], PB[3]], dma="mem")
            for l in range(n_layers):
                su = load_unit(wm[:, l * 4096:(l + 1) * 4096], 4096)
                for kc in range(8):
                    ACTV(sq(kc, 256), PG[:, kc * 256:(kc + 1) * 256], AF.Square, [PB[kc // 2]], [SQB[kc]])
                for kc in range(8):
                    MM(bank(6, 256), ONESM[:, :], sq(kc, 256), kc == 0, kc == 7, [SQB[kc], CB], [BK[6]])
                rstd_from_bank(6, 0, 256)
                for kc in range(8):
                    STT(PGb[:, 4 * 1024 + kc * 256:4 * 1024 + (kc + 1) * 256], PG[:, kc * 256:(kc + 1) * 256],
                        col(f"gme{l}", kc), RS[:, 0:256], ALU.mult, ALU.mult, [PB[kc // 2], RSB[0], CB],
                        [PB[4 + kc // 4]])

                def hm(kc, a=0, b=256):
                    return PGb[:, 4 * 1024 + kc * 256 + a:4 * 1024 + kc * 256 + b]
                for hp in range(2):
                    bk = nb1()
                    for kc in range(8):
                        MM(bank(bk, 256), rg(su, kc, hp * 128, 128), hm(kc), kc == 0, kc == 7,
                           [RB[su], PB[4 + kc // 4]], [BK[bk]])
                    ACTV(KM[:, (l * 2 + hp) * 256:(l * 2 + hp + 1) * 256], bank(bk, 256), AF.Identity, [BK[bk]], [KMB[l]])
                for mb in range(2):
                    bk = nb1()
                    for kc in range(8):
                        MM(bank(bk, 256), hm(kc, mb * 128, (mb + 1) * 128), rg(su, kc, 256, 256), kc == 0, kc == 7,
                           [RB[su], PB[4 + kc // 4]], [BK[bk]])
                    ACTV(VM[:, (l * 2 + mb) * 256:(l * 2 + mb + 1) * 256], bank(bk, 256), AF.Identity, [BK[bk]], [VMB[l]])

        XR3 = XR[:, :].rearrange("p (k n) -> p k n", k=8)
        out_evs = []
        for s in range(n_seq):
            seq_prologue(s)
            for st in range(NST):
                P.op("sp", lambda e, s=s, st=st: e.dma_start(out=XR3, in_=xT[s, :, :, st * ST:(st + 1) * ST]),
                     writes=[XB[0], XB[1]], dma="xin")
                for l in range(n_layers):
                    mix = (lambda tc, l=l: mixer_a(l, tc)) if l < 2 else (lambda tc, l=l, st=st: mixer_b(l, tc, st == 0))
                    if l == 0:
                        mixer_p1(l, 0)
                    fin0 = mix(0)
                    mixer_p1(l, 1)
                    fin0()
                    fin1 = mix(1)
                    prenorm(0, f"gfp{l}")
                    fin1()
                    prenorm(1, f"gfp{l}")
                    ffn(l, (lambda l=l: mixer_p1(l + 1, 0)) if l + 1 < n_layers else None)
                if n_layers > 2 and st < NST - 1:
                    for k in range(4):
                        CP(KT[:, k * 1152:k * 1152 + 128], KT[:, k * 1152 + 1024:k * 1152 + 1152], [KB[8]], [KB[0]])
                    CP(VT[:, 0:256], VT[:, 8 * 256:9 * 256], [VB[8]], [VB[0]])
                o = P.op("sp", lambda e, s=s, st=st: e.dma_start(out=yT[s, :, :, st * ST:(st + 1) * ST], in_=XR3),
                         reads=[XB[0], XB[1]], dma="yout")
                out_evs.append(o.ev)
        P.emit(nc, final_waits=[out_evs[-1]])
    return nc


_CACHE = {}


def kernel(_n_layers=4, _cores=NCORES, **inputs):
    key = (_n_layers,)
    if key not in _CACHE:
        _CACHE[key] = build(_n_layers)
    nc = _CACHE[key]
    sh = prep_shared(inputs)
    in_maps = []
    for c in range(_cores):
        m = dict(sh)
        m.update(prep_core(inputs, c))
        in_maps.append(m)
    res = run_bass_kernel_spmd(nc, in_maps, core_ids=list(range(_cores)))
    outs = []
    for c in range(_cores):
        yT = np.asarray(res.results[c]["yT"])
        outs.append(np.ascontiguousarray(yT.transpose(0, 3, 2, 1)).reshape(2, SEQ, D))
    return np.concatenate(outs, axis=0).astype(np.float32)
```

```python
import math
from contextlib import ExitStack
import numpy as np
import concourse.bass as bass
import concourse.mybir as mybir
from concourse.bass_utils import run_bass_kernel_spmd

F32 = mybir.dt.float32
BF16 = mybir.dt.bfloat16
AF = mybir.ActivationFunctionType
ALU = mybir.AluOpType

NCORES = 8
SEQ = 2048
ST = 1024
TC = 512
NST = SEQ // ST
D = 1024
DFF = 2816
NJ = DFF // 128
BIG = 1.0e6
EPS = 1e-6

ENGS = ("pe", "act", "dve", "pool", "sp")


class Ev:
    __slots__ = ("kind", "eng", "idx", "key", "value", "needed")

    def __init__(self, kind, eng=None, idx=0, key=None, value=0):
        self.kind = kind
        self.eng = eng
        self.idx = idx
        self.key = key
        self.value = value
        self.needed = False


class Buf:
    __slots__ = ("name", "lw", "rd", "alias", "excl")

    def __init__(self, name):
        self.name = name
        self.lw = None
        self.rd = []
        self.alias = []
        self.excl = False


class Op:
    __slots__ = ("eng", "fn", "waits", "ev", "dma_key")

    def __init__(self, eng, fn):
        self.eng = eng
        self.fn = fn
        self.waits = []
        self.ev = None
        self.dma_key = None


def alias(a_list, b_list):
    for a in a_list:
        for b in b_list:
            if b not in a.alias:
                a.alias.append(b)
            if a not in b.alias:
                b.alias.append(a)


class Prog:
    def __init__(self):
        self.ops = {e: [] for e in ENGS}
        self.seen = {e: {} for e in ENGS}
        self.dma_cnt = {}
        self.nbuf = 0

    def buf(self, name=None):
        self.nbuf += 1
        return Buf(name or f"b{self.nbuf}")

    def bufs(self, n, name="b"):
        return [self.buf(f"{name}{i}") for i in range(n)]

    def _need(self, op, ev, raw):
        if ev is None:
            return
        e = op.eng
        if ev.kind == "eng":
            if ev.eng == e and e == "pe":
                return
            key = ev.eng
            pos = ev.idx
        else:
            key = ev.key
            pos = ev.value
        if self.seen[e].get(key, -1) >= pos:
            return
        self.seen[e][key] = pos
        ev.needed = True
        op.waits.append(ev)

    def op(self, eng, fn, reads=(), writes=(), dma=None):
        o = Op(eng, fn)
        lst = self.ops[eng]
        wr = []
        for b in writes:
            wr.append(b)
            wr.extend(b.alias)
        if eng != "pe":
            for b in reads:
                if b.excl and b not in wr:
                    wr.append(b)
        for b in reads:
            self._need(o, b.lw, True)
            for a in b.alias:
                self._need(o, a.lw, True)
        for b in wr:
            self._need(o, b.lw, False)
            for r in b.rd:
                self._need(o, r, False)
        if dma is not None:
            c = self.dma_cnt.get(dma, 0) + 16
            self.dma_cnt[dma] = c
            o.ev = Ev("dma", key=dma, value=c)
            o.dma_key = dma
        else:
            o.ev = Ev("eng", eng=eng, idx=len(lst))
        best = {}
        for w in o.waits:
            k = w.eng if w.kind == "eng" else w.key
            p = w.idx if w.kind == "eng" else w.value
            if k not in best or p > best[k][0]:
                best[k] = (p, w)
        o.waits = [v[1] for v in best.values()]
        for b in wr:
            b.lw = o.ev
            b.rd = []
        for b in reads:
            b.rd.append(o.ev)
        lst.append(o)
        return o

    def emit(self, nc, final_waits=()):
        for e in ENGS:
            c = 0
            for o in self.ops[e]:
                if o.ev.kind == "eng" and o.ev.needed:
                    c += 1
                    o.ev.value = c
        with ExitStack() as st:
            esem = {e: st.enter_context(nc.semaphore(f"s_{e}")) for e in ENGS}
            dsem = {k: st.enter_context(nc.semaphore(f"d_{k}")) for k in self.dma_cnt}
            block = st.enter_context(nc.Block())

            def semof(ev):
                return esem[ev.eng] if ev.kind == "eng" else dsem[ev.key]

            def run(e, eng):
                for o in self.ops[e]:
                    for w in o.waits:
                        eng.wait_ge(semof(w), w.value)
                    ins = o.fn(eng)
                    if o.dma_key is not None:
                        ins.then_inc(dsem[o.dma_key], 16)
                    elif o.ev.needed:
                        ins.then_inc(esem[e], 1)

            @block.tensor
            def _(eng):
                run("pe", eng)

            @block.scalar
            def _(eng):
                run("act", eng)

            @block.vector
            def _(eng):
                run("dve", eng)

            @block.gpsimd
            def _(eng):
                run("pool", eng)

            @block.sync
            def _(eng):
                run("sp", eng)
                for ev in final_waits:
                    eng.wait_ge(semof(ev), ev.value)


def alibi_slopes(n):
    def pow2_slopes(m):
        start = 2.0 ** (-8.0 / m)
        return [start ** (i + 1) for i in range(m)]
    c = 2 ** int(math.floor(math.log2(n)))
    s = pow2_slopes(c)
    if c != n:
        s = s + pow2_slopes(2 * c)[0::2][: n - c]
    return [float(np.float32(v)) for v in s]


SLOPES = alibi_slopes(12)


def kmajor(W):
    K, N = W.shape
    nk = K // 128
    return np.ascontiguousarray(W.reshape(nk, 128, N).transpose(1, 0, 2)).reshape(128, nk * N)


def colvec(v):
    n = v.shape[0] // 128
    return np.ascontiguousarray(v.reshape(n, 128).T)


def layer_units(l):
    units = {}
    off = 0

    def add(name, ln):
        nonlocal off
        units[name] = (off, ln)
        off += ln
    if l < 2:
        add("A0", 4096)
        add("A1", 4096)
        add("G", 1536)
        add("A2", 4096)
        add("A3", 2048)
    else:
        add("B0", 4096)
        add("B1", 4096)
    add("O0", 4096)
    add("O1", 4096)
    for j2 in range(11):
        add(f"U{j2}", 4096)
    for m in range(8):
        add(f"D{m}", 2816)
    return units, off


COLMAP = {}
_ncol = 0


def _addcol(name, w):
    global _ncol
    COLMAP[name] = _ncol
    _ncol += w


for _l in range(4):
    for _n in ("gmp", "gmo", "gfp", "gfo", "gme"):
        _addcol(f"{_n}{_l}", 8)
    _addcol(f"fcw{_l}", 44 * 3)
    _addcol(f"fcb{_l}", 44)
_addcol("gkv", 8)
for _j in range(2):
    _addcol(f"cw{_j}", 24)
    for _n in ("cb", "br", "bi", "lam"):
        _addcol(f"{_n}{_j}", 6)
    _addcol(f"sk{_j}", 6)
NCOL = _ncol


def build_dt5():
    s = np.arange(128)[:, None].astype(np.float64)
    t = np.arange(128)[None, :].astype(np.float64)
    cur = np.where(t >= s, 8.0 * (t - s), BIG)
    prev = np.where(s > t, 8.0 * (t + 128 - s), BIG)
    dt = np.concatenate([cur, prev], axis=1)
    dt5 = np.concatenate([dt, dt, dt, cur, prev], axis=1)
    return np.ascontiguousarray(dt5.astype(np.float32))


def prep_shared(inp):
    f = lambda k: np.asarray(inp[k], dtype=np.float32)
    sh = {}
    w_in_a, w_in_b = f("w_in_a"), f("w_in_b")
    w_mix_out, w_up, w_down = f("w_mix_out"), f("w_ffn_up"), f("w_ffn_down")
    w_rg_r, w_rg_i = f("w_rg_r"), f("w_rg_i")
    for l in range(4):
        units, tot = layer_units(l)
        arr = np.zeros((128, tot), np.float32)

        def put(name, a):
            o, ln = units[name]
            assert a.shape == (128, ln), (name, a.shape, ln)
            arr[:, o:o + ln] = a
        if l < 2:
            W = w_in_a[l]
            put("A0", kmajor(W[:, 0:512]))
            put("A1", kmajor(np.concatenate([W[:, 512:768], W[:, 768:1024]], axis=1)))
            put("A2", kmajor(W[:, 1024:1536]))
            put("A3", kmajor(W[:, 1536:1792]))
            g = np.zeros((128, 12, 128), np.float32)
            for gi, wg in enumerate((w_rg_r[l], w_rg_i[l])):
                for c in range(6):
                    g[0:64, gi * 6 + c, 0:64] = wg[2 * c]
                    g[64:128, gi * 6 + c, 64:128] = wg[2 * c + 1]
            put("G", g.reshape(128, 1536))
        else:
            W = w_in_b[l - 2]
            put("B0", kmajor(W[:, 0:512]))
            put("B1", kmajor(W[:, 512:1024]))
        put("O0", kmajor(w_mix_out[l][:, 0:512]))
        put("O1", kmajor(w_mix_out[l][:, 512:1024]))
        for j2 in range(11):
            put(f"U{j2}", kmajor(np.concatenate([w_up[l][:, j2 * 256:(j2 + 1) * 256],
                                                 w_up[l][:, DFF + j2 * 256:DFF + (j2 + 1) * 256]], axis=1)))
        for m in range(8):
            put(f"D{m}", kmajor(w_down[l][:, m * 128:(m + 1) * 128]))
        sh[f"w{l}"] = arr
    sh["wm"] = np.concatenate([kmajor(f("w_mem_kv")[l]) for l in range(4)], axis=1)
    wkv = f("w_kv")
    kd = np.concatenate([wkv[:, (k // 2) * 64:(k // 2) * 64 + 64] for k in range(8)], axis=1)
    sh["wk"] = np.concatenate([kmajor(kd), kmajor(wkv[:, 256:512])], axis=1)
    cols = np.zeros((128, NCOL), np.float32)

    def pc(name, a):
        cols[:, COLMAP[name]:COLMAP[name] + a.shape[1]] = a
    for l in range(4):
        pc(f"gmp{l}", colvec(f("g_mix_pre")[l]))
        pc(f"gmo{l}", colvec(f("g_mix_post")[l]))
        pc(f"gfp{l}", colvec(f("g_ffn_pre")[l]))
        pc(f"gfo{l}", colvec(f("g_ffn_post")[l]))
        pc(f"gme{l}", colvec(f("g_mem")[l]))
        wc = f("w_ffn_conv")[l]
        pc(f"fcw{l}", np.ascontiguousarray(wc.reshape(3, 44, 128).transpose(2, 1, 0)).reshape(128, 132))
        pc(f"fcb{l}", colvec(f("b_ffn_conv")[l]))
    pc("gkv", colvec(f("g_kv")))
    for j in range(2):
        wc = f("w_conv_a")[j]
        pc(f"cw{j}", np.ascontiguousarray(wc.reshape(4, 6, 128).transpose(2, 1, 0)).reshape(128, 24))
        pc(f"cb{j}", colvec(f("b_conv_a")[j]))
        pc(f"br{j}", colvec(f("b_rg_r")[j].reshape(768)))
        pc(f"bi{j}", colvec(f("b_rg_i")[j].reshape(768)))
        pc(f"lam{j}", colvec(f("lru_lambda")[j]))
        sk = f("sinks_b")[j]
        pc(f"sk{j}", np.ascontiguousarray(np.repeat(sk.reshape(6, 2), 64, axis=1).T))
    sh["cols"] = cols
    sh["dt5"] = build_dt5()
    return sh


def prep_core(inp, core):
    x = np.asarray(inp["x"], dtype=np.float32)
    mem = np.asarray(inp["mem"], dtype=np.float32)
    xs = x[2 * core:2 * core + 2]
    xT = np.ascontiguousarray(xs.reshape(2, SEQ, 8, 128).transpose(0, 3, 2, 1))
    ms = mem[2 * core:2 * core + 2]
    memT = np.ascontiguousarray(ms.reshape(2, 256, 8, 128).transpose(0, 3, 2, 1))
    return {"xT": xT, "memT": memT}


def build(n_layers=4, n_seq=2, dbg="full"):
    nc = bass.Bass("TRN2", target_bir_lowering=False)
    P = Prog()
    LU = [layer_units(l) for l in range(4)]

    def din(name, shape):
        return nc.dram_tensor(name, shape, F32, kind="ExternalInput").ap()
    xT = din("xT", [2, 128, 8, SEQ])
    memT = din("memT", [2, 128, 8, 256])
    wl = [din(f"w{l}", [128, LU[l][1]]) for l in range(4)]
    wm = din("wm", [128, 4 * 4096])
    wk = din("wk", [128, 4096 + 2048])
    colsd = din("cols", [128, NCOL])
    dt5d = din("dt5", [128, 1024])
    yT = nc.dram_tensor("yT", [2, 128, 8, SEQ], F32, kind="ExternalOutput").ap()

    with ExitStack() as es:
        def sb(name, shape, dt):
            return es.enter_context(nc.sbuf_tensor(name, shape, dt))
        XR = sb("XR", [128, 8 * ST], F32)
        RG = sb("RG", [128, 4 * 4096], BF16)
        PG = sb("PG", [128, 29 * 512], F32)
        PGb = PG.bitcast(BF16)
        HT = sb("HT", [128, 8 * ST], BF16)
        FB1 = HT.bitcast(F32)
        ET = sb("ET", [128, 4 * 1026], F32)
        SQ = ET.bitcast(BF16)
        FB0 = sb("FB0", [128, 8 * 512], F32)
        RS = sb("RS", [128, 2 * 512], F32)
        T1 = sb("T1", [128, 512], F32)
        KT = sb("KT", [128, 4 * 1152], BF16)
        VT = sb("VT", [128, 9 * 256], BF16)
        KM = sb("KM", [128, 4 * 2 * 256], BF16)
        VM = sb("VM", [128, 4 * 2 * 256], BF16)
        DT5 = sb("DT5", [128, 1024], F32)
        COLS = sb("COLS", [128, NCOL], F32)
        DER = sb("DER", [128, 64], F32)
        CST = sb("CST", [128, 4], F32)
        ONESM = sb("ONESM", [128, 128], BF16)
        ONES64 = sb("ONES64", [128, 64], BF16)
        TAILS = sb("TAILS", [128, 4 * 44 * 2], F32)
        UXT = sb("UXT", [128, 2 * 6 * 3], F32)
        HST = sb("HST", [128, 2 * 6], F32)
        PS = es.enter_context(nc.psum_tensor("PS", [128, 8 * 512], F32))

        XB = P.bufs(2, "xb")
        RB = P.bufs(4, "ring")
        PB = P.bufs(29, "pg")
        HB = [[P.buf(f"ht{kc}_{t}") for t in range(2)] for kc in range(8)]
        F1 = P.bufs(8, "fb1")
        for m in range(8):
            alias([F1[m]], HB[m])
        EB = P.bufs(4, "ext")
        SQB = P.bufs(8, "sq")
        alias(SQB, [EB[0], EB[1]])
        F0 = P.bufs(8, "fb0")
        AC = P.bufs(4, "acc")
        for i in range(4):
            alias([AC[i]], [F0[2 * i], F0[2 * i + 1]])
        RSB = P.bufs(2, "rs")
        T1B = P.buf("t1")
        KB = P.bufs(9, "kt")
        VB = P.bufs(9, "vt")
        KMB = P.bufs(4, "km")
        VMB = P.bufs(4, "vm")
        CB = P.buf("consts")
        DERB = P.buf("der")
        TLB = [[P.buf(f"tl{l}_{c}") for c in range(44)] for l in range(4)]
        UXB = [[P.buf(f"ux{j}_{c}") for c in range(6)] for j in range(2)]
        HSB = [[P.buf(f"hs{j}_{c}") for c in range(6)] for j in range(2)]
        BK = P.bufs(8, "bank")
        for b in BK:
            b.excl = True

        def bank(i, n=512, off=0):
            return PS[:, i * 512 + off:i * 512 + off + n]

        def pgf(i, a=0, b=512):
            return PG[:, i * 512 + a:i * 512 + b]

        def pgb(i, a=0, b=1024):
            return PGb[:, i * 1024 + a:i * 1024 + b]

        def xr(kc, tc):
            return XR[:, kc * ST + tc * TC:kc * ST + (tc + 1) * TC]

        def ht(kc, a, b):
            return HT[:, kc * ST + a:kc * ST + b]

        def sq(kc, n=512):
            return SQ[:, kc * 512:kc * 512 + n]

        def col(name, i=0):
            c = COLMAP[name] + i
            return COLS[:, c:c + 1]

        DERMAP = {}
        dn = 0
        for j in range(2):
            for n_ in ("nbr", "nbi", "cl", "cl2", "esk"):
                DERMAP[f"{n_}{j}"] = dn
                dn += 6

        def der(name, i=0):
            c = DERMAP[name] + i
            return DER[:, c:c + 1]

        def MM(out, lhsT, rhs, start, stop, rd, wr):
            P.op("pe", lambda e: e.matmul(out, lhsT, rhs, start=start, stop=stop), reads=rd, writes=wr)

        def ACTV(out, in_, func, rd, wr, bias=None, scale=1.0):
            if bias is None:
                P.op("act", lambda e: e.activation(out=out, in_=in_, func=func, scale=scale), reads=rd, writes=wr)
            else:
                P.op("act", lambda e: e.activation(out=out, in_=in_, func=func, bias=bias, scale=scale),
                     reads=rd, writes=wr)

        def TS(out, in0, s1, s2, op0, op1, rd, wr, eng="dve"):
            if s2 is None:
                P.op(eng, lambda e: e.tensor_scalar(out=out, in0=in0, scalar1=s1, scalar2=None, op0=op0),
                     reads=rd, writes=wr)
            else:
                P.op(eng, lambda e: e.tensor_scalar(out=out, in0=in0, scalar1=s1, scalar2=s2, op0=op0, op1=op1),
                     reads=rd, writes=wr)

        def STT(out, in0, scalar, in1, op0, op1, rd, wr, eng="dve"):
            P.op(eng, lambda e: e.scalar_tensor_tensor(out=out, in0=in0, scalar=scalar, in1=in1, op0=op0, op1=op1),
                 reads=rd, writes=wr)

        def TT(out, in0, in1, op, rd, wr, eng="dve"):
            P.op(eng, lambda e: e.tensor_tensor(out=out, in0=in0, in1=in1, op=op), reads=rd, writes=wr)

        def CP(out, in_, rd, wr, eng="dve"):
            P.op(eng, lambda e: e.tensor_copy(out=out, in_=in_), reads=rd, writes=wr)

        def MSET(ap, v, wr, eng="dve"):
            P.op(eng, lambda e: e.memset(ap, v), writes=wr)

        bstate = {"s": 0, "p": 0}

        def nb1():
            b = bstate["s"]
            bstate["s"] = (b + 1) % 6
            return b

        def nb2():
            b = bstate["p"]
            bstate["p"] = (b + 1) % 3
            return 2 * b

        rstate = {"n": 0}

        class Slot(int):
            pass
        slot_gen = [0, 0, 0, 0]

        def load_unit(src_ap, ln):
            s = Slot(rstate["n"] % 4)
            rstate["n"] += 1
            slot_gen[s] += 1
            s.gen = slot_gen[s]
            dst = RG[:, s * 4096:s * 4096 + ln]
            P.op("pool", lambda e: e.dma_start(out=dst, in_=src_ap), writes=[RB[s]], dma=f"ring{s}")
            return s

        def chk(s):
            assert s.gen == slot_gen[s], "weight ring slot was recycled before its last use"
            return s

        def lu(l, name):
            o, ln = LU[l][0][name]
            return load_unit(wl[l][:, o:o + ln], ln)

        def rg(s, kc, c0, n, width=512):
            chk(s)
            base = s * 4096 + kc * width + c0
            return RG[:, base:base + n]

        P.op("sp", lambda e: e.dma_start(out=COLS[:, :], in_=colsd), writes=[CB], dma="cols")
        P.op("sp", lambda e: e.dma_start(out=DT5[:, :], in_=dt5d), writes=[CB], dma="cols")
        MSET(CST[:, 0:1], EPS, [CB])
        MSET(CST[:, 1:2], 1.0, [CB])
        MSET(CST[:, 2:3], 0.0, [CB])
        MSET(ONESM[:, :], 1.0 / 1024.0, [CB])
        MSET(ONES64[:, :], 1.0, [CB])
        for j in range(2):
            TS(DER[:, DERMAP[f"nbr{j}"]:DERMAP[f"nbr{j}"] + 6], COLS[:, COLMAP[f"br{j}"]:COLMAP[f"br{j}"] + 6],
               -1.0, None, ALU.mult, None, [CB], [DERB])
            TS(DER[:, DERMAP[f"nbi{j}"]:DERMAP[f"nbi{j}"] + 6], COLS[:, COLMAP[f"bi{j}"]:COLMAP[f"bi{j}"] + 6],
               -1.0, None, ALU.mult, None, [CB], [DERB])
            cl = DER[:, DERMAP[f"cl{j}"]:DERMAP[f"cl{j}"] + 6]
            cl2 = DER[:, DERMAP[f"cl2{j}"]:DERMAP[f"cl2{j}"] + 6]
            ACTV(cl, COLS[:, COLMAP[f"lam{j}"]:COLMAP[f"lam{j}"] + 6], AF.Exp, [CB], [DERB], scale=-1.0)
            ACTV(cl, cl, AF.Ln, [DERB, CB], [DERB], bias=CST[:, 1:2])
            TS(cl2, cl, -16.0, None, ALU.mult, None, [DERB], [DERB])
            TS(cl, cl, -8.0, None, ALU.mult, None, [DERB], [DERB])
            ACTV(DER[:, DERMAP[f"esk{j}"]:DERMAP[f"esk{j}"] + 6], COLS[:, COLMAP[f"sk{j}"]:COLMAP[f"sk{j}"] + 6],
                 AF.Exp, [CB], [DERB])

        def rstd_from_bank(bk, rs_i, n=512):
            ACTV(T1[:, 0:n], bank(bk, n), AF.Ln, [BK[bk], CB], [T1B], bias=CST[:, 0:1])
            ACTV(RS[:, rs_i * 512:rs_i * 512 + n], T1[:, 0:n], AF.Exp, [T1B], [RSB[rs_i]], scale=-0.5)

        def prenorm(tc, gname):
            sbk = 6 + tc
            ACTV(SQ[:, 0:4096].rearrange("p (k n) -> p k n", k=8),
                 XR[:, :].rearrange("p (k n) -> p k n", k=8)[:, :, tc * TC:(tc + 1) * TC], AF.Square, [XB[tc]], list(SQB))
            for kc in range(8):
                MM(bank(sbk), ONESM[:, :], sq(kc), kc == 0, kc == 7, [SQB[kc], CB], [BK[sbk]])
            rstd_from_bank(sbk, tc)
            for kc in range(8):
                STT(ht(kc, tc * TC, (tc + 1) * TC), xr(kc, tc), col(gname, kc), RS[:, tc * 512:(tc + 1) * 512],
                    ALU.mult, ALU.mult, [XB[tc], RSB[tc], CB], [HB[kc][tc]])

        def post_evac(m, bk, tc, fbv, fbB, gname, sqi=None):
            sbk = 6 + tc
            sqi = m if sqi is None else sqi
            ACTV(sq(sqi), bank(bk), AF.Square, [BK[bk]], [SQB[sqi]])
            TS(fbv, bank(bk), col(gname, m), None, ALU.mult, None, [BK[bk], CB], [fbB])
            return lambda: MM(bank(sbk), ONESM[:, :], sq(sqi), m == 0, m == 7, [SQB[sqi], CB], [BK[sbk]])

        def post_finish(tc, fbview, fbBs):
            sbk = 6 + tc
            rstd_from_bank(sbk, tc)
            FBT = FB0 if fbBs is F0 else FB1
            fb3 = FBT[:, 0:4096].rearrange("p (m n) -> p m n", m=8)
            rs3 = RS[:, tc * 512:(tc + 1) * 512].unsqueeze(1).broadcast_to([128, 8, 512])
            x3 = XR[:, :].rearrange("p (k n) -> p k n", k=8)[:, :, tc * TC:(tc + 1) * TC]
            TT(fb3, fb3, rs3, ALU.mult, list(fbBs) + [RSB[tc]], list(fbBs))
            TT(x3, x3, fb3, ALU.add, list(fbBs) + [XB[tc]], [XB[tc]])

        def fb0v(m):
            return FB0[:, m * 512:(m + 1) * 512]

        def fb1v(m):
            return FB1[:, m * 512:(m + 1) * 512]

        def mem_attention(l, qpages, ypage_of, tmp_pages):
            tp = 0
            for hp in range(2):
                nbk = nb1()
                dbk = nb1()
                ptp = []
                for hh in range(2):
                    h = 2 * hp + hh
                    o = 64 * hh
                    sb2 = nb2()
                    for mb in range(2):
                        MM(bank(sb2 + mb), KM[o:o + 64, (l * 2 + hp) * 256 + mb * 128:(l * 2 + hp) * 256 + (mb + 1) * 128],
                           pgb(qpages[hp], 0, 512)[o:o + 64, :], True, True,
                           [KMB[l], PB[qpages[hp]]], [BK[sb2 + mb]])
                    pt = tmp_pages[tp % len(tmp_pages)]
                    tp += 1
                    ACTV(pgb(pt), PS[:, sb2 * 512:sb2 * 512 + 1024], AF.Exp, [BK[sb2], BK[sb2 + 1]], [PB[pt]],
                         scale=0.125)
                    ptp.append(pt)
                for hh in range(2):
                    h = 2 * hp + hh
                    o = 64 * hh
                    pt = ptp[hh]
                    for mb in range(2):
                        MM(PS[o:o + 64, nbk * 512:(nbk + 1) * 512],
                           VM[:, (l * 2 + mb) * 256 + h * 64:(l * 2 + mb) * 256 + (h + 1) * 64],
                           pgb(pt, mb * 512, (mb + 1) * 512), mb == 0, mb == 1, [VMB[l], PB[pt]], [BK[nbk]])
                    for mb in range(2):
                        MM(PS[o:o + 64, dbk * 512:(dbk + 1) * 512], ONES64[:, :],
                           pgb(pt, mb * 512, (mb + 1) * 512), mb == 0, mb == 1, [CB, PB[pt]], [BK[dbk]])
                yap, yb = ypage_of(hp)
                tq = tmp_pages[tp % len(tmp_pages)]
                tp += 1
                ACTV(pgf(tq), bank(dbk), AF.Ln, [BK[dbk]], [PB[tq]])
                ACTV(pgf(tq), pgf(tq), AF.Exp, [PB[tq]], [PB[tq]], scale=-1.0)
                TT(yap, bank(nbk), pgf(tq), ALU.mult, [BK[nbk], PB[tq]], [yb])

        def out_proj_and_post(l, ycat_ap, ycat_bufs, tc):
            s0 = lu(l, "O0")
            s1 = lu(l, "O1")
            pend = None
            for m in range(8):
                s = s0 if m < 4 else s1
                bk = nb1()
                for kc in range(8):
                    MM(bank(bk), rg(s, kc, (m % 4) * 128, 128), ycat_ap(kc), kc == 0, kc == 7,
                       [RB[s], ycat_bufs[kc]], [BK[bk]])
                if pend is not None:
                    pend()
                pend = post_evac(m, bk, tc, fb0v(m), F0[m], f"gmo{l}")
            pend()
            post_finish(tc, fb0v, F0)

        def mixer_p1(l, tc):
            if l == 2:
                kv_project(tc)
            prenorm(tc, f"gmp{l}")

        def mixer_a(l, tc):
            j = l
            sA0 = lu(l, "A0")
            sA1 = lu(l, "A1")
            GG = [0, 1, 2, 3, 4, 5]
            YC = [6, 7, 8, 9]
            QM = [10, 11]
            TMP = list(range(12, 22))
            tstate = {"n": 0}

            def tmp():
                t = TMP[tstate["n"] % len(TMP)]
                tstate["n"] += 1
                return t

            def ycat_ap(kc):
                return pgb(YC[kc // 2], (kc % 2) * 512, (kc % 2 + 1) * 512)
            ycat_bufs = [PB[YC[kc // 2]] for kc in range(8)]
            hts = (tc * TC, (tc + 1) * TC)
            for c in range(6):
                s, c0 = (sA0, c * 128) if c < 4 else (sA1, (c - 4) * 128)
                bk = nb1()
                for kc in range(8):
                    MM(bank(bk), rg(s, kc, c0, 128), ht(kc, *hts), kc == 0, kc == 7, [RB[s], HB[kc][tc]], [BK[bk]])
                ACTV(pgf(GG[c]), bank(bk), AF.Gelu_apprx_tanh, [BK[bk]], [PB[GG[c]]])
            sG = lu(l, "G")
            sA2 = lu(l, "A2")

            def chain(c, q):
                s, c0 = (sA1, 256 + c * 128) if c < 2 else (sA2, (c - 2) * 128)
                base = 10 + 5 * q
                ta, t2, t3, t4, t5 = range(base, base + 5)
                tb = 25 + q // 2
                tbo = (q % 2) * 512
                ei = q + 1
                E = ET[:, ei * 1026:ei * 1026 + 515]
                uo = (j * 6 + c) * 3
                acc = pgf(ta)
                hso = j * 6 + c
                st = {}

                def s0():
                    bk = nb1()
                    st["bk"] = bk
                    for kc in range(8):
                        MM(bank(bk), rg(s, kc, c0, 128), ht(kc, *hts), kc == 0, kc == 7, [RB[s], HB[kc][tc]], [BK[bk]])

                def s1():
                    bk = st["bk"]
                    CP(E[:, 0:3], UXT[:, uo:uo + 3], [UXB[j][c]], [EB[ei]])
                    ACTV(E[:, 3:515], bank(bk), AF.Identity, [BK[bk]], [EB[ei]])
                    CP(UXT[:, uo:uo + 3], E[:, 512:515], [EB[ei]], [UXB[j][c]])
                    ACTV(acc, E[:, 3:515], AF.Identity, [EB[ei], CB], [PB[ta]], bias=col(f"cb{j}", c),
                         scale=col(f"cw{j}", c * 4 + 3))

                def s2():
                    for k in range(3):
                        STT(acc, E[:, k:k + 512], col(f"cw{j}", c * 4 + k), acc, ALU.mult, ALU.add,
                            [EB[ei], PB[ta], CB], [PB[ta]])
                    CP(pgb(tb, tbo, tbo + 512), acc, [PB[ta]], [PB[tb]])

                def s3():
                    bkr = nb1()
                    MM(bank(bkr), RG[:, chk(sG) * 4096 + c * 128:sG * 4096 + (c + 1) * 128], pgb(tb, tbo, tbo + 512), True, True,
                       [RB[sG], PB[tb]], [BK[bkr]])
                    bki = nb1()
                    MM(bank(bki), RG[:, sG * 4096 + (6 + c) * 128:sG * 4096 + (7 + c) * 128], pgb(tb, tbo, tbo + 512),
                       True, True, [RB[sG], PB[tb]], [BK[bki]])
                    st["bkr"], st["bki"] = bkr, bki

                def s4():
                    bkr, bki = st["bkr"], st["bki"]
                    ACTV(pgf(t2), bank(bkr), AF.Exp, [BK[bkr], DERB], [PB[t2]], bias=der(f"nbr{j}", c), scale=-1.0)
                    ACTV(pgf(t3), bank(bki), AF.Exp, [BK[bki], DERB], [PB[t3]], bias=der(f"nbi{j}", c), scale=-1.0)

                def s5():
                    ACTV(pgf(t2), pgf(t2), AF.Ln, [PB[t2], CB], [PB[t2]], bias=CST[:, 1:2])
                    ACTV(pgf(t3), pgf(t3), AF.Ln, [PB[t3], CB], [PB[t3]], bias=CST[:, 1:2])

                def s6():
                    ACTV(pgf(t2), pgf(t2), AF.Exp, [PB[t2]], [PB[t2]], scale=-1.0)

                def s7():
                    ACTV(pgf(t4), pgf(t2), AF.Exp, [PB[t2], DERB], [PB[t4]], scale=der(f"cl{j}", c))
                    ACTV(pgf(t5), pgf(t2), AF.Exp, [PB[t2], DERB], [PB[t5]], scale=der(f"cl2{j}", c))

                def s8():
                    TS(pgf(t5), pgf(t5), 0.99999994, None, ALU.min, None, [PB[t5]], [PB[t5]])

                def s9():
                    ACTV(pgf(t5), pgf(t5), AF.Ln, [PB[t5], CB], [PB[t5]], bias=CST[:, 1:2], scale=-1.0)

                def s10():
                    STT(pgf(t5), pgf(t5), 0.5, pgf(t3), ALU.mult, ALU.subtract, [PB[t5], PB[t3]], [PB[t5]])

                def s11():
                    ACTV(pgf(t5), pgf(t5), AF.Exp, [PB[t5]], [PB[t5]])

                def s12():
                    TT(pgf(t5), pgf(t5), acc, ALU.mult, [PB[t5], PB[ta]], [PB[t5]])
                    P.op("dve", lambda e, o_=pgf(t3), a_=pgf(t4), b_=pgf(t5), i_=HST[:, hso:hso + 1]:
                         e.tensor_tensor_scan(out=o_, data0=a_, data1=b_, initial=i_, op0=ALU.mult, op1=ALU.add),
                         reads=[PB[t4], PB[t5], HSB[j][c]], writes=[PB[t3]])
                    CP(HST[:, hso:hso + 1], pgf(t3, 511, 512), [PB[t3]], [HSB[j][c]])
                    TT(ycat_ap(c), pgf(t3), pgf(GG[c]), ALU.mult, [PB[t3], PB[GG[c]]], [ycat_bufs[c]])

                return [s0, s1, s2, s3, s4, s5, s6, s7, s8, s9, s10, s11, s12]

            sA3 = lu(l, "A3")
            for grp in range(2):
                ch = [chain(3 * grp + q, q) for q in range(3)]
                for k in range(len(ch[0])):
                    for q in range(3):
                        ch[q][k]()
            for hp in range(2):
                bk = nb1()
                for kc in range(8):
                    MM(bank(bk), rg(sA3, kc, hp * 128, 128, width=256), ht(kc, *hts), kc == 0, kc == 7,
                       [RB[sA3], HB[kc][tc]], [BK[bk]])
                ACTV(pgb(QM[hp], 0, 512), bank(bk), AF.Identity, [BK[bk]], [PB[QM[hp]]])
            mem_attention(l, QM, lambda hp: (ycat_ap(6 + hp), ycat_bufs[6 + hp]), TMP)
            return lambda: out_proj_and_post(l, ycat_ap, ycat_bufs, tc)

        def kv_project(tc):
            prenorm(tc, "gkv")
            hts = (tc * TC, (tc + 1) * TC)
            s0 = load_unit(wk[:, 0:4096], 4096)
            s1 = load_unit(wk[:, 4096:6144], 2048)
            for k in range(4):
                bk = nb1()
                for kc in range(8):
                    MM(bank(bk), rg(s0, kc, k * 128, 128), ht(kc, *hts), kc == 0, kc == 7, [RB[s0], HB[kc][tc]], [BK[bk]])
                ACTV(KT[:, k * 1152 + 128 + tc * 512:k * 1152 + 128 + (tc + 1) * 512], bank(bk), AF.Identity, [BK[bk]],
                     [KB[1 + 4 * tc + i] for i in range(4)])
            for tb in range(4):
                bk = nb1()
                for kc in range(8):
                    MM(bank(bk, 256), ht(kc, tc * TC + tb * 128, tc * TC + (tb + 1) * 128), rg(s1, kc, 0, 256, width=256),
                       kc == 0, kc == 7, [RB[s1], HB[kc][tc]], [BK[bk]])
                idx = 1 + 4 * tc + tb
                ACTV(VT[:, idx * 256:(idx + 1) * 256], bank(bk, 256), AF.Identity, [BK[bk]], [VB[idx]])

        def mixer_b(l, tc, first):
            j = l - 2
            hts = (tc * TC, (tc + 1) * TC)
            sB0 = lu(l, "B0")
            sB1 = lu(l, "B1")
            QP = list(range(0, 8))
            YC = [8, 9, 10, 11]
            TMP = list(range(16, 22))
            tstate = {"n": 0}
            sstate = {"n": 0}

            def tmp():
                t = TMP[tstate["n"] % len(TMP)]
                tstate["n"] += 1
                return t

            def ycat_ap(kc):
                return pgb(YC[kc // 2], (kc % 2) * 512, (kc % 2 + 1) * 512)
            ycat_bufs = [PB[YC[kc // 2]] for kc in range(8)]
            for cq in range(8):
                s, c0 = (sB0, cq * 128) if cq < 4 else (sB1, (cq - 4) * 128)
                bk = nb1()
                for kc in range(8):
                    MM(bank(bk), rg(s, kc, c0, 128), ht(kc, *hts), kc == 0, kc == 7, [RB[s], HB[kc][tc]], [BK[bk]])
                ACTV(pgb(QP[cq], 0, 512), bank(bk), AF.Identity, [BK[bk]], [PB[QP[cq]]])
            noprev = first and tc == 0
            ncol = 896 if noprev else 1024

            def stA(cq):
                pts = []
                for hh in range(2):
                    h = 2 * cq + hh
                    o = 64 * hh
                    k = h // 3
                    sb2 = 2 * hh
                    q = pgb(QP[cq], 0, 512)
                    for i in range(5):
                        if i == 0 and noprev:
                            continue
                        sidx = 4 * tc + i
                        kap = KT[o:o + 64, k * 1152 + sidx * 128:k * 1152 + (sidx + 1) * 128]
                        if i == 0:
                            c0, qa, qb = 896, 0, 128
                        elif i == 4:
                            c0, qa, qb = 768, 384, 512
                        else:
                            c0, qa, qb = (i - 1) * 256, (i - 1) * 128, (i + 1) * 128
                        bki = sb2 + c0 // 512
                        MM(PS[:, sb2 * 512 + c0:sb2 * 512 + c0 + (qb - qa)], kap, q[o:o + 64, qa:qb], True, True,
                           [KB[sidx], PB[QP[cq]]], [BK[bki]])
                    sp = 12 + 2 * (sstate["n"] % 2)
                    sstate["n"] += 1
                    spv = PG[:, sp * 512:sp * 512 + ncol]
                    STT(spv, DT5[:, 0:ncol], -SLOPES[h], PS[:, sb2 * 512:sb2 * 512 + ncol], ALU.mult, ALU.add,
                        [CB, BK[sb2], BK[sb2 + 1]], [PB[sp], PB[sp + 1]])
                    pt = tmp()
                    ACTV(pgb(pt, 0, ncol), spv, AF.Exp, [PB[sp], PB[sp + 1]], [PB[pt]], scale=0.125)
                    pts.append(pt)
                return pts

            def stB(cq, pts):
                nbk, dbk = (4, 5) if cq % 2 == 0 else (6, 7)
                for hh in range(2):
                    h = 2 * cq + hh
                    o = 64 * hh
                    k = h // 3
                    pt = pts[hh]
                    for qb_ in range(4):
                        srcs = []
                        if not (noprev and qb_ == 0):
                            pc0 = 896 if qb_ == 0 else (qb_ - 1) * 256 + 128
                            srcs.append((4 * tc + qb_, pc0))
                        cc0 = 768 if qb_ == 3 else qb_ * 256
                        srcs.append((4 * tc + qb_ + 1, cc0))
                        for which in ("n", "d"):
                            bkx = nbk if which == "n" else dbk
                            for si, (sidx, c0) in enumerate(srcs):
                                if which == "n":
                                    lhs = VT[:, sidx * 256 + k * 64:sidx * 256 + (k + 1) * 64]
                                    rd = [VB[sidx], PB[pt]]
                                else:
                                    lhs = ONES64[:, :]
                                    rd = [CB, PB[pt]]
                                MM(PS[o:o + 64, bkx * 512 + qb_ * 128:bkx * 512 + (qb_ + 1) * 128], lhs,
                                   pgb(pt, c0, c0 + 128), si == 0, si == len(srcs) - 1, rd, [BK[bkx]])

            def stC(cq):
                nbk, dbk = (4, 5) if cq % 2 == 0 else (6, 7)
                tq = tmp()
                ACTV(pgf(tq), bank(dbk), AF.Ln, [BK[dbk], DERB], [PB[tq]], bias=der(f"esk{j}", cq))
                ACTV(pgf(tq), pgf(tq), AF.Exp, [PB[tq]], [PB[tq]], scale=-1.0)
                TT(ycat_ap(cq), bank(nbk), pgf(tq), ALU.mult, [BK[nbk], PB[tq]], [ycat_bufs[cq]])

            nxt = stA(0)
            for cq in range(6):
                cur = nxt
                if cq + 1 < 6:
                    nxt = stA(cq + 1)
                stB(cq, cur)
                stC(cq)
            mem_attention(l, [QP[6], QP[7]], lambda hp: (ycat_ap(6 + hp), ycat_bufs[6 + hp]), TMP)
            return lambda: out_proj_and_post(l, ycat_ap, ycat_bufs, tc)

        def ffn(l, next_p1=None):
            units = {}

            def stage1(jn):
                j2, jj = divmod(jn, 2)
                if jj == 0:
                    units[j2] = lu(l, f"U{j2}")
                s = units[j2]
                st_ = jn % 2
                gb, vb = (0, 2) if st_ == 0 else (4, 6)
                for b0, c0 in ((gb, jj * 128), (vb, 256 + jj * 128)):
                    for tc in range(2):
                        for kc in range(8):
                            MM(bank(b0 + tc), rg(s, kc, c0, 128), ht(kc, tc * TC, (tc + 1) * TC), kc == 0, kc == 7,
                               [RB[s], HB[kc][tc]], [BK[b0 + tc]])

            def halves(jn):
                st_ = jn % 2
                gb, vb = (0, 2) if st_ == 0 else (4, 6)
                out = []
                for hi, (b0, ch) in enumerate(((gb, jn), (vb, 22 + jn))):
                    ei = 2 * st_ + hi
                    out.append((b0, ch, ei, ET[:, ei * 1026:(ei + 1) * 1026], (l * 44 + ch) * 2,
                                FB0[:, ei * 1024:(ei + 1) * 1024]))
                return out

            def stage2a(jn):
                for b0, ch, ei, E, to, acc in halves(jn):
                    CP(E[:, 0:2], TAILS[:, to:to + 2], [TLB[l][ch]], [EB[ei]])

            def stage2b(jn):
                for b0, ch, ei, E, to, acc in halves(jn):
                    ACTV(E[:, 2:1026], PS[:, b0 * 512:b0 * 512 + 1024], AF.Identity, [BK[b0], BK[b0 + 1]], [EB[ei]])
                for b0, ch, ei, E, to, acc in halves(jn):
                    ACTV(acc, E[:, 2:1026], AF.Identity, [EB[ei], CB], [AC[ei]], bias=col(f"fcb{l}", ch),
                         scale=col(f"fcw{l}", ch * 3 + 2))

            def stage2c(jn):
                accs = []
                for b0, ch, ei, E, to, acc in halves(jn):
                    CP(TAILS[:, to:to + 2], E[:, 1024:1026], [EB[ei]], [TLB[l][ch]])
                for b0, ch, ei, E, to, acc in halves(jn):
                    for k in range(2):
                        STT(acc, E[:, k:k + 1024], col(f"fcw{l}", ch * 3 + k), acc, ALU.mult, ALU.add,
                            [EB[ei], AC[ei], CB], [AC[ei]])
                    accs.append((acc, AC[ei]))
                return accs

            def stage3(jn, accs):
                ACTV(accs[0][0], accs[0][0], AF.Gelu_apprx_tanh, [accs[0][1]], [accs[0][1]])
                TT(pgb(jn), accs[1][0], accs[0][0], ALU.mult, [accs[0][1], accs[1][1]], [PB[jn]])

            prev = None
            stage2a(0)
            for jn in range(NJ):
                stage1(jn)
                stage2b(jn)
                if jn + 1 < NJ:
                    stage2a(jn + 1)
                accs = stage2c(jn)
                if prev is not None:
                    stage3(*prev)
                prev = (jn, accs)
            stage3(*prev)
            pend = None

            def down_mms(s, bk, tc, j0, j1):
                for jn in range(j0, j1):
                    MM(bank(bk), RG[:, chk(s) * 4096 + jn * 128:s * 4096 + (jn + 1) * 128],
                       pgb(jn, tc * 512, (tc + 1) * 512), jn == 0, jn == NJ - 1, [RB[s], PB[jn]], [BK[bk]])

            NW = 3
            wave = {}
            for m in range(NW):
                s = lu(l, f"D{m}")
                for tc in range(2):
                    bk = nb1()
                    wave[(m, tc)] = (s, bk)
                    down_mms(s, bk, tc, 0, NJ - 2)
            for m in range(8):
                if m >= NW:
                    s = lu(l, f"D{m}")
                for tc in range(2):
                    if m < NW:
                        s, bk = wave[(m, tc)]
                        down_mms(s, bk, tc, NJ - 2, NJ)
                    else:
                        bk = nb1()
                        down_mms(s, bk, tc, 0, NJ)
                    if pend is not None:
                        pend()
                    sqi = (2 * m + tc) % 8
                    if tc == 0:
                        pend = post_evac(m, bk, 0, fb1v(m), F1[m], f"gfo{l}", sqi)
                    else:
                        pend = post_evac(m, bk, 1, fb0v(m), F0[m], f"gfo{l}", sqi)
            pend()
            post_finish(0, fb1v, F1)
            if next_p1 is not None:
                next_p1()
            post_finish(1, fb0v, F0)

        def seq_prologue(s):
            for l in range(4):
                for ch in range(44):
                    pass
            P.op("dve", lambda e: e.memset(TAILS[:, :], 0.0), writes=[b for l in range(4) for b in TLB[l]])
            P.op("dve", lambda e: e.memset(UXT[:, :], 0.0), writes=[b for j in range(2) for b in UXB[j]])
            P.op("dve", lambda e: e.memset(HST[:, :], 0.0), writes=[b for j in range(2) for b in HSB[j]])
            MT = PG[:, 0:2048]
            P.op("sp", lambda e: e.dma_start(out=MT.rearrange("p (k n) -> p k n", k=8), in_=memT[s]),
                 writes=[PB[0], PB[1], PB[2], PB[3]], dma="mem")
            for l in range(n_layers):
                su = load_unit(wm[:, l * 4096:(l + 1) * 4096], 4096)
                for kc in range(8):
                    ACTV(sq(kc, 256), PG[:, kc * 256:(kc + 1) * 256], AF.Square, [PB[kc // 2]], [SQB[kc]])
                for kc in range(8):
                    MM(bank(6, 256), ONESM[:, :], sq(kc, 256), kc == 0, kc == 7, [SQB[kc], CB], [BK[6]])
                rstd_from_bank(6, 0, 256)
                for kc in range(8):
                    STT(PGb[:, 4 * 1024 + kc * 256:4 * 1024 + (kc + 1) * 256], PG[:, kc * 256:(kc + 1) * 256],
                        col(f"gme{l}", kc), RS[:, 0:256], ALU.mult, ALU.mult, [PB[kc // 2], RSB[0], CB],
                        [PB[4 + kc // 4]])

                def hm(kc, a=0, b=256):
                    return PGb[:, 4 * 1024 + kc * 256 + a:4 * 1024 + kc * 256 + b]
                for hp in range(2):
                    bk = nb1()
                    for kc in range(8):
                        MM(bank(bk, 256), rg(su, kc, hp * 128, 128), hm(kc), kc == 0, kc == 7,
                           [RB[su], PB[4 + kc // 4]], [BK[bk]])
                    ACTV(KM[:, (l * 2 + hp) * 256:(l * 2 + hp + 1) * 256], bank(bk, 256), AF.Identity, [BK[bk]], [KMB[l]])
                for mb in range(2):
                    bk = nb1()
                    for kc in range(8):
                        MM(bank(bk, 256), hm(kc, mb * 128, (mb + 1) * 128), rg(su, kc, 256, 256), kc == 0, kc == 7,
                           [RB[su], PB[4 + kc // 4]], [BK[bk]])
                    ACTV(VM[:, (l * 2 + mb) * 256:(l * 2 + mb + 1) * 256], bank(bk, 256), AF.Identity, [BK[bk]], [VMB[l]])

        XR3 = XR[:, :].rearrange("p (k n) -> p k n", k=8)
        out_evs = []
        for s in range(n_seq):
            seq_prologue(s)
            for st in range(NST):
                P.op("sp", lambda e, s=s, st=st: e.dma_start(out=XR3, in_=xT[s, :, :, st * ST:(st + 1) * ST]),
                     writes=[XB[0], XB[1]], dma="xin")
                for l in range(n_layers):
                    mix = (lambda tc, l=l: mixer_a(l, tc)) if l < 2 else (lambda tc, l=l, st=st: mixer_b(l, tc, st == 0))
                    if l == 0:
                        mixer_p1(l, 0)
                    fin0 = mix(0)
                    mixer_p1(l, 1)
                    fin0()
                    fin1 = mix(1)
                    prenorm(0, f"gfp{l}")
                    fin1()
                    prenorm(1, f"gfp{l}")
                    ffn(l, (lambda l=l: mixer_p1(l + 1, 0)) if l + 1 < n_layers else None)
                if n_layers > 2 and st < NST - 1:
                    for k in range(4):
                        CP(KT[:, k * 1152:k * 1152 + 128], KT[:, k * 1152 + 1024:k * 1152 + 1152], [KB[8]], [KB[0]])
                    CP(VT[:, 0:256], VT[:, 8 * 256:9 * 256], [VB[8]], [VB[0]])
                o = P.op("sp", lambda e, s=s, st=st: e.dma_start(out=yT[s, :, :, st * ST:(st + 1) * ST], in_=XR3),
                         reads=[XB[0], XB[1]], dma="yout")
                out_evs.append(o.ev)
        P.emit(nc, final_waits=[out_evs[-1]])
    return nc


_CACHE = {}


def kernel(_n_layers=4, _cores=NCORES, **inputs):
    key = (_n_layers,)
    if key not in _CACHE:
        _CACHE[key] = build(_n_layers)
    nc = _CACHE[key]
    sh = prep_shared(inputs)
    in_maps = []
    for c in range(_cores):
        m = dict(sh)
        m.update(prep_core(inputs, c))
        in_maps.append(m)
    res = run_bass_kernel_spmd(nc, in_maps, core_ids=list(range(_cores)))
    outs = []
    for c in range(_cores):
        yT = np.asarray(res.results[c]["yT"])
        outs.append(np.ascontiguousarray(yT.transpose(0, 3, 2, 1)).reshape(2, SEQ, D))
    return np.concatenate(outs, axis=0).astype(np.float32)
```

```python
import math
from contextlib import ExitStack
import numpy as np
import concourse.bass as bass
import concourse.mybir as mybir
from concourse.bass_utils import run_bass_kernel_spmd

F32 = mybir.dt.float32
BF16 = mybir.dt.bfloat16
AF = mybir.ActivationFunctionType
ALU = mybir.AluOpType

NCORES = 8
SEQ = 2048
ST = 1024
TC = 512
NST = SEQ // ST
D = 1024
DFF = 2816
NJ = DFF // 128
BIG = 1.0e6
EPS = 1e-6

ENGS = ("pe", "act", "dve", "pool", "sp")


class Ev:
    __slots__ = ("kind", "eng", "idx", "key", "value", "needed")

    def __init__(self, kind, eng=None, idx=0, key=None, value=0):
        self.kind = kind
        self.eng = eng
        self.idx = idx
        self.key = key
        self.value = value
        self.needed = False


class Buf:
    __slots__ = ("name", "lw", "rd", "alias", "excl")

    def __init__(self, name):
        self.name = name
        self.lw = None
        self.rd = []
        self.alias = []
        self.excl = False


class Op:
    __slots__ = ("eng", "fn", "waits", "ev", "dma_key")

    def __init__(self, eng, fn):
        self.eng = eng
        self.fn = fn
        self.waits = []
        self.ev = None
        self.dma_key = None


def alias(a_list, b_list):
    for a in a_list:
        for b in b_list:
            if b not in a.alias:
                a.alias.append(b)
            if a not in b.alias:
                b.alias.append(a)


class Prog:
    def __init__(self):
        self.ops = {e: [] for e in ENGS}
        self.seen = {e: {} for e in ENGS}
        self.dma_cnt = {}
        self.nbuf = 0

    def buf(self, name=None):
        self.nbuf += 1
        return Buf(name or f"b{self.nbuf}")

    def bufs(self, n, name="b"):
        return [self.buf(f"{name}{i}") for i in range(n)]

    def _need(self, op, ev, raw):
        if ev is None:
            return
        e = op.eng
        if ev.kind == "eng":
            if ev.eng == e and e == "pe":
                return
            key = ev.eng
            pos = ev.idx
        else:
            key = ev.key
            pos = ev.value
        if self.seen[e].get(key, -1) >= pos:
            return
        self.seen[e][key] = pos
        ev.needed = True
        op.waits.append(ev)

    def op(self, eng, fn, reads=(), writes=(), dma=None):
        o = Op(eng, fn)
        lst = self.ops[eng]
        wr = []
        for b in writes:
            wr.append(b)
            wr.extend(b.alias)
        if eng != "pe":
            for b in reads:
                if b.excl and b not in wr:
                    wr.append(b)
        for b in reads:
            self._need(o, b.lw, True)
            for a in b.alias:
                self._need(o, a.lw, True)
        for b in wr:
            self._need(o, b.lw, False)
            for r in b.rd:
                self._need(o, r, False)
        if dma is not None:
            c = self.dma_cnt.get(dma, 0) + 16
            self.dma_cnt[dma] = c
            o.ev = Ev("dma", key=dma, value=c)
            o.dma_key = dma
        else:
            o.ev = Ev("eng", eng=eng, idx=len(lst))
        best = {}
        for w in o.waits:
            k = w.eng if w.kind == "eng" else w.key
            p = w.idx if w.kind == "eng" else w.value
            if k not in best or p > best[k][0]:
                best[k] = (p, w)
        o.waits = [v[1] for v in best.values()]
        for b in wr:
            b.lw = o.ev
            b.rd = []
        for b in reads:
            b.rd.append(o.ev)
        lst.append(o)
        return o

    def emit(self, nc, final_waits=()):
        for e in ENGS:
            c = 0
            for o in self.ops[e]:
                if o.ev.kind == "eng" and o.ev.needed:
                    c += 1
                    o.ev.value = c
        with ExitStack() as st:
            esem = {e: st.enter_context(nc.semaphore(f"s_{e}")) for e in ENGS}
            dsem = {k: st.enter_context(nc.semaphore(f"d_{k}")) for k in self.dma_cnt}
            block = st.enter_context(nc.Block())

            def semof(ev):
                return esem[ev.eng] if ev.kind == "eng" else dsem[ev.key]

            def run(e, eng):
                for o in self.ops[e]:
                    for w in o.waits:
                        eng.wait_ge(semof(w), w.value)
                    ins = o.fn(eng)
                    if o.dma_key is not None:
                        ins.then_inc(dsem[o.dma_key], 16)
                    elif o.ev.needed:
                        ins.then_inc(esem[e], 1)

            @block.tensor
            def _(eng):
                run("pe", eng)

            @block.scalar
            def _(eng):
                run("act", eng)

            @block.vector
            def _(eng):
                run("dve", eng)

            @block.gpsimd
            def _(eng):
                run("pool", eng)

            @block.sync
            def _(eng):
                run("sp", eng)
                for ev in final_waits:
                    eng.wait_ge(semof(ev), ev.value)


def alibi_slopes(n):
    def pow2_slopes(m):
        start = 2.0 ** (-8.0 / m)
        return [start ** (i + 1) for i in range(m)]
    c = 2 ** int(math.floor(math.log2(n)))
    s = pow2_slopes(c)
    if c != n:
        s = s + pow2_slopes(2 * c)[0::2][: n - c]
    return [float(np.float32(v)) for v in s]


SLOPES = alibi_slopes(12)


def kmajor(W):
    K, N = W.shape
    nk = K // 128
    return np.ascontiguousarray(W.reshape(nk, 128, N).transpose(1, 0, 2)).reshape(128, nk * N)


def colvec(v):
    n = v.shape[0] // 128
    return np.ascontiguousarray(v.reshape(n, 128).T)


def layer_units(l):
    units = {}
    off = 0

    def add(name, ln):
        nonlocal off
        units[name] = (off, ln)
        off += ln
    if l < 2:
        add("A0", 4096)
        add("A1", 4096)
        add("G", 1536)
        add("A2", 4096)
        add("A3", 2048)
    else:
        add("B0", 4096)
        add("B1", 4096)
    add("O0", 4096)
    add("O1", 4096)
    for j2 in range(11):
        add(f"U{j2}", 4096)
    for m in range(8):
        add(f"D{m}", 2816)
    return units, off


COLMAP = {}
_ncol = 0


def _addcol(name, w):
    global _ncol
    COLMAP[name] = _ncol
    _ncol += w


for _l in range(4):
    for _n in ("gmp", "gmo", "gfp", "gfo", "gme"):
        _addcol(f"{_n}{_l}", 8)
    _addcol(f"fcw{_l}", 44 * 3)
    _addcol(f"fcb{_l}", 44)
_addcol("gkv", 8)
for _j in range(2):
    _addcol(f"cw{_j}", 24)
    for _n in ("cb", "br", "bi", "lam"):
        _addcol(f"{_n}{_j}", 6)
    _addcol(f"sk{_j}", 6)
NCOL = _ncol


def build_dt5():
    s = np.arange(128)[:, None].astype(np.float64)
    t = np.arange(128)[None, :].astype(np.float64)
    cur = np.where(t >= s, 8.0 * (t - s), BIG)
    prev = np.where(s > t, 8.0 * (t + 128 - s), BIG)
    dt = np.concatenate([cur, prev], axis=1)
    dt5 = np.concatenate([dt, dt, dt, cur, prev], axis=1)
    return np.ascontiguousarray(dt5.astype(np.float32))


def prep_shared(inp):
    f = lambda k: np.asarray(inp[k], dtype=np.float32)
    sh = {}
    w_in_a, w_in_b = f("w_in_a"), f("w_in_b")
    w_mix_out, w_up, w_down = f("w_mix_out"), f("w_ffn_up"), f("w_ffn_down")
    w_rg_r, w_rg_i = f("w_rg_r"), f("w_rg_i")
    for l in range(4):
        units, tot = layer_units(l)
        arr = np.zeros((128, tot), np.float32)

        def put(name, a):
            o, ln = units[name]
            assert a.shape == (128, ln), (name, a.shape, ln)
            arr[:, o:o + ln] = a
        if l < 2:
            W = w_in_a[l]
            put("A0", kmajor(W[:, 0:512]))
            put("A1", kmajor(np.concatenate([W[:, 512:768], W[:, 768:1024]], axis=1)))
            put("A2", kmajor(W[:, 1024:1536]))
            put("A3", kmajor(W[:, 1536:1792]))
            g = np.zeros((128, 12, 128), np.float32)
            for gi, wg in enumerate((w_rg_r[l], w_rg_i[l])):
                for c in range(6):
                    g[0:64, gi * 6 + c, 0:64] = wg[2 * c]
                    g[64:128, gi * 6 + c, 64:128] = wg[2 * c + 1]
            put("G", g.reshape(128, 1536))
        else:
            W = w_in_b[l - 2]
            put("B0", kmajor(W[:, 0:512]))
            put("B1", kmajor(W[:, 512:1024]))
        put("O0", kmajor(w_mix_out[l][:, 0:512]))
        put("O1", kmajor(w_mix_out[l][:, 512:1024]))
        for j2 in range(11):
            put(f"U{j2}", kmajor(np.concatenate([w_up[l][:, j2 * 256:(j2 + 1) * 256],
                                                 w_up[l][:, DFF + j2 * 256:DFF + (j2 + 1) * 256]], axis=1)))
        for m in range(8):
            put(f"D{m}", kmajor(w_down[l][:, m * 128:(m + 1) * 128]))
        sh[f"w{l}"] = arr
    sh["wm"] = np.concatenate([kmajor(f("w_mem_kv")[l]) for l in range(4)], axis=1)
    wkv = f("w_kv")
    kd = np.concatenate([wkv[:, (k // 2) * 64:(k // 2) * 64 + 64] for k in range(8)], axis=1)
    sh["wk"] = np.concatenate([kmajor(kd), kmajor(wkv[:, 256:512])], axis=1)
    cols = np.zeros((128, NCOL), np.float32)

    def pc(name, a):
        cols[:, COLMAP[name]:COLMAP[name] + a.shape[1]] = a
    for l in range(4):
        pc(f"gmp{l}", colvec(f("g_mix_pre")[l]))
        pc(f"gmo{l}", colvec(f("g_mix_post")[l]))
        pc(f"gfp{l}", colvec(f("g_ffn_pre")[l]))
        pc(f"gfo{l}", colvec(f("g_ffn_post")[l]))
        pc(f"gme{l}", colvec(f("g_mem")[l]))
        wc = f("w_ffn_conv")[l]
        pc(f"fcw{l}", np.ascontiguousarray(wc.reshape(3, 44, 128).transpose(2, 1, 0)).reshape(128, 132))
        pc(f"fcb{l}", colvec(f("b_ffn_conv")[l]))
    pc("gkv", colvec(f("g_kv")))
    for j in range(2):
        wc = f("w_conv_a")[j]
        pc(f"cw{j}", np.ascontiguousarray(wc.reshape(4, 6, 128).transpose(2, 1, 0)).reshape(128, 24))
        pc(f"cb{j}", colvec(f("b_conv_a")[j]))
        pc(f"br{j}", colvec(f("b_rg_r")[j].reshape(768)))
        pc(f"bi{j}", colvec(f("b_rg_i")[j].reshape(768)))
        pc(f"lam{j}", colvec(f("lru_lambda")[j]))
        sk = f("sinks_b")[j]
        pc(f"sk{j}", np.ascontiguousarray(np.repeat(sk.reshape(6, 2), 64, axis=1).T))
    sh["cols"] = cols
    sh["dt5"] = build_dt5()
    return sh


def prep_core(inp, core):
    x = np.asarray(inp["x"], dtype=np.float32)
    mem = np.asarray(inp["mem"], dtype=np.float32)
    xs = x[2 * core:2 * core + 2]
    xT = np.ascontiguousarray(xs.reshape(2, SEQ, 8, 128).transpose(0, 3, 2, 1))
    ms = mem[2 * core:2 * core + 2]
    memT = np.ascontiguousarray(ms.reshape(2, 256, 8, 128).transpose(0, 3, 2, 1))
    return {"xT": xT, "memT": memT}


def build(n_layers=4, n_seq=2, dbg="full"):
    nc = bass.Bass("TRN2", target_bir_lowering=False)
    P = Prog()
    LU = [layer_units(l) for l in range(4)]

    def din(name, shape):
        return nc.dram_tensor(name, shape, F32, kind="ExternalInput").ap()
    xT = din("xT", [2, 128, 8, SEQ])
    memT = din("memT", [2, 128, 8, 256])
    wl = [din(f"w{l}", [128, LU[l][1]]) for l in range(4)]
    wm = din("wm", [128, 4 * 4096])
    wk = din("wk", [128, 4096 + 2048])
    colsd = din("cols", [128, NCOL])
    dt5d = din("dt5", [128, 1024])
    yT = nc.dram_tensor("yT", [2, 128, 8, SEQ], F32, kind="ExternalOutput").ap()

    with ExitStack() as es:
        def sb(name, shape, dt):
            return es.enter_context(nc.sbuf_tensor(name, shape, dt))
        XR = sb("XR", [128, 8 * ST], F32)
        RG = sb("RG", [128, 4 * 4096], BF16)
        PG = sb("PG", [128, 29 * 512], F32)
        PGb = PG.bitcast(BF16)
        HT = sb("HT", [128, 8 * ST], BF16)
        FB1 = HT.bitcast(F32)
        ET = sb("ET", [128, 4 * 1026], F32)
        SQ = ET.bitcast(BF16)
        FB0 = sb("FB0", [128, 8 * 512], F32)
        RS = sb("RS", [128, 2 * 512], F32)
        T1 = sb("T1", [128, 512], F32)
        KT = sb("KT", [128, 4 * 1152], BF16)
        VT = sb("VT", [128, 9 * 256], BF16)
        KM = sb("KM", [128, 4 * 2 * 256], BF16)
        VM = sb("VM", [128, 4 * 2 * 256], BF16)
        DT5 = sb("DT5", [128, 1024], F32)
        COLS = sb("COLS", [128, NCOL], F32)
        DER = sb("DER", [128, 64], F32)
        CST = sb("CST", [128, 4], F32)
        ONESM = sb("ONESM", [128, 128], BF16)
        ONES64 = sb("ONES64", [128, 64], BF16)
        TAILS = sb("TAILS", [128, 4 * 44 * 2], F32)
        UXT = sb("UXT", [128, 2 * 6 * 3], F32)
        HST = sb("HST", [128, 2 * 6], F32)
        PS = es.enter_context(nc.psum_tensor("PS", [128, 8 * 512], F32))

        XB = P.bufs(2, "xb")
        RB = P.bufs(4, "ring")
        PB = P.bufs(29, "pg")
        HB = [[P.buf(f"ht{kc}_{t}") for t in range(2)] for kc in range(8)]
        F1 = P.bufs(8, "fb1")
        for m in range(8):
            alias([F1[m]], HB[m])
        EB = P.bufs(4, "ext")
        SQB = P.bufs(8, "sq")
        alias(SQB, [EB[0], EB[1]])
        F0 = P.bufs(8, "fb0")
        AC = P.bufs(4, "acc")
        for i in range(4):
            alias([AC[i]], [F0[2 * i], F0[2 * i + 1]])
        RSB = P.bufs(2, "rs")
        T1B = P.buf("t1")
        KB = P.bufs(9, "kt")
        VB = P.bufs(9, "vt")
        KMB = P.bufs(4, "km")
        VMB = P.bufs(4, "vm")
        CB = P.buf("consts")
        DERB = P.buf("der")
        TLB = [[P.buf(f"tl{l}_{c}") for c in range(44)] for l in range(4)]
        UXB = [[P.buf(f"ux{j}_{c}") for c in range(6)] for j in range(2)]
        HSB = [[P.buf(f"hs{j}_{c}") for c in range(6)] for j in range(2)]
        BK = P.bufs(8, "bank")
        for b in BK:
            b.excl = True

        def bank(i, n=512, off=0):
            return PS[:, i * 512 + off:i * 512 + off + n]

        def pgf(i, a=0, b=512):
            return PG[:, i * 512 + a:i * 512 + b]

        def pgb(i, a=0, b=1024):
            return PGb[:, i * 1024 + a:i * 1024 + b]

        def xr(kc, tc):
            return XR[:, kc * ST + tc * TC:kc * ST + (tc + 1) * TC]

        def ht(kc, a, b):
            return HT[:, kc * ST + a:kc * ST + b]

        def sq(kc, n=512):
            return SQ[:, kc * 512:kc * 512 + n]

        def col(name, i=0):
            c = COLMAP[name] + i
            return COLS[:, c:c + 1]

        DERMAP = {}
        dn = 0
        for j in range(2):
            for n_ in ("nbr", "nbi", "cl", "cl2", "esk"):
                DERMAP[f"{n_}{j}"] = dn
                dn += 6

        def der(name, i=0):
            c = DERMAP[name] + i
            return DER[:, c:c + 1]

        def MM(out, lhsT, rhs, start, stop, rd, wr):
            P.op("pe", lambda e: e.matmul(out, lhsT, rhs, start=start, stop=stop), reads=rd, writes=wr)

        def ACTV(out, in_, func, rd, wr, bias=None, scale=1.0):
            if bias is None:
                P.op("act", lambda e: e.activation(out=out, in_=in_, func=func, scale=scale), reads=rd, writes=wr)
            else:
                P.op("act", lambda e: e.activation(out=out, in_=in_, func=func, bias=bias, scale=scale),
                     reads=rd, writes=wr)

        def TS(out, in0, s1, s2, op0, op1, rd, wr, eng="dve"):
            if s2 is None:
                P.op(eng, lambda e: e.tensor_scalar(out=out, in0=in0, scalar1=s1, scalar2=None, op0=op0),
                     reads=rd, writes=wr)
            else:
                P.op(eng, lambda e: e.tensor_scalar(out=out, in0=in0, scalar1=s1, scalar2=s2, op0=op0, op1=op1),
                     reads=rd, writes=wr)

        def STT(out, in0, scalar, in1, op0, op1, rd, wr, eng="dve"):
            P.op(eng, lambda e: e.scalar_tensor_tensor(out=out, in0=in0, scalar=scalar, in1=in1, op0=op0, op1=op1),
                 reads=rd, writes=wr)

        def TT(out, in0, in1, op, rd, wr, eng="dve"):
            P.op(eng, lambda e: e.tensor_tensor(out=out, in0=in0, in1=in1, op=op), reads=rd, writes=wr)

        def CP(out, in_, rd, wr, eng="dve"):
            P.op(eng, lambda e: e.tensor_copy(out=out, in_=in_), reads=rd, writes=wr)

        def MSET(ap, v, wr, eng="dve"):
            P.op(eng, lambda e: e.memset(ap, v), writes=wr)

        bstate = {"s": 0, "p": 0}

        def nb1():
            b = bstate["s"]
            bstate["s"] = (b + 1) % 6
            return b

        def nb2():
            b = bstate["p"]
            bstate["p"] = (b + 1) % 3
            return 2 * b

        rstate = {"n": 0}

        class Slot(int):
            pass
        slot_gen = [0, 0, 0, 0]

        def load_unit(src_ap, ln):
            s = Slot(rstate["n"] % 4)
            rstate["n"] += 1
            slot_gen[s] += 1
            s.gen = slot_gen[s]
            dst = RG[:, s * 4096:s * 4096 + ln]
            P.op("pool", lambda e: e.dma_start(out=dst, in_=src_ap), writes=[RB[s]], dma=f"ring{s}")
            return s

        def chk(s):
            assert s.gen == slot_gen[s], "weight ring slot was recycled before its last use"
            return s

        def lu(l, name):
            o, ln = LU[l][0][name]
            return load_unit(wl[l][:, o:o + ln], ln)

        def rg(s, kc, c0, n, width=512):
            chk(s)
            base = s * 4096 + kc * width + c0
            return RG[:, base:base + n]

        P.op("sp", lambda e: e.dma_start(out=COLS[:, :], in_=colsd), writes=[CB], dma="cols")
        P.op("sp", lambda e: e.dma_start(out=DT5[:, :], in_=dt5d), writes=[CB], dma="cols")
        MSET(CST[:, 0:1], EPS, [CB])
        MSET(CST[:, 1:2], 1.0, [CB])
        MSET(CST[:, 2:3], 0.0, [CB])
        MSET(ONESM[:, :], 1.0 / 1024.0, [CB])
        MSET(ONES64[:, :], 1.0, [CB])
        for j in range(2):
            TS(DER[:, DERMAP[f"nbr{j}"]:DERMAP[f"nbr{j}"] + 6], COLS[:, COLMAP[f"br{j}"]:COLMAP[f"br{j}"] + 6],
               -1.0, None, ALU.mult, None, [CB], [DERB])
            TS(DER[:, DERMAP[f"nbi{j}"]:DERMAP[f"nbi{j}"] + 6], COLS[:, COLMAP[f"bi{j}"]:COLMAP[f"bi{j}"] + 6],
               -1.0, None, ALU.mult, None, [CB], [DERB])
            cl = DER[:, DERMAP[f"cl{j}"]:DERMAP[f"cl{j}"] + 6]
            cl2 = DER[:, DERMAP[f"cl2{j}"]:DERMAP[f"cl2{j}"] + 6]
            ACTV(cl, COLS[:, COLMAP[f"lam{j}"]:COLMAP[f"lam{j}"] + 6], AF.Exp, [CB], [DERB], scale=-1.0)
            ACTV(cl, cl, AF.Ln, [DERB, CB], [DERB], bias=CST[:, 1:2])
            TS(cl2, cl, -16.0, None, ALU.mult, None, [DERB], [DERB])
            TS(cl, cl, -8.0, None, ALU.mult, None, [DERB], [DERB])
            ACTV(DER[:, DERMAP[f"esk{j}"]:DERMAP[f"esk{j}"] + 6], COLS[:, COLMAP[f"sk{j}"]:COLMAP[f"sk{j}"] + 6],
                 AF.Exp, [CB], [DERB])

        def rstd_from_bank(bk, rs_i, n=512):
            ACTV(T1[:, 0:n], bank(bk, n), AF.Ln, [BK[bk], CB], [T1B], bias=CST[:, 0:1])
            ACTV(RS[:, rs_i * 512:rs_i * 512 + n], T1[:, 0:n], AF.Exp, [T1B], [RSB[rs_i]], scale=-0.5)

        def prenorm(tc, gname):
            sbk = 6 + tc
            ACTV(SQ[:, 0:4096].rearrange("p (k n) -> p k n", k=8),
                 XR[:, :].rearrange("p (k n) -> p k n", k=8)[:, :, tc * TC:(tc + 1) * TC], AF.Square, [XB[tc]], list(SQB))
            for kc in range(8):
                MM(bank(sbk), ONESM[:, :], sq(kc), kc == 0, kc == 7, [SQB[kc], CB], [BK[sbk]])
            rstd_from_bank(sbk, tc)
            for kc in range(8):
                STT(ht(kc, tc * TC, (tc + 1) * TC), xr(kc, tc), col(gname, kc), RS[:, tc * 512:(tc + 1) * 512],
                    ALU.mult, ALU.mult, [XB[tc], RSB[tc], CB], [HB[kc][tc]])

        def post_evac(m, bk, tc, fbv, fbB, gname, sqi=None):
            sbk = 6 + tc
            sqi = m if sqi is None else sqi
            ACTV(sq(sqi), bank(bk), AF.Square, [BK[bk]], [SQB[sqi]])
            TS(fbv, bank(bk), col(gname, m), None, ALU.mult, None, [BK[bk], CB], [fbB])
            return lambda: MM(bank(sbk), ONESM[:, :], sq(sqi), m == 0, m == 7, [SQB[sqi], CB], [BK[sbk]])

        def post_finish(tc, fbview, fbBs):
            sbk = 6 + tc
            rstd_from_bank(sbk, tc)
            FBT = FB0 if fbBs is F0 else FB1
            fb3 = FBT[:, 0:4096].rearrange("p (m n) -> p m n", m=8)
            rs3 = RS[:, tc * 512:(tc + 1) * 512].unsqueeze(1).broadcast_to([128, 8, 512])
            x3 = XR[:, :].rearrange("p (k n) -> p k n", k=8)[:, :, tc * TC:(tc + 1) * TC]
            TT(fb3, fb3, rs3, ALU.mult, list(fbBs) + [RSB[tc]], list(fbBs))
            TT(x3, x3, fb3, ALU.add, list(fbBs) + [XB[tc]], [XB[tc]])

        def fb0v(m):
            return FB0[:, m * 512:(m + 1) * 512]

        def fb1v(m):
            return FB1[:, m * 512:(m + 1) * 512]

        def mem_attention(l, qpages, ypage_of, tmp_pages):
            tp = 0
            for hp in range(2):
                nbk = nb1()
                dbk = nb1()
                ptp = []
                for hh in range(2):
                    h = 2 * hp + hh
                    o = 64 * hh
                    sb2 = nb2()
                    for mb in range(2):
                        MM(bank(sb2 + mb), KM[o:o + 64, (l * 2 + hp) * 256 + mb * 128:(l * 2 + hp) * 256 + (mb + 1) * 128],
                           pgb(qpages[hp], 0, 512)[o:o + 64, :], True, True,
                           [KMB[l], PB[qpages[hp]]], [BK[sb2 + mb]])
                    pt = tmp_pages[tp % len(tmp_pages)]
                    tp += 1
                    ACTV(pgb(pt), PS[:, sb2 * 512:sb2 * 512 + 1024], AF.Exp, [BK[sb2], BK[sb2 + 1]], [PB[pt]],
                         scale=0.125)
                    ptp.append(pt)
                for hh in range(2):
                    h = 2 * hp + hh
                    o = 64 * hh
                    pt = ptp[hh]
                    for mb in range(2):
                        MM(PS[o:o + 64, nbk * 512:(nbk + 1) * 512],
                           VM[:, (l * 2 + mb) * 256 + h * 64:(l * 2 + mb) * 256 + (h + 1) * 64],
                           pgb(pt, mb * 512, (mb + 1) * 512), mb == 0, mb == 1, [VMB[l], PB[pt]], [BK[nbk]])
                    for mb in range(2):
                        MM(PS[o:o + 64, dbk * 512:(dbk + 1) * 512], ONES64[:, :],
                           pgb(pt, mb * 512, (mb + 1) * 512), mb == 0, mb == 1, [CB, PB[pt]], [BK[dbk]])
                yap, yb = ypage_of(hp)
                tq = tmp_pages[tp % len(tmp_pages)]
                tp += 1
                ACTV(pgf(tq), bank(dbk), AF.Ln, [BK[dbk]], [PB[tq]])
                ACTV(pgf(tq), pgf(tq), AF.Exp, [PB[tq]], [PB[tq]], scale=-1.0)
                TT(yap, bank(nbk), pgf(tq), ALU.mult, [BK[nbk], PB[tq]], [yb])

        def out_proj_and_post(l, ycat_ap, ycat_bufs, tc):
            s0 = lu(l, "O0")
            s1 = lu(l, "O1")
            pend = None
            for m in range(8):
                s = s0 if m < 4 else s1
                bk = nb1()
                for kc in range(8):
                    MM(bank(bk), rg(s, kc, (m % 4) * 128, 128), ycat_ap(kc), kc == 0, kc == 7,
                       [RB[s], ycat_bufs[kc]], [BK[bk]])
                if pend is not None:
                    pend()
                pend = post_evac(m, bk, tc, fb0v(m), F0[m], f"gmo{l}")
            pend()
            post_finish(tc, fb0v, F0)

        def mixer_p1(l, tc):
            if l == 2:
                kv_project(tc)
            prenorm(tc, f"gmp{l}")

        def mixer_a(l, tc):
            j = l
            sA0 = lu(l, "A0")
            sA1 = lu(l, "A1")
            GG = [0, 1, 2, 3, 4, 5]
            YC = [6, 7, 8, 9]
            QM = [10, 11]
            TMP = list(range(12, 22))
            tstate = {"n": 0}

            def tmp():
                t = TMP[tstate["n"] % len(TMP)]
                tstate["n"] += 1
                return t

            def ycat_ap(kc):
                return pgb(YC[kc // 2], (kc % 2) * 512, (kc % 2 + 1) * 512)
            ycat_bufs = [PB[YC[kc // 2]] for kc in range(8)]
            hts = (tc * TC, (tc + 1) * TC)
            for c in range(6):
                s, c0 = (sA0, c * 128) if c < 4 else (sA1, (c - 4) * 128)
                bk = nb1()
                for kc in range(8):
                    MM(bank(bk), rg(s, kc, c0, 128), ht(kc, *hts), kc == 0, kc == 7, [RB[s], HB[kc][tc]], [BK[bk]])
                ACTV(pgf(GG[c]), bank(bk), AF.Gelu_apprx_tanh, [BK[bk]], [PB[GG[c]]])
            sG = lu(l, "G")
            sA2 = lu(l, "A2")

            def chain(c, q):
                s, c0 = (sA1, 256 + c * 128) if c < 2 else (sA2, (c - 2) * 128)
                base = 10 + 5 * q
                ta, t2, t3, t4, t5 = range(base, base + 5)
                tb = 25 + q // 2
                tbo = (q % 2) * 512
                ei = q + 1
                E = ET[:, ei * 1026:ei * 1026 + 515]
                uo = (j * 6 + c) * 3
                acc = pgf(ta)
                hso = j * 6 + c
                st = {}

                def s0():
                    bk = nb1()
                    st["bk"] = bk
                    for kc in range(8):
                        MM(bank(bk), rg(s, kc, c0, 128), ht(kc, *hts), kc == 0, kc == 7, [RB[s], HB[kc][tc]], [BK[bk]])

                def s1():
                    bk = st["bk"]
                    CP(E[:, 0:3], UXT[:, uo:uo + 3], [UXB[j][c]], [EB[ei]])
                    ACTV(E[:, 3:515], bank(bk), AF.Identity, [BK[bk]], [EB[ei]])
                    CP(UXT[:, uo:uo + 3], E[:, 512:515], [EB[ei]], [UXB[j][c]])
                    ACTV(acc, E[:, 3:515], AF.Identity, [EB[ei], CB], [PB[ta]], bias=col(f"cb{j}", c),
                         scale=col(f"cw{j}", c * 4 + 3))

                def s2():
                    for k in range(3):
                        STT(acc, E[:, k:k + 512], col(f"cw{j}", c * 4 + k), acc, ALU.mult, ALU.add,
                            [EB[ei], PB[ta], CB], [PB[ta]])
                    CP(pgb(tb, tbo, tbo + 512), acc, [PB[ta]], [PB[tb]])

                def s3():
                    bkr = nb1()
                    MM(bank(bkr), RG[:, chk(sG) * 4096 + c * 128:sG * 4096 + (c + 1) * 128], pgb(tb, tbo, tbo + 512), True, True,
                       [RB[sG], PB[tb]], [BK[bkr]])
                    bki = nb1()
                    MM(bank(bki), RG[:, sG * 4096 + (6 + c) * 128:sG * 4096 + (7 + c) * 128], pgb(tb, tbo, tbo + 512),
                       True, True, [RB[sG], PB[tb]], [BK[bki]])
                    st["bkr"], st["bki"] = bkr, bki

                def s4():
                    bkr, bki = st["bkr"], st["bki"]
                    ACTV(pgf(t2), bank(bkr), AF.Exp, [BK[bkr], DERB], [PB[t2]], bias=der(f"nbr{j}", c), scale=-1.0)
                    ACTV(pgf(t3), bank(bki), AF.Exp, [BK[bki], DERB], [PB[t3]], bias=der(f"nbi{j}", c), scale=-1.0)

                def s5():
                    ACTV(pgf(t2), pgf(t2), AF.Ln, [PB[t2], CB], [PB[t2]], bias=CST[:, 1:2])
                    ACTV(pgf(t3), pgf(t3), AF.Ln, [PB[t3], CB], [PB[t3]], bias=CST[:, 1:2])

                def s6():
                    ACTV(pgf(t2), pgf(t2), AF.Exp, [PB[t2]], [PB[t2]], scale=-1.0)

                def s7():
                    ACTV(pgf(t4), pgf(t2), AF.Exp, [PB[t2], DERB], [PB[t4]], scale=der(f"cl{j}", c))
                    ACTV(pgf(t5), pgf(t2), AF.Exp, [PB[t2], DERB], [PB[t5]], scale=der(f"cl2{j}", c))

                def s8():
                    TS(pgf(t5), pgf(t5), 0.99999994, None, ALU.min, None, [PB[t5]], [PB[t5]])

                def s9():
                    ACTV(pgf(t5), pgf(t5), AF.Ln, [PB[t5], CB], [PB[t5]], bias=CST[:, 1:2], scale=-1.0)

                def s10():
                    STT(pgf(t5), pgf(t5), 0.5, pgf(t3), ALU.mult, ALU.subtract, [PB[t5], PB[t3]], [PB[t5]])

                def s11():
                    ACTV(pgf(t5), pgf(t5), AF.Exp, [PB[t5]], [PB[t5]])

                def s12():
                    TT(pgf(t5), pgf(t5), acc, ALU.mult, [PB[t5], PB[ta]], [PB[t5]])
                    P.op("dve", lambda e, o_=pgf(t3), a_=pgf(t4), b_=pgf(t5), i_=HST[:, hso:hso + 1]:
                         e.tensor_tensor_scan(out=o_, data0=a_, data1=b_, initial=i_, op0=ALU.mult, op1=ALU.add),
                         reads=[PB[t4], PB[t5], HSB[j][c]], writes=[PB[t3]])
                    CP(HST[:, hso:hso + 1], pgf(t3, 511, 512), [PB[t3]], [HSB[j][c]])
                    TT(ycat_ap(c), pgf(t3), pgf(GG[c]), ALU.mult, [PB[t3], PB[GG[c]]], [ycat_bufs[c]])

                return [s0, s1, s2, s3, s4, s5, s6, s7, s8, s9, s10, s11, s12]

            sA3 = lu(l, "A3")
            for grp in range(2):
                ch = [chain(3 * grp + q, q) for q in range(3)]
                for k in range(len(ch[0])):
                    for q in range(3):
                        ch[q][k]()
            for hp in range(2):
                bk = nb1()
                for kc in range(8):
                    MM(bank(bk), rg(sA3, kc, hp * 128, 128, width=256), ht(kc, *hts), kc == 0, kc == 7,
                       [RB[sA3], HB[kc][tc]], [BK[bk]])
                ACTV(pgb(QM[hp], 0, 512), bank(bk), AF.Identity, [BK[bk]], [PB[QM[hp]]])
            mem_attention(l, QM, lambda hp: (ycat_ap(6 + hp), ycat_bufs[6 + hp]), TMP)
            return lambda: out_proj_and_post(l, ycat_ap, ycat_bufs, tc)

        def kv_project(tc):
            prenorm(tc, "gkv")
            hts = (tc * TC, (tc + 1) * TC)
            s0 = load_unit(wk[:, 0:4096], 4096)
            s1 = load_unit(wk[:, 4096:6144], 2048)
            for k in range(4):
                bk = nb1()
                for kc in range(8):
                    MM(bank(bk), rg(s0, kc, k * 128, 128), ht(kc, *hts), kc == 0, kc == 7, [RB[s0], HB[kc][tc]], [BK[bk]])
                ACTV(KT[:, k * 1152 + 128 + tc * 512:k * 1152 + 128 + (tc + 1) * 512], bank(bk), AF.Identity, [BK[bk]],
                     [KB[1 + 4 * tc + i] for i in range(4)])
            for tb in range(4):
                bk = nb1()
                for kc in range(8):
                    MM(bank(bk, 256), ht(kc, tc * TC + tb * 128, tc * TC + (tb + 1) * 128), rg(s1, kc, 0, 256, width=256),
                       kc == 0, kc == 7, [RB[s1], HB[kc][tc]], [BK[bk]])
                idx = 1 + 4 * tc + tb
                ACTV(VT[:, idx * 256:(idx + 1) * 256], bank(bk, 256), AF.Identity, [BK[bk]], [VB[idx]])

        def mixer_b(l, tc, first):
            j = l - 2
            hts = (tc * TC, (tc + 1) * TC)
            sB0 = lu(l, "B0")
            sB1 = lu(l, "B1")
            QP = list(range(0, 8))
            YC = [8, 9, 10, 11]
            TMP = list(range(16, 22))
            tstate = {"n": 0}
            sstate = {"n": 0}

            def tmp():
                t = TMP[tstate["n"] % len(TMP)]
                tstate["n"] += 1
                return t

            def ycat_ap(kc):
                return pgb(YC[kc // 2], (kc % 2) * 512, (kc % 2 + 1) * 512)
            ycat_bufs = [PB[YC[kc // 2]] for kc in range(8)]
            for cq in range(8):
                s, c0 = (sB0, cq * 128) if cq < 4 else (sB1, (cq - 4) * 128)
                bk = nb1()
                for kc in range(8):
                    MM(bank(bk), rg(s, kc, c0, 128), ht(kc, *hts), kc == 0, kc == 7, [RB[s], HB[kc][tc]], [BK[bk]])
                ACTV(pgb(QP[cq], 0, 512), bank(bk), AF.Identity, [BK[bk]], [PB[QP[cq]]])
            noprev = first and tc == 0
            ncol = 896 if noprev else 1024

            def stA(cq):
                pts = []
                for hh in range(2):
                    h = 2 * cq + hh
                    o = 64 * hh
                    k = h // 3
                    sb2 = 2 * hh
                    q = pgb(QP[cq], 0, 512)
                    for i in range(5):
                        if i == 0 and noprev:
                            continue
                        sidx = 4 * tc + i
                        kap = KT[o:o + 64, k * 1152 + sidx * 128:k * 1152 + (sidx + 1) * 128]
                        if i == 0:
                            c0, qa, qb = 896, 0, 128
                        elif i == 4:
                            c0, qa, qb = 768, 384, 512
                        else:
                            c0, qa, qb = (i - 1) * 256, (i - 1) * 128, (i + 1) * 128
                        bki = sb2 + c0 // 512
                        MM(PS[:, sb2 * 512 + c0:sb2 * 512 + c0 + (qb - qa)], kap, q[o:o + 64, qa:qb], True, True,
                           [KB[sidx], PB[QP[cq]]], [BK[bki]])
                    sp = 12 + 2 * (sstate["n"] % 2)
                    sstate["n"] += 1
                    spv = PG[:, sp * 512:sp * 512 + ncol]
                    STT(spv, DT5[:, 0:ncol], -SLOPES[h], PS[:, sb2 * 512:sb2 * 512 + ncol], ALU.mult, ALU.add,
                        [CB, BK[sb2], BK[sb2 + 1]], [PB[sp], PB[sp + 1]])
                    pt = tmp()
                    ACTV(pgb(pt, 0, ncol), spv, AF.Exp, [PB[sp], PB[sp + 1]], [PB[pt]], scale=0.125)
                    pts.append(pt)
                return pts

            def stB(cq, pts):
                nbk, dbk = (4, 5) if cq % 2 == 0 else (6, 7)
                for hh in range(2):
                    h = 2 * cq + hh
                    o = 64 * hh
                    k = h // 3
                    pt = pts[hh]
                    for qb_ in range(4):
                        srcs = []
                        if not (noprev and qb_ == 0):
                            pc0 = 896 if qb_ == 0 else (qb_ - 1) * 256 + 128
                            srcs.append((4 * tc + qb_, pc0))
                        cc0 = 768 if qb_ == 3 else qb_ * 256
                        srcs.append((4 * tc + qb_ + 1, cc0))
                        for which in ("n", "d"):
                            bkx = nbk if which == "n" else dbk
                            for si, (sidx, c0) in enumerate(srcs):
                                if which == "n":
                                    lhs = VT[:, sidx * 256 + k * 64:sidx * 256 + (k + 1) * 64]
                                    rd = [VB[sidx], PB[pt]]
                                else:
                                    lhs = ONES64[:, :]
                                    rd = [CB, PB[pt]]
                                MM(PS[o:o + 64, bkx * 512 + qb_ * 128:bkx * 512 + (qb_ + 1) * 128], lhs,
                                   pgb(pt, c0, c0 + 128), si == 0, si == len(srcs) - 1, rd, [BK[bkx]])

            def stC(cq):
                nbk, dbk = (4, 5) if cq % 2 == 0 else (6, 7)
                tq = tmp()
                ACTV(pgf(tq), bank(dbk), AF.Ln, [BK[dbk], DERB], [PB[tq]], bias=der(f"esk{j}", cq))
                ACTV(pgf(tq), pgf(tq), AF.Exp, [PB[tq]], [PB[tq]], scale=-1.0)
                TT(ycat_ap(cq), bank(nbk), pgf(tq), ALU.mult, [BK[nbk], PB[tq]], [ycat_bufs[cq]])

            nxt = stA(0)
            for cq in range(6):
                cur = nxt
                if cq + 1 < 6:
                    nxt = stA(cq + 1)
                stB(cq, cur)
                stC(cq)
            mem_attention(l, [QP[6], QP[7]], lambda hp: (ycat_ap(6 + hp), ycat_bufs[6 + hp]), TMP)
            return lambda: out_proj_and_post(l, ycat_ap, ycat_bufs, tc)

        def ffn(l, next_p1=None):
            units = {}

            def stage1(jn):
                j2, jj = divmod(jn, 2)
                if jj == 0:
                    units[j2] = lu(l, f"U{j2}")
                s = units[j2]
                st_ = jn % 2
                gb, vb = (0, 2) if st_ == 0 else (4, 6)
                order = [(b0, c0, tc) for b0, c0 in ((gb, jj * 128), (vb, 256 + jj * 128)) for tc in range(2)]
                if jn == 0:
                    order.sort(key=lambda t: t[2])
                for b0, c0, tc in order:
                    for kc in range(8):
                        MM(bank(b0 + tc), rg(s, kc, c0, 128), ht(kc, tc * TC, (tc + 1) * TC), kc == 0, kc == 7,
                           [RB[s], HB[kc][tc]], [BK[b0 + tc]])

            def halves(jn):
                st_ = jn % 2
                gb, vb = (0, 2) if st_ == 0 else (4, 6)
                out = []
                for hi, (b0, ch) in enumerate(((gb, jn), (vb, 22 + jn))):
                    ei = 2 * st_ + hi
                    out.append((b0, ch, ei, ET[:, ei * 1026:(ei + 1) * 1026], (l * 44 + ch) * 2,
                                FB0[:, ei * 1024:(ei + 1) * 1024]))
                return out

            def stage2a(jn):
                for b0, ch, ei, E, to, acc in halves(jn):
                    CP(E[:, 0:2], TAILS[:, to:to + 2], [TLB[l][ch]], [EB[ei]])

            def stage2b(jn):
                for b0, ch, ei, E, to, acc in halves(jn):
                    ACTV(E[:, 2:1026], PS[:, b0 * 512:b0 * 512 + 1024], AF.Identity, [BK[b0], BK[b0 + 1]], [EB[ei]])
                for b0, ch, ei, E, to, acc in halves(jn):
                    ACTV(acc, E[:, 2:1026], AF.Identity, [EB[ei], CB], [AC[ei]], bias=col(f"fcb{l}", ch),
                         scale=col(f"fcw{l}", ch * 3 + 2))

            def stage2c(jn):
                accs = []
                for b0, ch, ei, E, to, acc in halves(jn):
                    CP(TAILS[:, to:to + 2], E[:, 1024:1026], [EB[ei]], [TLB[l][ch]])
                for b0, ch, ei, E, to, acc in halves(jn):
                    for k in range(2):
                        STT(acc, E[:, k:k + 1024], col(f"fcw{l}", ch * 3 + k), acc, ALU.mult, ALU.add,
                            [EB[ei], AC[ei], CB], [AC[ei]])
                    accs.append((acc, AC[ei]))
                return accs

            def stage3(jn, accs):
                ACTV(accs[0][0], accs[0][0], AF.Gelu_apprx_tanh, [accs[0][1]], [accs[0][1]])
                TT(pgb(jn), accs[1][0], accs[0][0], ALU.mult, [accs[0][1], accs[1][1]], [PB[jn]])

            prev = None
            stage2a(0)
            for jn in range(NJ):
                stage1(jn)
                stage2b(jn)
                if jn + 1 < NJ:
                    stage2a(jn + 1)
                accs = stage2c(jn)
                if prev is not None:
                    stage3(*prev)
                prev = (jn, accs)
            stage3(*prev)
            pend = None

            def down_mms(s, bk, tc, j0, j1):
                for jn in range(j0, j1):
                    MM(bank(bk), RG[:, chk(s) * 4096 + jn * 128:s * 4096 + (jn + 1) * 128],
                       pgb(jn, tc * 512, (tc + 1) * 512), jn == 0, jn == NJ - 1, [RB[s], PB[jn]], [BK[bk]])

            NW = 3
            wave = {}
            for m in range(NW):
                s = lu(l, f"D{m}")
                for tc in range(2):
                    bk = nb1()
                    wave[(m, tc)] = (s, bk)
                    down_mms(s, bk, tc, 0, NJ - 2)
            for m in range(8):
                if m >= NW:
                    s = lu(l, f"D{m}")
                for tc in range(2):
                    if m < NW:
                        s, bk = wave[(m, tc)]
                        down_mms(s, bk, tc, NJ - 2, NJ)
                    else:
                        bk = nb1()
                        down_mms(s, bk, tc, 0, NJ)
                    if pend is not None:
                        pend()
                    sqi = (2 * m + tc) % 8
                    if tc == 0:
                        pend = post_evac(m, bk, 0, fb1v(m), F1[m], f"gfo{l}", sqi)
                    else:
                        pend = post_evac(m, bk, 1, fb0v(m), F0[m], f"gfo{l}", sqi)
            pend()
            post_finish(0, fb1v, F1)
            if next_p1 is not None:
                next_p1()
            post_finish(1, fb0v, F0)

        def seq_prologue(s):
            for l in range(4):
                for ch in range(44):
                    pass
            P.op("dve", lambda e: e.memset(TAILS[:, :], 0.0), writes=[b for l in range(4) for b in TLB[l]])
            P.op("dve", lambda e: e.memset(UXT[:, :], 0.0), writes=[b for j in range(2) for b in UXB[j]])
            P.op("dve", lambda e: e.memset(HST[:, :], 0.0), writes=[b for j in range(2) for b in HSB[j]])
            MT = PG[:, 0:2048]
            P.op("sp", lambda e: e.dma_start(out=MT.rearrange("p (k n) -> p k n", k=8), in_=memT[s]),
                 writes=[PB[0], PB[1], PB[2], PB[3]], dma="mem")
            for kc in range(8):
                ACTV(sq(kc, 256), PG[:, kc * 256:(kc + 1) * 256], AF.Square, [PB[kc // 2]], [SQB[kc]])
            for kc in range(8):
                MM(bank(6, 256), ONESM[:, :], sq(kc, 256), kc == 0, kc == 7, [SQB[kc], CB], [BK[6]])
            rstd_from_bank(6, 0, 256)
            for l in range(n_layers):
                su = load_unit(wm[:, l * 4096:(l + 1) * 4096], 4096)
                for kc in range(8):
                    STT(PGb[:, 4 * 1024 + kc * 256:4 * 1024 + (kc + 1) * 256], PG[:, kc * 256:(kc + 1) * 256],
                        col(f"gme{l}", kc), RS[:, 0:256], ALU.mult, ALU.mult, [PB[kc // 2], RSB[0], CB],
                        [PB[4 + kc // 4]])

                def hm(kc, a=0, b=256):
                    return PGb[:, 4 * 1024 + kc * 256 + a:4 * 1024 + kc * 256 + b]
                for hp in range(2):
                    bk = nb1()
                    for kc in range(8):
                        MM(bank(bk, 256), rg(su, kc, hp * 128, 128), hm(kc), kc == 0, kc == 7,
                           [RB[su], PB[4 + kc // 4]], [BK[bk]])
                    ACTV(KM[:, (l * 2 + hp) * 256:(l * 2 + hp + 1) * 256], bank(bk, 256), AF.Identity, [BK[bk]], [KMB[l]])
                for mb in range(2):
                    bk = nb1()
                    for kc in range(8):
                        MM(bank(bk, 256), hm(kc, mb * 128, (mb + 1) * 128), rg(su, kc, 256, 256), kc == 0, kc == 7,
                           [RB[su], PB[4 + kc // 4]], [BK[bk]])
                    ACTV(VM[:, (l * 2 + mb) * 256:(l * 2 + mb + 1) * 256], bank(bk, 256), AF.Identity, [BK[bk]], [VMB[l]])

        XR3 = XR[:, :].rearrange("p (k n) -> p k n", k=8)
        out_evs = []
        for s in range(n_seq):
            seq_prologue(s)
            for st in range(NST):
                P.op("sp", lambda e, s=s, st=st: e.dma_start(out=XR3, in_=xT[s, :, :, st * ST:(st + 1) * ST]),
                     writes=[XB[0], XB[1]], dma="xin")
                for l in range(n_layers):
                    mix = (lambda tc, l=l: mixer_a(l, tc)) if l < 2 else (lambda tc, l=l, st=st: mixer_b(l, tc, st == 0))
                    if l == 0:
                        mixer_p1(l, 0)
                    fin0 = mix(0)
                    mixer_p1(l, 1)
                    fin0()
                    fin1 = mix(1)
                    prenorm(0, f"gfp{l}")
                    fin1()
                    prenorm(1, f"gfp{l}")
                    ffn(l, (lambda l=l: mixer_p1(l + 1, 0)) if l + 1 < n_layers else None)
                if n_layers > 2 and st < NST - 1:
                    for k in range(4):
                        CP(KT[:, k * 1152:k * 1152 + 128], KT[:, k * 1152 + 1024:k * 1152 + 1152], [KB[8]], [KB[0]])
                    CP(VT[:, 0:256], VT[:, 8 * 256:9 * 256], [VB[8]], [VB[0]])
                o = P.op("sp", lambda e, s=s, st=st: e.dma_start(out=yT[s, :, :, st * ST:(st + 1) * ST], in_=XR3),
                         reads=[XB[0], XB[1]], dma="yout")
                out_evs.append(o.ev)
        P.emit(nc, final_waits=[out_evs[-1]])
    return nc


_CACHE = {}


def kernel(_n_layers=4, _cores=NCORES, **inputs):
    key = (_n_layers,)
    if key not in _CACHE:
        _CACHE[key] = build(_n_layers)
    nc = _CACHE[key]
    sh = prep_shared(inputs)
    in_maps = []
    for c in range(_cores):
        m = dict(sh)
        m.update(prep_core(inputs, c))
        in_maps.append(m)
    res = run_bass_kernel_spmd(nc, in_maps, core_ids=list(range(_cores)))
    outs = []
    for c in range(_cores):
        yT = np.asarray(res.results[c]["yT"])
        outs.append(np.ascontiguousarray(yT.transpose(0, 3, 2, 1)).reshape(2, SEQ, D))
    return np.concatenate(outs, axis=0).astype(np.float32)
```

```python
import math
from contextlib import ExitStack
import numpy as np
import concourse.bass as bass
import concourse.mybir as mybir
from concourse.bass_utils import run_bass_kernel_spmd

F32 = mybir.dt.float32
BF16 = mybir.dt.bfloat16
AF = mybir.ActivationFunctionType
ALU = mybir.AluOpType

NCORES = 8
SEQ = 2048
ST = 1024
TC = 512
NST = SEQ // ST
D = 1024
DFF = 2816
NJ = DFF // 128
BIG = 1.0e6
EPS = 1e-6

ENGS = ("pe", "act", "dve", "pool", "sp")


class Ev:
    __slots__ = ("kind", "eng", "idx", "key", "value", "needed")

    def __init__(self, kind, eng=None, idx=0, key=None, value=0):
        self.kind = kind
        self.eng = eng
        self.idx = idx
        self.key = key
        self.value = value
        self.needed = False


class Buf:
    __slots__ = ("name", "lw", "rd", "alias", "excl")

    def __init__(self, name):
        self.name = name
        self.lw = None
        self.rd = []
        self.alias = []
        self.excl = False


class Op:
    __slots__ = ("eng", "fn", "waits", "ev", "dma_key")

    def __init__(self, eng, fn):
        self.eng = eng
        self.fn = fn
        self.waits = []
        self.ev = None
        self.dma_key = None


def alias(a_list, b_list):
    for a in a_list:
        for b in b_list:
            if b not in a.alias:
                a.alias.append(b)
            if a not in b.alias:
                b.alias.append(a)


class Prog:
    def __init__(self):
        self.ops = {e: [] for e in ENGS}
        self.seen = {e: {} for e in ENGS}
        self.dma_cnt = {}
        self.nbuf = 0

    def buf(self, name=None):
        self.nbuf += 1
        return Buf(name or f"b{self.nbuf}")

    def bufs(self, n, name="b"):
        return [self.buf(f"{name}{i}") for i in range(n)]

    def _need(self, op, ev, raw):
        if ev is None:
            return
        e = op.eng
        if ev.kind == "eng":
            if ev.eng == e and e == "pe":
                return
            key = ev.eng
            pos = ev.idx
        else:
            key = ev.key
            pos = ev.value
        if self.seen[e].get(key, -1) >= pos:
            return
        self.seen[e][key] = pos
        ev.needed = True
        op.waits.append(ev)

    def op(self, eng, fn, reads=(), writes=(), dma=None):
        o = Op(eng, fn)
        lst = self.ops[eng]
        wr = []
        for b in writes:
            wr.append(b)
            wr.extend(b.alias)
        if eng != "pe":
            for b in reads:
                if b.excl and b not in wr:
                    wr.append(b)
        for b in reads:
            self._need(o, b.lw, True)
            for a in b.alias:
                self._need(o, a.lw, True)
        for b in wr:
            self._need(o, b.lw, False)
            for r in b.rd:
                self._need(o, r, False)
        if dma is not None:
            c = self.dma_cnt.get(dma, 0) + 16
            self.dma_cnt[dma] = c
            o.ev = Ev("dma", key=dma, value=c)
            o.dma_key = dma
        else:
            o.ev = Ev("eng", eng=eng, idx=len(lst))
        best = {}
        for w in o.waits:
            k = w.eng if w.kind == "eng" else w.key
            p = w.idx if w.kind == "eng" else w.value
            if k not in best or p > best[k][0]:
                best[k] = (p, w)
        o.waits = [v[1] for v in best.values()]
        for b in wr:
            b.lw = o.ev
            b.rd = []
        for b in reads:
            b.rd.append(o.ev)
        lst.append(o)
        return o

    def emit(self, nc, final_waits=()):
        for e in ENGS:
            c = 0
            for o in self.ops[e]:
                if o.ev.kind == "eng" and o.ev.needed:
                    c += 1
                    o.ev.value = c
        with ExitStack() as st:
            esem = {e: st.enter_context(nc.semaphore(f"s_{e}")) for e in ENGS}
            dsem = {k: st.enter_context(nc.semaphore(f"d_{k}")) for k in self.dma_cnt}
            block = st.enter_context(nc.Block())

            def semof(ev):
                return esem[ev.eng] if ev.kind == "eng" else dsem[ev.key]

            def run(e, eng):
                for o in self.ops[e]:
                    for w in o.waits:
                        eng.wait_ge(semof(w), w.value)
                    ins = o.fn(eng)
                    if o.dma_key is not None:
                        ins.then_inc(dsem[o.dma_key], 16)
                    elif o.ev.needed:
                        ins.then_inc(esem[e], 1)

            @block.tensor
            def _(eng):
                run("pe", eng)

            @block.scalar
            def _(eng):
                run("act", eng)

            @block.vector
            def _(eng):
                run("dve", eng)

            @block.gpsimd
            def _(eng):
                run("pool", eng)

            @block.sync
            def _(eng):
                run("sp", eng)
                for ev in final_waits:
                    eng.wait_ge(semof(ev), ev.value)


def alibi_slopes(n):
    def pow2_slopes(m):
        start = 2.0 ** (-8.0 / m)
        return [start ** (i + 1) for i in range(m)]
    c = 2 ** int(math.floor(math.log2(n)))
    s = pow2_slopes(c)
    if c != n:
        s = s + pow2_slopes(2 * c)[0::2][: n - c]
    return [float(np.float32(v)) for v in s]


SLOPES = alibi_slopes(12)


def kmajor(W):
    K, N = W.shape
    nk = K // 128
    return np.ascontiguousarray(W.reshape(nk, 128, N).transpose(1, 0, 2)).reshape(128, nk * N)


def colvec(v):
    n = v.shape[0] // 128
    return np.ascontiguousarray(v.reshape(n, 128).T)


def layer_units(l):
    units = {}
    off = 0

    def add(name, ln):
        nonlocal off
        units[name] = (off, ln)
        off += ln
    if l < 2:
        add("A0", 4096)
        add("A1", 4096)
        add("G", 1536)
        add("A2", 4096)
        add("A3", 2048)
    else:
        add("B0", 4096)
        add("B1", 4096)
    add("O0", 4096)
    add("O1", 4096)
    for j2 in range(11):
        add(f"U{j2}", 4096)
    for m in range(8):
        add(f"D{m}", 2816)
    return units, off


COLMAP = {}
_ncol = 0


def _addcol(name, w):
    global _ncol
    COLMAP[name] = _ncol
    _ncol += w


for _l in range(4):
    for _n in ("gmp", "gmo", "gfp", "gfo", "gme"):
        _addcol(f"{_n}{_l}", 8)
    _addcol(f"fcw{_l}", 44 * 3)
    _addcol(f"fcb{_l}", 44)
_addcol("gkv", 8)
for _j in range(2):
    _addcol(f"cw{_j}", 24)
    for _n in ("cb", "br", "bi", "lam"):
        _addcol(f"{_n}{_j}", 6)
    _addcol(f"sk{_j}", 6)
NCOL = _ncol


def build_dt5():
    s = np.arange(128)[:, None].astype(np.float64)
    t = np.arange(128)[None, :].astype(np.float64)
    cur = np.where(t >= s, 8.0 * (t - s), BIG)
    prev = np.where(s > t, 8.0 * (t + 128 - s), BIG)
    dt = np.concatenate([cur, prev], axis=1)
    dt5 = np.concatenate([dt, dt, dt, cur, prev], axis=1)
    return np.ascontiguousarray(dt5.astype(np.float32))


def prep_shared(inp):
    f = lambda k: np.asarray(inp[k], dtype=np.float32)
    sh = {}
    w_in_a, w_in_b = f("w_in_a"), f("w_in_b")
    w_mix_out, w_up, w_down = f("w_mix_out"), f("w_ffn_up"), f("w_ffn_down")
    w_rg_r, w_rg_i = f("w_rg_r"), f("w_rg_i")
    for l in range(4):
        units, tot = layer_units(l)
        arr = np.zeros((128, tot), np.float32)

        def put(name, a):
            o, ln = units[name]
            assert a.shape == (128, ln), (name, a.shape, ln)
            arr[:, o:o + ln] = a
        if l < 2:
            W = w_in_a[l]
            put("A0", kmajor(W[:, 0:512]))
            put("A1", kmajor(np.concatenate([W[:, 512:768], W[:, 768:1024]], axis=1)))
            put("A2", kmajor(W[:, 1024:1536]))
            put("A3", kmajor(W[:, 1536:1792]))
            g = np.zeros((128, 12, 128), np.float32)
            for gi, wg in enumerate((w_rg_r[l], w_rg_i[l])):
                for c in range(6):
                    g[0:64, gi * 6 + c, 0:64] = wg[2 * c]
                    g[64:128, gi * 6 + c, 64:128] = wg[2 * c + 1]
            put("G", g.reshape(128, 1536))
        else:
            W = w_in_b[l - 2]
            put("B0", kmajor(W[:, 0:512]))
            put("B1", kmajor(W[:, 512:1024]))
        put("O0", kmajor(w_mix_out[l][:, 0:512]))
        put("O1", kmajor(w_mix_out[l][:, 512:1024]))
        for j2 in range(11):
            put(f"U{j2}", kmajor(np.concatenate([w_up[l][:, j2 * 256:(j2 + 1) * 256],
                                                 w_up[l][:, DFF + j2 * 256:DFF + (j2 + 1) * 256]], axis=1)))
        for m in range(8):
            put(f"D{m}", kmajor(w_down[l][:, m * 128:(m + 1) * 128]))
        sh[f"w{l}"] = arr
    sh["wm"] = np.concatenate([kmajor(f("w_mem_kv")[l]) for l in range(4)], axis=1)
    wkv = f("w_kv")
    kd = np.concatenate([wkv[:, (k // 2) * 64:(k // 2) * 64 + 64] for k in range(8)], axis=1)
    sh["wk"] = np.concatenate([kmajor(kd), kmajor(wkv[:, 256:512])], axis=1)
    cols = np.zeros((128, NCOL), np.float32)

    def pc(name, a):
        cols[:, COLMAP[name]:COLMAP[name] + a.shape[1]] = a
    for l in range(4):
        pc(f"gmp{l}", colvec(f("g_mix_pre")[l]))
        pc(f"gmo{l}", colvec(f("g_mix_post")[l]))
        pc(f"gfp{l}", colvec(f("g_ffn_pre")[l]))
        pc(f"gfo{l}", colvec(f("g_ffn_post")[l]))
        pc(f"gme{l}", colvec(f("g_mem")[l]))
        wc = f("w_ffn_conv")[l]
        pc(f"fcw{l}", np.ascontiguousarray(wc.reshape(3, 44, 128).transpose(2, 1, 0)).reshape(128, 132))
        pc(f"fcb{l}", colvec(f("b_ffn_conv")[l]))
    pc("gkv", colvec(f("g_kv")))
    for j in range(2):
        wc = f("w_conv_a")[j]
        pc(f"cw{j}", np.ascontiguousarray(wc.reshape(4, 6, 128).transpose(2, 1, 0)).reshape(128, 24))
        pc(f"cb{j}", colvec(f("b_conv_a")[j]))
        pc(f"br{j}", colvec(f("b_rg_r")[j].reshape(768)))
        pc(f"bi{j}", colvec(f("b_rg_i")[j].reshape(768)))
        pc(f"lam{j}", colvec(f("lru_lambda")[j]))
        sk = f("sinks_b")[j]
        pc(f"sk{j}", np.ascontiguousarray(np.repeat(sk.reshape(6, 2), 64, axis=1).T))
    sh["cols"] = cols
    sh["dt5"] = build_dt5()
    return sh


def prep_core(inp, core):
    x = np.asarray(inp["x"], dtype=np.float32)
    mem = np.asarray(inp["mem"], dtype=np.float32)
    xs = x[2 * core:2 * core + 2]
    xT = np.ascontiguousarray(xs.reshape(2, SEQ, 8, 128).transpose(0, 3, 2, 1))
    ms = mem[2 * core:2 * core + 2]
    memT = np.ascontiguousarray(ms.reshape(2, 256, 8, 128).transpose(0, 3, 2, 1))
    return {"xT": xT, "memT": memT}


def build(n_layers=4, n_seq=2, dbg="full"):
    nc = bass.Bass("TRN2", target_bir_lowering=False)
    P = Prog()
    LU = [layer_units(l) for l in range(4)]

    def din(name, shape):
        return nc.dram_tensor(name, shape, F32, kind="ExternalInput").ap()
    xT = din("xT", [2, 128, 8, SEQ])
    memT = din("memT", [2, 128, 8, 256])
    wl = [din(f"w{l}", [128, LU[l][1]]) for l in range(4)]
    wm = din("wm", [128, 4 * 4096])
    wk = din("wk", [128, 4096 + 2048])
    colsd = din("cols", [128, NCOL])
    dt5d = din("dt5", [128, 1024])
    yT = nc.dram_tensor("yT", [2, 128, 8, SEQ], F32, kind="ExternalOutput").ap()

    with ExitStack() as es:
        def sb(name, shape, dt):
            return es.enter_context(nc.sbuf_tensor(name, shape, dt))
        XR = sb("XR", [128, 8 * ST], F32)
        RG = sb("RG", [128, 4 * 4096], BF16)
        PG = sb("PG", [128, 29 * 512], F32)
        PGb = PG.bitcast(BF16)
        HT = sb("HT", [128, 8 * ST], BF16)
        FB1 = HT.bitcast(F32)
        ET = sb("ET", [128, 4 * 1026], F32)
        SQ = ET.bitcast(BF16)
        FB0 = sb("FB0", [128, 8 * 512], F32)
        RS = sb("RS", [128, 2 * 512], F32)
        T1 = sb("T1", [128, 512], F32)
        KT = sb("KT", [128, 4 * 1152], BF16)
        VT = sb("VT", [128, 9 * 256], BF16)
        KM = sb("KM", [128, 4 * 2 * 256], BF16)
        VM = sb("VM", [128, 4 * 2 * 256], BF16)
        DT5 = sb("DT5", [128, 1024], F32)
        COLS = sb("COLS", [128, NCOL], F32)
        DER = sb("DER", [128, 64], F32)
        CST = sb("CST", [128, 4], F32)
        ONESM = sb("ONESM", [128, 128], BF16)
        ONES64 = sb("ONES64", [128, 64], BF16)
        TAILS = sb("TAILS", [128, 4 * 44 * 2], F32)
        UXT = sb("UXT", [128, 2 * 6 * 3], F32)
        HST = sb("HST", [128, 2 * 6], F32)
        PS = es.enter_context(nc.psum_tensor("PS", [128, 8 * 512], F32))

        XB = P.bufs(2, "xb")
        RB = P.bufs(4, "ring")
        PB = P.bufs(29, "pg")
        HB = [[P.buf(f"ht{kc}_{t}") for t in range(2)] for kc in range(8)]
        F1 = P.bufs(8, "fb1")
        for m in range(8):
            alias([F1[m]], HB[m])
        EB = P.bufs(4, "ext")
        SQB = P.bufs(8, "sq")
        alias(SQB, [EB[0], EB[1]])
        F0 = P.bufs(8, "fb0")
        AC = P.bufs(4, "acc")
        for i in range(4):
            alias([AC[i]], [F0[2 * i], F0[2 * i + 1]])
        RSB = P.bufs(2, "rs")
        T1B = P.buf("t1")
        KB = P.bufs(9, "kt")
        VB = P.bufs(9, "vt")
        KMB = P.bufs(4, "km")
        VMB = P.bufs(4, "vm")
        CB = P.buf("consts")
        DERB = P.buf("der")
        TLB = [[P.buf(f"tl{l}_{c}") for c in range(44)] for l in range(4)]
        UXB = [[P.buf(f"ux{j}_{c}") for c in range(6)] for j in range(2)]
        HSB = [[P.buf(f"hs{j}_{c}") for c in range(6)] for j in range(2)]
        BK = P.bufs(8, "bank")
        for b in BK:
            b.excl = True

        def bank(i, n=512, off=0):
            return PS[:, i * 512 + off:i * 512 + off + n]

        def pgf(i, a=0, b=512):
            return PG[:, i * 512 + a:i * 512 + b]

        def pgb(i, a=0, b=1024):
            return PGb[:, i * 1024 + a:i * 1024 + b]

        def xr(kc, tc):
            return XR[:, kc * ST + tc * TC:kc * ST + (tc + 1) * TC]

        def ht(kc, a, b):
            return HT[:, kc * ST + a:kc * ST + b]

        def sq(kc, n=512):
            return SQ[:, kc * 512:kc * 512 + n]

        def col(name, i=0):
            c = COLMAP[name] + i
            return COLS[:, c:c + 1]

        DERMAP = {}
        dn = 0
        for j in range(2):
            for n_ in ("nbr", "nbi", "cl", "cl2", "esk"):
                DERMAP[f"{n_}{j}"] = dn
                dn += 6

        def der(name, i=0):
            c = DERMAP[name] + i
            return DER[:, c:c + 1]

        def MM(out, lhsT, rhs, start, stop, rd, wr):
            P.op("pe", lambda e: e.matmul(out, lhsT, rhs, start=start, stop=stop), reads=rd, writes=wr)

        def ACTV(out, in_, func, rd, wr, bias=None, scale=1.0):
            if bias is None:
                P.op("act", lambda e: e.activation(out=out, in_=in_, func=func, scale=scale), reads=rd, writes=wr)
            else:
                P.op("act", lambda e: e.activation(out=out, in_=in_, func=func, bias=bias, scale=scale),
                     reads=rd, writes=wr)

        def TS(out, in0, s1, s2, op0, op1, rd, wr, eng="dve"):
            if s2 is None:
                P.op(eng, lambda e: e.tensor_scalar(out=out, in0=in0, scalar1=s1, scalar2=None, op0=op0),
                     reads=rd, writes=wr)
            else:
                P.op(eng, lambda e: e.tensor_scalar(out=out, in0=in0, scalar1=s1, scalar2=s2, op0=op0, op1=op1),
                     reads=rd, writes=wr)

        def STT(out, in0, scalar, in1, op0, op1, rd, wr, eng="dve"):
            P.op(eng, lambda e: e.scalar_tensor_tensor(out=out, in0=in0, scalar=scalar, in1=in1, op0=op0, op1=op1),
                 reads=rd, writes=wr)

        def TT(out, in0, in1, op, rd, wr, eng="dve"):
            P.op(eng, lambda e: e.tensor_tensor(out=out, in0=in0, in1=in1, op=op), reads=rd, writes=wr)

        def CP(out, in_, rd, wr, eng="dve"):
            P.op(eng, lambda e: e.tensor_copy(out=out, in_=in_), reads=rd, writes=wr)

        def MSET(ap, v, wr, eng="dve"):
            P.op(eng, lambda e: e.memset(ap, v), writes=wr)

        bstate = {"s": 0, "p": 0}

        def nb1():
            b = bstate["s"]
            bstate["s"] = (b + 1) % 6
            return b

        def nb2():
            b = bstate["p"]
            bstate["p"] = (b + 1) % 3
            return 2 * b

        rstate = {"n": 0}

        class Slot(int):
            pass
        slot_gen = [0, 0, 0, 0]

        def load_unit(src_ap, ln):
            s = Slot(rstate["n"] % 4)
            rstate["n"] += 1
            slot_gen[s] += 1
            s.gen = slot_gen[s]
            dst = RG[:, s * 4096:s * 4096 + ln]
            P.op("pool", lambda e: e.dma_start(out=dst, in_=src_ap), writes=[RB[s]], dma=f"ring{s}")
            return s

        def chk(s):
            assert s.gen == slot_gen[s], "weight ring slot was recycled before its last use"
            return s

        def lu(l, name):
            o, ln = LU[l][0][name]
            return load_unit(wl[l][:, o:o + ln], ln)

        def rg(s, kc, c0, n, width=512):
            chk(s)
            base = s * 4096 + kc * width + c0
            return RG[:, base:base + n]

        P.op("sp", lambda e: e.dma_start(out=COLS[:, :], in_=colsd), writes=[CB], dma="cols")
        P.op("sp", lambda e: e.dma_start(out=DT5[:, :], in_=dt5d), writes=[CB], dma="cols")
        MSET(CST[:, 0:1], EPS, [CB])
        MSET(CST[:, 1:2], 1.0, [CB])
        MSET(CST[:, 2:3], 0.0, [CB])
        MSET(ONESM[:, :], 1.0 / 1024.0, [CB])
        MSET(ONES64[:, :], 1.0, [CB])
        for j in range(2):
            TS(DER[:, DERMAP[f"nbr{j}"]:DERMAP[f"nbr{j}"] + 6], COLS[:, COLMAP[f"br{j}"]:COLMAP[f"br{j}"] + 6],
               -1.0, None, ALU.mult, None, [CB], [DERB])
            TS(DER[:, DERMAP[f"nbi{j}"]:DERMAP[f"nbi{j}"] + 6], COLS[:, COLMAP[f"bi{j}"]:COLMAP[f"bi{j}"] + 6],
               -1.0, None, ALU.mult, None, [CB], [DERB])
            cl = DER[:, DERMAP[f"cl{j}"]:DERMAP[f"cl{j}"] + 6]
            cl2 = DER[:, DERMAP[f"cl2{j}"]:DERMAP[f"cl2{j}"] + 6]
            ACTV(cl, COLS[:, COLMAP[f"lam{j}"]:COLMAP[f"lam{j}"] + 6], AF.Exp, [CB], [DERB], scale=-1.0)
            ACTV(cl, cl, AF.Ln, [DERB, CB], [DERB], bias=CST[:, 1:2])
            TS(cl2, cl, -16.0, None, ALU.mult, None, [DERB], [DERB])
            TS(cl, cl, -8.0, None, ALU.mult, None, [DERB], [DERB])
            ACTV(DER[:, DERMAP[f"esk{j}"]:DERMAP[f"esk{j}"] + 6], COLS[:, COLMAP[f"sk{j}"]:COLMAP[f"sk{j}"] + 6],
                 AF.Exp, [CB], [DERB])

        def rstd_from_bank(bk, rs_i, n=512):
            ACTV(T1[:, 0:n], bank(bk, n), AF.Ln, [BK[bk], CB], [T1B], bias=CST[:, 0:1])
            ACTV(RS[:, rs_i * 512:rs_i * 512 + n], T1[:, 0:n], AF.Exp, [T1B], [RSB[rs_i]], scale=-0.5)

        def prenorm(tc, gname, reuse_rstd=False):
            sbk = 6 + tc
            if not reuse_rstd:
                ACTV(SQ[:, 0:4096].rearrange("p (k n) -> p k n", k=8),
                     XR[:, :].rearrange("p (k n) -> p k n", k=8)[:, :, tc * TC:(tc + 1) * TC], AF.Square, [XB[tc]], list(SQB))
                for kc in range(8):
                    MM(bank(sbk), ONESM[:, :], sq(kc), kc == 0, kc == 7, [SQB[kc], CB], [BK[sbk]])
                rstd_from_bank(sbk, tc)
            for kc in range(8):
                STT(ht(kc, tc * TC, (tc + 1) * TC), xr(kc, tc), col(gname, kc), RS[:, tc * 512:(tc + 1) * 512],
                    ALU.mult, ALU.mult, [XB[tc], RSB[tc], CB], [HB[kc][tc]])

        def post_evac(m, bk, tc, fbv, fbB, gname, sqi=None):
            sbk = 6 + tc
            sqi = m if sqi is None else sqi
            ACTV(sq(sqi), bank(bk), AF.Square, [BK[bk]], [SQB[sqi]])
            TS(fbv, bank(bk), col(gname, m), None, ALU.mult, None, [BK[bk], CB], [fbB])
            return lambda: MM(bank(sbk), ONESM[:, :], sq(sqi), m == 0, m == 7, [SQB[sqi], CB], [BK[sbk]])

        def post_finish(tc, fbview, fbBs):
            sbk = 6 + tc
            rstd_from_bank(sbk, tc)
            FBT = FB0 if fbBs is F0 else FB1
            fb3 = FBT[:, 0:4096].rearrange("p (m n) -> p m n", m=8)
            rs3 = RS[:, tc * 512:(tc + 1) * 512].unsqueeze(1).broadcast_to([128, 8, 512])
            x3 = XR[:, :].rearrange("p (k n) -> p k n", k=8)[:, :, tc * TC:(tc + 1) * TC]
            TT(fb3, fb3, rs3, ALU.mult, list(fbBs) + [RSB[tc]], list(fbBs))
            TT(x3, x3, fb3, ALU.add, list(fbBs) + [XB[tc]], [XB[tc]])

        def fb0v(m):
            return FB0[:, m * 512:(m + 1) * 512]

        def fb1v(m):
            return FB1[:, m * 512:(m + 1) * 512]

        def mem_attention(l, qpages, ypage_of, tmp_pages):
            tp = 0
            for hp in range(2):
                nbk = nb1()
                dbk = nb1()
                ptp = []
                for hh in range(2):
                    h = 2 * hp + hh
                    o = 64 * hh
                    sb2 = nb2()
                    for mb in range(2):
                        MM(bank(sb2 + mb), KM[o:o + 64, (l * 2 + hp) * 256 + mb * 128:(l * 2 + hp) * 256 + (mb + 1) * 128],
                           pgb(qpages[hp], 0, 512)[o:o + 64, :], True, True,
                           [KMB[l], PB[qpages[hp]]], [BK[sb2 + mb]])
                    pt = tmp_pages[tp % len(tmp_pages)]
                    tp += 1
                    ACTV(pgb(pt), PS[:, sb2 * 512:sb2 * 512 + 1024], AF.Exp, [BK[sb2], BK[sb2 + 1]], [PB[pt]],
                         scale=0.125)
                    ptp.append(pt)
                for hh in range(2):
                    h = 2 * hp + hh
                    o = 64 * hh
                    pt = ptp[hh]
                    for mb in range(2):
                        MM(PS[o:o + 64, nbk * 512:(nbk + 1) * 512],
                           VM[:, (l * 2 + mb) * 256 + h * 64:(l * 2 + mb) * 256 + (h + 1) * 64],
                           pgb(pt, mb * 512, (mb + 1) * 512), mb == 0, mb == 1, [VMB[l], PB[pt]], [BK[nbk]])
                    for mb in range(2):
                        MM(PS[o:o + 64, dbk * 512:(dbk + 1) * 512], ONES64[:, :],
                           pgb(pt, mb * 512, (mb + 1) * 512), mb == 0, mb == 1, [CB, PB[pt]], [BK[dbk]])
                yap, yb = ypage_of(hp)
                tq = tmp_pages[tp % len(tmp_pages)]
                tp += 1
                ACTV(pgf(tq), bank(dbk), AF.Ln, [BK[dbk]], [PB[tq]])
                ACTV(pgf(tq), pgf(tq), AF.Exp, [PB[tq]], [PB[tq]], scale=-1.0)
                TT(yap, bank(nbk), pgf(tq), ALU.mult, [BK[nbk], PB[tq]], [yb])

        def out_proj_and_post(l, ycat_ap, ycat_bufs, tc):
            s0 = lu(l, "O0")
            s1 = lu(l, "O1")
            pend = None
            for m in range(8):
                s = s0 if m < 4 else s1
                bk = nb1()
                for kc in range(8):
                    MM(bank(bk), rg(s, kc, (m % 4) * 128, 128), ycat_ap(kc), kc == 0, kc == 7,
                       [RB[s], ycat_bufs[kc]], [BK[bk]])
                if pend is not None:
                    pend()
                pend = post_evac(m, bk, tc, fb0v(m), F0[m], f"gmo{l}")
            pend()
            post_finish(tc, fb0v, F0)

        def mixer_p1(l, tc):
            if l == 2:
                kv_project(tc)
            prenorm(tc, f"gmp{l}", reuse_rstd=(l == 2))

        def mixer_a(l, tc):
            j = l
            sA0 = lu(l, "A0")
            sA1 = lu(l, "A1")
            GG = [0, 1, 2, 3, 4, 5]
            YC = [6, 7, 8, 9]
            QM = [10, 11]
            TMP = list(range(12, 22))
            tstate = {"n": 0}

            def tmp():
                t = TMP[tstate["n"] % len(TMP)]
                tstate["n"] += 1
                return t

            def ycat_ap(kc):
                return pgb(YC[kc // 2], (kc % 2) * 512, (kc % 2 + 1) * 512)
            ycat_bufs = [PB[YC[kc // 2]] for kc in range(8)]
            hts = (tc * TC, (tc + 1) * TC)
            for c in range(6):
                s, c0 = (sA0, c * 128) if c < 4 else (sA1, (c - 4) * 128)
                bk = nb1()
                for kc in range(8):
                    MM(bank(bk), rg(s, kc, c0, 128), ht(kc, *hts), kc == 0, kc == 7, [RB[s], HB[kc][tc]], [BK[bk]])
                ACTV(pgf(GG[c]), bank(bk), AF.Gelu_apprx_tanh, [BK[bk]], [PB[GG[c]]])
            sG = lu(l, "G")
            sA2 = lu(l, "A2")

            def chain(c, q):
                s, c0 = (sA1, 256 + c * 128) if c < 2 else (sA2, (c - 2) * 128)
                base = 10 + 5 * q
                ta, t2, t3, t4, t5 = range(base, base + 5)
                tb = 25 + q // 2
                tbo = (q % 2) * 512
                ei = q + 1
                E = ET[:, ei * 1026:ei * 1026 + 515]
                uo = (j * 6 + c) * 3
                acc = pgf(ta)
                hso = j * 6 + c
                st = {}

                def s0():
                    bk = nb1()
                    st["bk"] = bk
                    for kc in range(8):
                        MM(bank(bk), rg(s, kc, c0, 128), ht(kc, *hts), kc == 0, kc == 7, [RB[s], HB[kc][tc]], [BK[bk]])

                def s1():
                    bk = st["bk"]
                    CP(E[:, 0:3], UXT[:, uo:uo + 3], [UXB[j][c]], [EB[ei]])
                    ACTV(E[:, 3:515], bank(bk), AF.Identity, [BK[bk]], [EB[ei]])
                    CP(UXT[:, uo:uo + 3], E[:, 512:515], [EB[ei]], [UXB[j][c]])
                    ACTV(acc, E[:, 3:515], AF.Identity, [EB[ei], CB], [PB[ta]], bias=col(f"cb{j}", c),
                         scale=col(f"cw{j}", c * 4 + 3))

                def s2():
                    for k in range(3):
                        STT(acc, E[:, k:k + 512], col(f"cw{j}", c * 4 + k), acc, ALU.mult, ALU.add,
                            [EB[ei], PB[ta], CB], [PB[ta]])
                    CP(pgb(tb, tbo, tbo + 512), acc, [PB[ta]], [PB[tb]])

                def s3():
                    bkr = nb1()
                    MM(bank(bkr), RG[:, chk(sG) * 4096 + c * 128:sG * 4096 + (c + 1) * 128], pgb(tb, tbo, tbo + 512), True, True,
                       [RB[sG], PB[tb]], [BK[bkr]])
                    bki = nb1()
                    MM(bank(bki), RG[:, sG * 4096 + (6 + c) * 128:sG * 4096 + (7 + c) * 128], pgb(tb, tbo, tbo + 512),
                       True, True, [RB[sG], PB[tb]], [BK[bki]])
                    st["bkr"], st["bki"] = bkr, bki

                def s4():
                    bkr, bki = st["bkr"], st["bki"]
                    ACTV(pgf(t2), bank(bkr), AF.Exp, [BK[bkr], DERB], [PB[t2]], bias=der(f"nbr{j}", c), scale=-1.0)
                    ACTV(pgf(t3), bank(bki), AF.Exp, [BK[bki], DERB], [PB[t3]], bias=der(f"nbi{j}", c), scale=-1.0)

                def s5():
                    ACTV(pgf(t2), pgf(t2), AF.Ln, [PB[t2], CB], [PB[t2]], bias=CST[:, 1:2])
                    ACTV(pgf(t3), pgf(t3), AF.Ln, [PB[t3], CB], [PB[t3]], bias=CST[:, 1:2])

                def s6():
                    ACTV(pgf(t2), pgf(t2), AF.Exp, [PB[t2]], [PB[t2]], scale=-1.0)

                def s7():
                    ACTV(pgf(t4), pgf(t2), AF.Exp, [PB[t2], DERB], [PB[t4]], scale=der(f"cl{j}", c))
                    ACTV(pgf(t5), pgf(t2), AF.Exp, [PB[t2], DERB], [PB[t5]], scale=der(f"cl2{j}", c))

                def s8():
                    TS(pgf(t5), pgf(t5), 0.99999994, None, ALU.min, None, [PB[t5]], [PB[t5]])

                def s9():
                    ACTV(pgf(t5), pgf(t5), AF.Ln, [PB[t5], CB], [PB[t5]], bias=CST[:, 1:2], scale=-1.0)

                def s10():
                    STT(pgf(t5), pgf(t5), 0.5, pgf(t3), ALU.mult, ALU.subtract, [PB[t5], PB[t3]], [PB[t5]])

                def s11():
                    ACTV(pgf(t5), pgf(t5), AF.Exp, [PB[t5]], [PB[t5]])

                def s12():
                    TT(pgf(t5), pgf(t5), acc, ALU.mult, [PB[t5], PB[ta]], [PB[t5]])
                    P.op("dve", lambda e, o_=pgf(t3), a_=pgf(t4), b_=pgf(t5), i_=HST[:, hso:hso + 1]:
                         e.tensor_tensor_scan(out=o_, data0=a_, data1=b_, initial=i_, op0=ALU.mult, op1=ALU.add),
                         reads=[PB[t4], PB[t5], HSB[j][c]], writes=[PB[t3]])
                    CP(HST[:, hso:hso + 1], pgf(t3, 511, 512), [PB[t3]], [HSB[j][c]])
                    TT(ycat_ap(c), pgf(t3), pgf(GG[c]), ALU.mult, [PB[t3], PB[GG[c]]], [ycat_bufs[c]])

                return [s0, s1, s2, s3, s4, s5, s6, s7, s8, s9, s10, s11, s12]

            sA3 = lu(l, "A3")
            for grp in range(2):
                ch = [chain(3 * grp + q, q) for q in range(3)]
                for k in range(len(ch[0])):
                    for q in range(3):
                        ch[q][k]()
            for hp in range(2):
                bk = nb1()
                for kc in range(8):
                    MM(bank(bk), rg(sA3, kc, hp * 128, 128, width=256), ht(kc, *hts), kc == 0, kc == 7,
                       [RB[sA3], HB[kc][tc]], [BK[bk]])
                ACTV(pgb(QM[hp], 0, 512), bank(bk), AF.Identity, [BK[bk]], [PB[QM[hp]]])
            mem_attention(l, QM, lambda hp: (ycat_ap(6 + hp), ycat_bufs[6 + hp]), TMP)
            return lambda: out_proj_and_post(l, ycat_ap, ycat_bufs, tc)

        def kv_project(tc):
            prenorm(tc, "gkv")
            hts = (tc * TC, (tc + 1) * TC)
            s0 = load_unit(wk[:, 0:4096], 4096)
            s1 = load_unit(wk[:, 4096:6144], 2048)
            for k in range(4):
                bk = nb1()
                for kc in range(8):
                    MM(bank(bk), rg(s0, kc, k * 128, 128), ht(kc, *hts), kc == 0, kc == 7, [RB[s0], HB[kc][tc]], [BK[bk]])
                ACTV(KT[:, k * 1152 + 128 + tc * 512:k * 1152 + 128 + (tc + 1) * 512], bank(bk), AF.Identity, [BK[bk]],
                     [KB[1 + 4 * tc + i] for i in range(4)])
            for tb in range(4):
                bk = nb1()
                for kc in range(8):
                    MM(bank(bk, 256), ht(kc, tc * TC + tb * 128, tc * TC + (tb + 1) * 128), rg(s1, kc, 0, 256, width=256),
                       kc == 0, kc == 7, [RB[s1], HB[kc][tc]], [BK[bk]])
                idx = 1 + 4 * tc + tb
                ACTV(VT[:, idx * 256:(idx + 1) * 256], bank(bk, 256), AF.Identity, [BK[bk]], [VB[idx]])

        def mixer_b(l, tc, first):
            j = l - 2
            hts = (tc * TC, (tc + 1) * TC)
            sB0 = lu(l, "B0")
            sB1 = lu(l, "B1")
            QP = list(range(0, 8))
            YC = [8, 9, 10, 11]
            TMP = list(range(16, 22))
            tstate = {"n": 0}
            sstate = {"n": 0}

            def tmp():
                t = TMP[tstate["n"] % len(TMP)]
                tstate["n"] += 1
                return t

            def ycat_ap(kc):
                return pgb(YC[kc // 2], (kc % 2) * 512, (kc % 2 + 1) * 512)
            ycat_bufs = [PB[YC[kc // 2]] for kc in range(8)]
            for cq in range(8):
                s, c0 = (sB0, cq * 128) if cq < 4 else (sB1, (cq - 4) * 128)
                bk = nb1()
                for kc in range(8):
                    MM(bank(bk), rg(s, kc, c0, 128), ht(kc, *hts), kc == 0, kc == 7, [RB[s], HB[kc][tc]], [BK[bk]])
                ACTV(pgb(QP[cq], 0, 512), bank(bk), AF.Identity, [BK[bk]], [PB[QP[cq]]])
            noprev = first and tc == 0
            ncol = 896 if noprev else 1024

            def stA(cq):
                pts = []
                for hh in range(2):
                    h = 2 * cq + hh
                    o = 64 * hh
                    k = h // 3
                    sb2 = 2 * hh
                    q = pgb(QP[cq], 0, 512)
                    for i in range(5):
                        if i == 0 and noprev:
                            continue
                        sidx = 4 * tc + i
                        kap = KT[o:o + 64, k * 1152 + sidx * 128:k * 1152 + (sidx + 1) * 128]
                        if i == 0:
                            c0, qa, qb = 896, 0, 128
                        elif i == 4:
                            c0, qa, qb = 768, 384, 512
                        else:
                            c0, qa, qb = (i - 1) * 256, (i - 1) * 128, (i + 1) * 128
                        bki = sb2 + c0 // 512
                        MM(PS[:, sb2 * 512 + c0:sb2 * 512 + c0 + (qb - qa)], kap, q[o:o + 64, qa:qb], True, True,
                           [KB[sidx], PB[QP[cq]]], [BK[bki]])
                    sp = 12 + 2 * (sstate["n"] % 2)
                    sstate["n"] += 1
                    spv = PG[:, sp * 512:sp * 512 + ncol]
                    STT(spv, DT5[:, 0:ncol], -SLOPES[h], PS[:, sb2 * 512:sb2 * 512 + ncol], ALU.mult, ALU.add,
                        [CB, BK[sb2], BK[sb2 + 1]], [PB[sp], PB[sp + 1]])
                    pt = tmp()
                    ACTV(pgb(pt, 0, ncol), spv, AF.Exp, [PB[sp], PB[sp + 1]], [PB[pt]], scale=0.125)
                    pts.append(pt)
                return pts

            def stB(cq, pts):
                nbk, dbk = (4, 5) if cq % 2 == 0 else (6, 7)
                for hh in range(2):
                    h = 2 * cq + hh
                    o = 64 * hh
                    k = h // 3
                    pt = pts[hh]
                    for qb_ in range(4):
                        srcs = []
                        if not (noprev and qb_ == 0):
                            pc0 = 896 if qb_ == 0 else (qb_ - 1) * 256 + 128
                            srcs.append((4 * tc + qb_, pc0))
                        cc0 = 768 if qb_ == 3 else qb_ * 256
                        srcs.append((4 * tc + qb_ + 1, cc0))
                        for which in ("n", "d"):
                            bkx = nbk if which == "n" else dbk
                            for si, (sidx, c0) in enumerate(srcs):
                                if which == "n":
                                    lhs = VT[:, sidx * 256 + k * 64:sidx * 256 + (k + 1) * 64]
                                    rd = [VB[sidx], PB[pt]]
                                else:
                                    lhs = ONES64[:, :]
                                    rd = [CB, PB[pt]]
                                MM(PS[o:o + 64, bkx * 512 + qb_ * 128:bkx * 512 + (qb_ + 1) * 128], lhs,
                                   pgb(pt, c0, c0 + 128), si == 0, si == len(srcs) - 1, rd, [BK[bkx]])

            def stC(cq):
                nbk, dbk = (4, 5) if cq % 2 == 0 else (6, 7)
                tq = tmp()
                ACTV(pgf(tq), bank(dbk), AF.Ln, [BK[dbk], DERB], [PB[tq]], bias=der(f"esk{j}", cq))
                ACTV(pgf(tq), pgf(tq), AF.Exp, [PB[tq]], [PB[tq]], scale=-1.0)
                TT(ycat_ap(cq), bank(nbk), pgf(tq), ALU.mult, [BK[nbk], PB[tq]], [ycat_bufs[cq]])

            nxt = stA(0)
            for cq in range(6):
                cur = nxt
                if cq + 1 < 6:
                    nxt = stA(cq + 1)
                stB(cq, cur)
                stC(cq)
            mem_attention(l, [QP[6], QP[7]], lambda hp: (ycat_ap(6 + hp), ycat_bufs[6 + hp]), TMP)
            return lambda: out_proj_and_post(l, ycat_ap, ycat_bufs, tc)

        def ffn(l, next_p1=None):
            units = {}

            def stage1(jn):
                j2, jj = divmod(jn, 2)
                if jj == 0:
                    units[j2] = lu(l, f"U{j2}")
                s = units[j2]
                st_ = jn % 2
                gb, vb = (0, 2) if st_ == 0 else (4, 6)
                order = [(b0, c0, tc) for b0, c0 in ((gb, jj * 128), (vb, 256 + jj * 128)) for tc in range(2)]
                if jn == 0:
                    order.sort(key=lambda t: t[2])
                for b0, c0, tc in order:
                    for kc in range(8):
                        MM(bank(b0 + tc), rg(s, kc, c0, 128), ht(kc, tc * TC, (tc + 1) * TC), kc == 0, kc == 7,
                           [RB[s], HB[kc][tc]], [BK[b0 + tc]])

            def halves(jn):
                st_ = jn % 2
                gb, vb = (0, 2) if st_ == 0 else (4, 6)
                out = []
                for hi, (b0, ch) in enumerate(((gb, jn), (vb, 22 + jn))):
                    ei = 2 * st_ + hi
                    out.append((b0, ch, ei, ET[:, ei * 1026:(ei + 1) * 1026], (l * 44 + ch) * 2,
                                FB0[:, ei * 1024:(ei + 1) * 1024]))
                return out

            def stage2a(jn):
                for b0, ch, ei, E, to, acc in halves(jn):
                    CP(E[:, 0:2], TAILS[:, to:to + 2], [TLB[l][ch]], [EB[ei]])

            def stage2b(jn):
                for b0, ch, ei, E, to, acc in halves(jn):
                    ACTV(E[:, 2:1026], PS[:, b0 * 512:b0 * 512 + 1024], AF.Identity, [BK[b0], BK[b0 + 1]], [EB[ei]])
                for b0, ch, ei, E, to, acc in halves(jn):
                    ACTV(acc, E[:, 2:1026], AF.Identity, [EB[ei], CB], [AC[ei]], bias=col(f"fcb{l}", ch),
                         scale=col(f"fcw{l}", ch * 3 + 2))

            def stage2c(jn):
                accs = []
                for b0, ch, ei, E, to, acc in halves(jn):
                    CP(TAILS[:, to:to + 2], E[:, 1024:1026], [EB[ei]], [TLB[l][ch]])
                for b0, ch, ei, E, to, acc in halves(jn):
                    for k in range(2):
                        STT(acc, E[:, k:k + 1024], col(f"fcw{l}", ch * 3 + k), acc, ALU.mult, ALU.add,
                            [EB[ei], AC[ei], CB], [AC[ei]])
                    accs.append((acc, AC[ei]))
                return accs

            def stage3(jn, accs):
                ACTV(accs[0][0], accs[0][0], AF.Gelu_apprx_tanh, [accs[0][1]], [accs[0][1]])
                TT(pgb(jn), accs[1][0], accs[0][0], ALU.mult, [accs[0][1], accs[1][1]], [PB[jn]])

            prev = None
            stage2a(0)
            for jn in range(NJ):
                stage1(jn)
                stage2b(jn)
                if jn + 1 < NJ:
                    stage2a(jn + 1)
                accs = stage2c(jn)
                if prev is not None:
                    stage3(*prev)
                prev = (jn, accs)
            stage3(*prev)
            pend = None

            def down_mms(s, bk, tc, j0, j1):
                for jn in range(j0, j1):
                    MM(bank(bk), RG[:, chk(s) * 4096 + jn * 128:s * 4096 + (jn + 1) * 128],
                       pgb(jn, tc * 512, (tc + 1) * 512), jn == 0, jn == NJ - 1, [RB[s], PB[jn]], [BK[bk]])

            NW = 3
            wave = {}
            for m in range(NW):
                s = lu(l, f"D{m}")
                for tc in range(2):
                    bk = nb1()
                    wave[(m, tc)] = (s, bk)
                    down_mms(s, bk, tc, 0, NJ - 2)
            for m in range(8):
                if m >= NW:
                    s = lu(l, f"D{m}")
                for tc in range(2):
                    if m < NW:
                        s, bk = wave[(m, tc)]
                        down_mms(s, bk, tc, NJ - 2, NJ)
                    else:
                        bk = nb1()
                        down_mms(s, bk, tc, 0, NJ)
                    if pend is not None:
                        pend()
                    sqi = (2 * m + tc) % 8
                    if tc == 0:
                        pend = post_evac(m, bk, 0, fb1v(m), F1[m], f"gfo{l}", sqi)
                    else:
                        pend = post_evac(m, bk, 1, fb0v(m), F0[m], f"gfo{l}", sqi)
            pend()
            post_finish(0, fb1v, F1)
            if next_p1 is not None:
                next_p1()
            post_finish(1, fb0v, F0)

        def seq_prologue(s):
            for l in range(4):
                for ch in range(44):
                    pass
            P.op("dve", lambda e: e.memset(TAILS[:, :], 0.0), writes=[b for l in range(4) for b in TLB[l]])
            P.op("dve", lambda e: e.memset(UXT[:, :], 0.0), writes=[b for j in range(2) for b in UXB[j]])
            P.op("dve", lambda e: e.memset(HST[:, :], 0.0), writes=[b for j in range(2) for b in HSB[j]])
            MT = PG[:, 0:2048]
            P.op("sp", lambda e: e.dma_start(out=MT.rearrange("p (k n) -> p k n", k=8), in_=memT[s]),
                 writes=[PB[0], PB[1], PB[2], PB[3]], dma="mem")
            for kc in range(8):
                ACTV(sq(kc, 256), PG[:, kc * 256:(kc + 1) * 256], AF.Square, [PB[kc // 2]], [SQB[kc]])
            for kc in range(8):
                MM(bank(6, 256), ONESM[:, :], sq(kc, 256), kc == 0, kc == 7, [SQB[kc], CB], [BK[6]])
            rstd_from_bank(6, 0, 256)
            for l in range(n_layers):
                su = load_unit(wm[:, l * 4096:(l + 1) * 4096], 4096)
                for kc in range(8):
                    STT(PGb[:, 4 * 1024 + kc * 256:4 * 1024 + (kc + 1) * 256], PG[:, kc * 256:(kc + 1) * 256],
                        col(f"gme{l}", kc), RS[:, 0:256], ALU.mult, ALU.mult, [PB[kc // 2], RSB[0], CB],
                        [PB[4 + kc // 4]])

                def hm(kc, a=0, b=256):
                    return PGb[:, 4 * 1024 + kc * 256 + a:4 * 1024 + kc * 256 + b]
                for hp in range(2):
                    bk = nb1()
                    for kc in range(8):
                        MM(bank(bk, 256), rg(su, kc, hp * 128, 128), hm(kc), kc == 0, kc == 7,
                           [RB[su], PB[4 + kc // 4]], [BK[bk]])
                    ACTV(KM[:, (l * 2 + hp) * 256:(l * 2 + hp + 1) * 256], bank(bk, 256), AF.Identity, [BK[bk]], [KMB[l]])
                for mb in range(2):
                    bk = nb1()
                    for kc in range(8):
                        MM(bank(bk, 256), hm(kc, mb * 128, (mb + 1) * 128), rg(su, kc, 256, 256), kc == 0, kc == 7,
                           [RB[su], PB[4 + kc // 4]], [BK[bk]])
                    ACTV(VM[:, (l * 2 + mb) * 256:(l * 2 + mb + 1) * 256], bank(bk, 256), AF.Identity, [BK[bk]], [VMB[l]])

        XR3 = XR[:, :].rearrange("p (k n) -> p k n", k=8)
        out_evs = []
        for s in range(n_seq):
            seq_prologue(s)
            for st in range(NST):
                P.op("sp", lambda e, s=s, st=st: e.dma_start(out=XR3, in_=xT[s, :, :, st * ST:(st + 1) * ST]),
                     writes=[XB[0], XB[1]], dma="xin")
                for l in range(n_layers):
                    mix = (lambda tc, l=l: mixer_a(l, tc)) if l < 2 else (lambda tc, l=l, st=st: mixer_b(l, tc, st == 0))
                    if l == 0:
                        mixer_p1(l, 0)
                    fin0 = mix(0)
                    mixer_p1(l, 1)
                    fin0()
                    fin1 = mix(1)
                    prenorm(0, f"gfp{l}")
                    fin1()
                    prenorm(1, f"gfp{l}")
                    ffn(l, (lambda l=l: mixer_p1(l + 1, 0)) if l + 1 < n_layers else None)
                if n_layers > 2 and st < NST - 1:
                    for k in range(4):
                        CP(KT[:, k * 1152:k * 1152 + 128], KT[:, k * 1152 + 1024:k * 1152 + 1152], [KB[8]], [KB[0]])
                    CP(VT[:, 0:256], VT[:, 8 * 256:9 * 256], [VB[8]], [VB[0]])
                o = P.op("sp", lambda e, s=s, st=st: e.dma_start(out=yT[s, :, :, st * ST:(st + 1) * ST], in_=XR3),
                         reads=[XB[0], XB[1]], dma="yout")
                out_evs.append(o.ev)
        P.emit(nc, final_waits=[out_evs[-1]])
    return nc


_CACHE = {}


def kernel(_n_layers=4, _cores=NCORES, **inputs):
    key = (_n_layers,)
    if key not in _CACHE:
        _CACHE[key] = build(_n_layers)
    nc = _CACHE[key]
    sh = prep_shared(inputs)
    in_maps = []
    for c in range(_cores):
        m = dict(sh)
        m.update(prep_core(inputs, c))
        in_maps.append(m)
    res = run_bass_kernel_spmd(nc, in_maps, core_ids=list(range(_cores)))
    outs = []
    for c in range(_cores):
        yT = np.asarray(res.results[c]["yT"])
        outs.append(np.ascontiguousarray(yT.transpose(0, 3, 2, 1)).reshape(2, SEQ, D))
    return np.concatenate(outs, axis=0).astype(np.float32)
```
